# Optimizing a Trainium2 kernel written in Bass

```python
import math
import jax, jax.numpy as jnp
from jax import lax
import numpy as np

D_MODEL = 1024
BATCH = 8
SEQ = 2048
DEPTH = 1
DEC_BATCH = 16
DEC_SEQ = 4096
PAST_LEN = 128

N_DN = 4
DK_DN = 128
DV_DN = 128
DN_CONV = 5
CHUNK = 64
N_DA = 4
DH_DA = 64
Q_BLOCK = 128
ROT_DIM = DH_DA // 4
ROPE_THETA = 500000.0
D_FF = 2816
FFN_CONV = 3
PLE_DIM = 256
EPS = 1e-6

DN_QK = N_DN * DK_DN
DN_V = N_DN * DV_DN
DA_QK = N_DA * 2 * DH_DA
DA_V = N_DA * 2 * DH_DA
MIX_WIDTH = DN_V + DA_V
SPLIT_SIZES = [DN_QK, DN_QK, DN_V, DN_V, 2 * N_DN, 2 * N_DN, DA_QK, DA_QK, DA_V]
IN_COLS = int(sum(SPLIT_SIZES))
SPLIT_IDX = [int(c) for c in np.cumsum(SPLIT_SIZES)[:-1]]

kernel_name = "hybrid_bidir_deltanet_diffattn_encoder"


def rms_norm(x, g):
    xf = x.astype(jnp.float32)
    y = xf * lax.rsqrt(jnp.mean(xf * xf, axis=-1, keepdims=True) + EPS)
    return (y * g.astype(jnp.float32)).astype(x.dtype)


def l2_norm(x):
    return x * lax.rsqrt(jnp.sum(x * x, axis=-1, keepdims=True) + EPS)


def centred_dwconv(x, w):
    k = w.shape[0]
    r = k // 2
    s = x.shape[1]
    xp = jnp.pad(x, ((0, 0), (r, r), (0, 0)))
    out = xp[:, 0:s] * w[0]
    for j in range(1, k):
        out = out + xp[:, j:j + s] * w[j]
    return out


def rope_tables(s):
    inv_freq = ROPE_THETA ** (-jnp.arange(0, ROT_DIM, 2, dtype=jnp.float32) / ROT_DIM)
    ang = jnp.arange(s, dtype=jnp.float32)[:, None] * inv_freq[None, :]
    return jnp.cos(ang), jnp.sin(ang)


def partial_rope(x, cos, sin):
    c = cos[None, :, None, None, :].astype(x.dtype)
    s_ = sin[None, :, None, None, :].astype(x.dtype)
    half = ROT_DIM // 2
    x1 = x[..., :half]
    x2 = x[..., half:ROT_DIM]
    return jnp.concatenate([x1 * c - x2 * s_, x2 * c + x1 * s_, x[..., ROT_DIM:]], axis=-1)


def gated_delta_rule(q, k, v, g, beta):
    out_dtype = v.dtype
    b, t, h, dk = q.shape
    dv = v.shape[-1]
    n = t // CHUNK
    f32 = jnp.float32
    q = q.astype(f32).transpose(0, 2, 1, 3).reshape(b, h, n, CHUNK, dk)
    k = k.astype(f32).transpose(0, 2, 1, 3).reshape(b, h, n, CHUNK, dk)
    v = v.astype(f32).transpose(0, 2, 1, 3).reshape(b, h, n, CHUNK, dv)
    g = g.astype(f32).transpose(0, 2, 1).reshape(b, h, n, CHUNK)
    beta = beta.astype(f32).transpose(0, 2, 1).reshape(b, h, n, CHUNK)

    gc = jnp.cumsum(g, axis=-1)
    tril = jnp.tril(jnp.ones((CHUNK, CHUNK), dtype=bool))
    strict = jnp.tril(jnp.ones((CHUNK, CHUNK), dtype=bool), -1)
    decay = jnp.exp(jnp.where(tril, gc[..., :, None] - gc[..., None, :], -jnp.inf))

    kb = k * beta[..., None]
    lower = jnp.where(strict, jnp.einsum('bhnid,bhnjd->bhnij', kb, k) * decay, 0.0)
    rhs = jnp.concatenate([v * beta[..., None], kb * jnp.exp(gc)[..., None]], axis=-1)
    sol = lax.linalg.triangular_solve(lower, rhs, left_side=True, lower=True, unit_diagonal=True)
    u = sol[..., :dv]
    w = sol[..., dv:]

    qk = jnp.einsum('bhnid,bhnjd->bhnij', q, k) * decay
    q_dec = q * jnp.exp(gc)[..., None]
    k_dec = k * jnp.exp(gc[..., -1:] - gc)[..., None]
    g_last = jnp.exp(gc[..., -1])

    xs = tuple(jnp.moveaxis(a, 2, 0) for a in (qk, q_dec, k_dec, u, w, g_last))

    def step(state, inp):
        qk_c, qd_c, kd_c, u_c, w_c, gl_c = inp
        v_new = u_c - jnp.einsum('bhck,bhkv->bhcv', w_c, state)
        o = jnp.einsum('bhck,bhkv->bhcv', qd_c, state) + jnp.einsum('bhij,bhjv->bhiv', qk_c, v_new)
        state = state * gl_c[..., None, None] + jnp.einsum('bhck,bhcv->bhkv', kd_c, v_new)
        return state, o

    s0 = jnp.zeros((b, h, dk, dv), f32)
    _, o = lax.scan(step, s0, xs)
    o = jnp.moveaxis(o, 0, 2).reshape(b, h, t, dv).transpose(0, 2, 1, 3)
    return o.astype(out_dtype)


def diff_attention(q, k, v, lam):
    b, s, h, _, dh = q.shape
    nb = s // Q_BLOCK
    scale = dh ** -0.5
    f32 = jnp.float32
    qb = q.astype(f32).reshape(b, nb, Q_BLOCK, h, 2, dh).transpose(1, 0, 3, 4, 2, 5)
    kt = k.astype(f32).transpose(0, 2, 3, 1, 4)
    vt = v.astype(f32).transpose(0, 2, 1, 3)

    def block(qblk):
        scores = jnp.einsum('bhcqd,bhckd->bhcqk', qblk, kt) * scale
        p = jax.nn.softmax(scores, axis=-1)
        a = p[:, :, 0] - lam * p[:, :, 1]
        return jnp.einsum('bhqk,bhke->bhqe', a, vt)

    o = lax.map(block, qb)
    o = o.transpose(1, 0, 3, 2, 4).reshape(b, s, h, 2 * dh)
    return o.astype(v.dtype)


def encoder_layer(x, pe, layer_idx, ln1_g, w_in, dn_conv_w, dn_a_log, dn_dt_bias, dn_norm_g,
                  da_qk_norm_g, da_lambda, da_subln_g, w_out, ln2_g, w_up, ffn_conv_w, ffn_conv_b,
                  w_down, ple_proj, ple_norm_g, ple_gate_norm_g, w_ple_gate):
    b, s, _ = x.shape
    lam_init = 0.8 - 0.6 * math.exp(-0.3 * layer_idx)

    hn = rms_norm(x, ln1_g)
    proj = hn @ w_in
    dq, dk, dv, dz, da, db, aq, ak, av = jnp.split(proj, SPLIT_IDX, axis=-1)

    qkv = jax.nn.silu(centred_dwconv(jnp.concatenate([dq, dk, dv], axis=-1), dn_conv_w))
    dq, dk, dv = jnp.split(qkv, [DN_QK, 2 * DN_QK], axis=-1)
    dq = (l2_norm(dq.reshape(b, s, N_DN, DK_DN).astype(jnp.float32)) * (DK_DN ** -0.5))
    dk = l2_norm(dk.reshape(b, s, N_DN, DK_DN).astype(jnp.float32))
    dv = dv.reshape(b, s, N_DN, DV_DN)
    da = da.reshape(b, s, 2, N_DN).astype(jnp.float32)
    db = db.reshape(b, s, 2, N_DN).astype(jnp.float32)
    log_decay = -jnp.exp(dn_a_log.astype(jnp.float32)) * jax.nn.softplus(da + dn_dt_bias.astype(jnp.float32))
    beta = jax.nn.sigmoid(db)
    o_fwd = gated_delta_rule(dq, dk, dv, log_decay[:, :, 0], beta[:, :, 0])
    flip = lambda a: jnp.flip(a, axis=1)
    o_bwd = flip(gated_delta_rule(flip(dq), flip(dk), flip(dv), flip(log_decay[:, :, 1]), flip(beta[:, :, 1])))
    o_dn = rms_norm(o_fwd + o_bwd, dn_norm_g) * jax.nn.silu(dz.reshape(b, s, N_DN, DV_DN))

    cos, sin = rope_tables(s)
    aq = partial_rope(rms_norm(aq.reshape(b, s, N_DA, 2, DH_DA), da_qk_norm_g[0]), cos, sin)
    ak = partial_rope(rms_norm(ak.reshape(b, s, N_DA, 2, DH_DA), da_qk_norm_g[1]), cos, sin)
    av = av.reshape(b, s, N_DA, 2 * DH_DA)
    lf = da_lambda.astype(jnp.float32)
    lam = jnp.exp(jnp.sum(lf[0] * lf[1])) - jnp.exp(jnp.sum(lf[2] * lf[3])) + lam_init
    o_da = diff_attention(aq, ak, av, lam)
    o_da = rms_norm(o_da, da_subln_g) * (1.0 - lam_init)

    mix = jnp.concatenate([o_dn.reshape(b, s, DN_V), o_da.reshape(b, s, DA_V)], axis=-1)
    x = x + (mix @ w_out).astype(x.dtype)

    hn = rms_norm(x, ln2_g)
    gate, up = jnp.split(hn @ w_up, [D_FF], axis=-1)
    gate = centred_dwconv(gate, ffn_conv_w) + ffn_conv_b
    x = x + ((jax.nn.silu(gate) * up) @ w_down).astype(x.dtype)

    e = rms_norm(pe @ ple_proj, ple_norm_g)
    gt = jax.nn.sigmoid(rms_norm(x, ple_gate_norm_g) @ w_ple_gate)
    return x + (gt * e).astype(x.dtype)


def trunk(x, p, weights):
    for i in range(DEPTH):
        x = encoder_layer(x, p[i], i, *[w[i] for w in weights])
    return x


def setup_inputs(seed: int = 0) -> dict:
    key = jax.random.key(seed)
    ks = jax.random.split(key, 24)
    f32 = jnp.float32
    nrm = lambda k, shape, scale: jax.random.normal(k, shape, f32) * scale
    gain = lambda k, shape: 1.0 + 0.02 * jax.random.normal(k, shape, f32)
    dt = jnp.exp(jax.random.uniform(ks[6], (DEPTH, 2, N_DN), f32, math.log(1e-3), math.log(1e-1)))
    return {
        "x_prompt": jax.random.normal(ks[0], (BATCH, SEQ, D_MODEL), f32),
        "x_sample": jax.random.normal(ks[1], (DEC_BATCH, DEC_SEQ, D_MODEL), f32),
        "p_prompt": jax.random.normal(ks[2], (DEPTH, BATCH, SEQ, PLE_DIM), f32),
        "p_sample": jax.random.normal(ks[3], (DEPTH, DEC_BATCH, DEC_SEQ, PLE_DIM), f32),
        "ln1_g": gain(ks[4], (DEPTH, D_MODEL)),
        "w_in": nrm(ks[5], (DEPTH, D_MODEL, IN_COLS), D_MODEL ** -0.5),
        "dn_conv_w": nrm(ks[7], (DEPTH, DN_CONV, 2 * DN_QK + DN_V), DN_CONV ** -0.5),
        "dn_a_log": jnp.log(jax.random.uniform(ks[8], (DEPTH, 2, N_DN), f32, 1.0, 16.0)),
        "dn_dt_bias": dt + jnp.log(-jnp.expm1(-dt)),
        "dn_norm_g": gain(ks[9], (DEPTH, DV_DN)),
        "da_qk_norm_g": gain(ks[10], (DEPTH, 2, DH_DA)),
        "da_lambda": nrm(ks[11], (DEPTH, 4, DH_DA), 0.1),
        "da_subln_g": gain(ks[12], (DEPTH, 2 * DH_DA)),
        "w_out": nrm(ks[13], (DEPTH, MIX_WIDTH, D_MODEL), MIX_WIDTH ** -0.5),
        "ln2_g": gain(ks[14], (DEPTH, D_MODEL)),
        "w_up": nrm(ks[15], (DEPTH, D_MODEL, 2 * D_FF), D_MODEL ** -0.5),
        "ffn_conv_w": nrm(ks[16], (DEPTH, FFN_CONV, D_FF), FFN_CONV ** -0.5),
        "ffn_conv_b": nrm(ks[17], (DEPTH, D_FF), 0.02),
        "w_down": nrm(ks[18], (DEPTH, D_FF, D_MODEL), D_FF ** -0.5),
        "ple_proj": nrm(ks[19], (DEPTH, PLE_DIM, D_MODEL), PLE_DIM ** -0.5),
        "ple_norm_g": gain(ks[20], (DEPTH, D_MODEL)),
        "ple_gate_norm_g": gain(ks[21], (DEPTH, D_MODEL)),
        "w_ple_gate": nrm(ks[22], (DEPTH, D_MODEL, D_MODEL), D_MODEL ** -0.5),
    }


def reference(x_prompt, x_sample, p_prompt, p_sample, ln1_g, w_in, dn_conv_w, dn_a_log, dn_dt_bias,
              dn_norm_g, da_qk_norm_g, da_lambda, da_subln_g, w_out, ln2_g, w_up, ffn_conv_w,
              ffn_conv_b, w_down, ple_proj, ple_norm_g, ple_gate_norm_g, w_ple_gate):
    weights = (ln1_g, w_in, dn_conv_w, dn_a_log, dn_dt_bias, dn_norm_g, da_qk_norm_g, da_lambda,
               da_subln_g, w_out, ln2_g, w_up, ffn_conv_w, ffn_conv_b, w_down, ple_proj,
               ple_norm_g, ple_gate_norm_g, w_ple_gate)
    y_prompt = trunk(x_prompt, p_prompt, weights)
    y_sample = trunk(x_sample, p_sample, weights)
    return (y_prompt, y_sample)
```

```python
import contextlib
import math

import numpy as np
import concourse.bass as bass
import concourse.mybir as mybir
from concourse.bass_utils import run_bass_kernel_spmd

F32 = mybir.dt.float32
BF16 = mybir.dt.bfloat16
AF = mybir.ActivationFunctionType
ALU = mybir.AluOpType
AX = mybir.AxisListType

D = 1024
NCOL = 3600
DFF = 2816
PLE = 256
EPS = 1e-6
LAM_INIT = 0.8 - 0.6 * math.exp(-0.3 * 0)
ROPE_THETA = 500000.0

ENGS = ("pe", "act", "dve", "pool", "sp")
NDMASEM = 16


class Op:
    __slots__ = ("eng", "fn", "reads", "writes", "dma", "idx", "waits", "sig", "sem", "val")

    def __init__(self, eng, fn, reads, writes, dma):
        self.eng = eng
        self.fn = fn
        self.reads = reads
        self.writes = writes
        self.dma = dma
        self.waits = []
        self.sig = False
        self.sem = None
        self.val = 0


class Sched:
    def __init__(self, nc):
        self.nc = nc
        self.ops = []
        self.last_w = {}
        self.readers = {}
        self.last_eng = {}
        self.recent_dma = {e: [] for e in ENGS}

    def add(self, eng, fn, reads=(), writes=(), dma=False):
        op = Op(eng, fn, tuple(reads), tuple(writes), dma)
        op.idx = len(self.ops)
        deps = {}
        for t in op.reads:
            w = self.last_w.get(t)
            if w is not None:
                deps[w.idx] = w
        for t in op.writes:
            w = self.last_w.get(t)
            if w is not None:
                deps[w.idx] = w
            for r in self.readers.get(t, ()):
                deps[r.idx] = r
        for d in deps.values():
            if self._needs_sync(d, op):
                op.waits.append(d)
                d.sig = True
        for t in op.writes:
            self.last_w[t] = op
            self.readers[t] = []
        for t in op.reads:
            if t not in op.writes:
                self.readers.setdefault(t, []).append(op)
        self.ops.append(op)
        if dma:
            lst = self.recent_dma[eng]
            lst.append(op)
            if len(lst) > NDMASEM:
                lst.pop(0)
        else:
            self.last_eng[eng] = op
        return op

    def barrier(self):
        prods = list(self.last_eng.values())
        for e in ENGS:
            prods.extend(self.recent_dma[e])
        new = []
        for e in ENGS:
            op = Op(e, None, (), (), False)
            op.idx = len(self.ops)
            for d in prods:
                op.waits.append(d)
                d.sig = True
            self.ops.append(op)
            new.append(op)
        for op in new:
            self.last_eng[op.eng] = op
        self.last_w = {}
        self.readers = {}

    def _needs_sync(self, prod, cons):
        if prod.dma or cons.dma:
            return True
        if prod.eng != cons.eng:
            return True
        if prod.eng == "pe":
            return False
        for t in cons.reads:
            if t in prod.writes:
                return True
        for t in cons.writes:
            if t in prod.writes:
                return True
        return False

    def emit(self):
        nc = self.nc
        with contextlib.ExitStack() as es:
            csem = {e: es.enter_context(nc.semaphore("c_" + e)) for e in ENGS}
            dsem = {e: [es.enter_context(nc.semaphore("d_%s%d" % (e, i))) for i in range(NDMASEM)] for e in ENGS}
            ccount = {e: 0 for e in ENGS}
            dcount = {e: [0] * NDMASEM for e in ENGS}
            dn = {e: 0 for e in ENGS}
            prevdma = {}
            for op in self.ops:
                if op.fn is None:
                    op.sig = False
                    continue
                if op.dma:
                    k = dn[op.eng] % NDMASEM
                    dn[op.eng] += 1
                    dcount[op.eng][k] += 16
                    op.sem = dsem[op.eng][k]
                    op.val = dcount[op.eng][k]
                    op.sig = True
                    prev = prevdma.get((op.eng, k))
                    if prev is not None:
                        op.waits.append(prev)
                    prevdma[(op.eng, k)] = op
                elif op.sig:
                    ccount[op.eng] += 1
                    op.sem = csem[op.eng]
                    op.val = ccount[op.eng]
            per = {e: [o for o in self.ops if o.eng == e] for e in ENGS}
            finals = {}
            for op in self.ops:
                if op.dma:
                    key = id(op.sem)
                    if key not in finals or finals[key][1] < op.val:
                        finals[key] = (op.sem, op.val)

            def run(e, eng):
                waited = {}
                for op in per[e]:
                    for d in op.waits:
                        if d.sem is None:
                            continue
                        key = id(d.sem)
                        if waited.get(key, 0) >= d.val:
                            continue
                        eng.wait_ge(d.sem, d.val)
                        waited[key] = d.val
                    if op.fn is None:
                        continue
                    ins = op.fn(eng)
                    if op.sig:
                        ins.then_inc(op.sem, 16 if op.dma else 1)
                if e == "sp":
                    for sem, val in finals.values():
                        eng.wait_ge(sem, val)

            with nc.Block() as block:
                @block.tensor
                def _(eng):
                    run("pe", eng)

                @block.scalar
                def _(eng):
                    run("act", eng)

                @block.vector
                def _(eng):
                    run("dve", eng)

                @block.gpsimd
                def _(eng):
                    run("pool", eng)

                @block.sync
                def _(eng):
                    run("sp", eng)


class Arena:
    def __init__(self, ap, words):
        self.ap = ap
        self.words = words
        self.off = 0

    def f32(self, n):
        n2 = (n + 7) // 8 * 8
        assert self.off + n2 <= self.words, ("SBUF arena overflow", self.off, n2, self.words)
        a = self.ap[:, self.off:self.off + n]
        self.off += n2
        return a

    def bf16(self, n):
        w = (n + 1) // 2
        w2 = (w + 7) // 8 * 8
        assert self.off + w2 <= self.words, ("SBUF arena overflow", self.off, w2, self.words)
        a = self.ap[:, self.off:self.off + w].bitcast(BF16)[:, 0:n]
        self.off += w2
        return a


class Builder:
    def __init__(self, seqs, debug=False):
        self.seqs = list(seqs)
        self.NT = sum(self.seqs)
        self.debug = debug
        self.nc = bass.Bass("TRN2", target_bir_lowering=False)
        self.S = Sched(self.nc)
        self.uid = 0

    def u(self, base):
        self.uid += 1
        return "%s#%d" % (base, self.uid)

    def act(self, out, in_, func, r, w, **kw):
        return self.S.add("act", lambda e: e.activation(out=out, in_=in_, func=func, **kw), r, w)

    def tt(self, eng, out, in0, in1, op, r, w):
        return self.S.add(eng, lambda e: e.tensor_tensor(out=out, in0=in0, in1=in1, op=op), r, w)

    def ts(self, eng, out, in0, s1, s2, op0, op1, r, w, **kw):
        if op1 is None:
            return self.S.add(eng, lambda e: e.tensor_scalar(out=out, in0=in0, scalar1=s1, scalar2=None, op0=op0, **kw), r, w)
        return self.S.add(eng, lambda e: e.tensor_scalar(out=out, in0=in0, scalar1=s1, scalar2=s2, op0=op0, op1=op1, **kw), r, w)

    def stt(self, out, in0, scalar, in1, op0, op1, r, w):
        return self.S.add("dve", lambda e: e.scalar_tensor_tensor(out=out, in0=in0, scalar=scalar, in1=in1, op0=op0, op1=op1), r, w)

    def cp(self, eng, out, in_, r, w):
        if eng == "act":
            return self.S.add("act", lambda e: e.copy(out=out, in_=in_), r, w)
        return self.S.add(eng, lambda e: e.tensor_copy(out=out, in_=in_), r, w)

    def red(self, out, in_, r, w, op=ALU.add):
        return self.S.add("dve", lambda e: e.tensor_reduce(out=out, in_=in_, axis=AX.X, op=op), r, w)

    def mm(self, out, lhsT, rhs, start, stop, r, w):
        return self.S.add("pe", lambda e: e.matmul(out, lhsT=lhsT, rhs=rhs, start=start, stop=stop, skip_group_check=True), r, w)

    def tr(self, out, in_, ident, r, w):
        return self.S.add("pe", lambda e: e.transpose(out=out, in_=in_, identity=ident), r, w)

    def dma(self, out, in_, r, w, q="sp"):
        return self.S.add(q, lambda e: e.dma_start(out=out, in_=in_), r, w, dma=True)

    def memset(self, eng, ap, val, w):
        return self.S.add(eng, lambda e: e.memset(ap, val), (), w)

    def rsqrt_small(self, out, in_, scale, r, w):
        n = in_.shape[-1]
        tmp = self.u("rs_tmp")
        self.ts("dve", out, in_, scale, EPS, ALU.mult, ALU.add, r, [tmp] + list(w))
        self.tt("pool", out, out, self.mhalf[:, 0:n], ALU.pow, [tmp] + list(w) + ["const"], w)

    def build(self):
        nc = self.nc
        NT = self.NT
        dbg = "ExternalOutput" if self.debug else "Internal"

        def din(name, shape, dt=F32):
            return nc.dram_tensor(name, list(shape), dt, kind="ExternalInput").ap()

        def dscr(name, shape, dt):
            return nc.dram_tensor(name, list(shape), dt, kind=dbg).ap()

        self.x = din("x", [NT, D])
        self.p = din("p", [NT, PLE])
        self.y = nc.dram_tensor("y", [NT, D], F32, kind="ExternalOutput").ap()
        self.w_in = din("w_in", [D, NCOL])
        self.w_out = din("w_out", [D, D])
        self.w_up = din("w_up", [D, 2 * DFF])
        self.w_down = din("w_down", [DFF, D])
        self.ple_proj = din("ple_proj", [PLE, D])
        self.w_ple_gate = din("w_ple_gate", [D, D])
        self.ln1_g = din("ln1_g", [128, 8])
        self.ln2_g = din("ln2_g", [128, 8])
        self.ple_norm_g = din("ple_norm_g", [1, D])
        self.ple_gate_norm_g = din("ple_gate_norm_g", [128, 8])
        self.cw_l = din("cw_l", [128, 60])
        self.dn_a_log = din("dn_a_log", [1, 8])
        self.dn_dt_bias = din("dn_dt_bias", [1, 8])
        self.dn_norm_g = din("dn_norm_g", [1, 128])
        self.da_qk_norm_g = din("da_qk_norm_g", [1, 128])
        self.da_lambda = din("da_lambda", [1, 256])
        self.da_subln_g = din("da_subln_g", [1, 128])
        self.fw_l = din("fw_l", [128, 66])
        self.fb_l = din("fb_l", [128, 22])
        self.rope_cs = din("rope_cs", [128, 32 * 16])

        self.d_kn = dscr("d_kn", [NT, 512], BF16)
        self.d_v = dscr("d_v", [NT, 512], BF16)
        self.d_kT = dscr("d_kT", [512, NT], BF16)
        self.d_qT = dscr("d_qT", [512, NT], BF16)
        self.d_dz = dscr("d_dz", [NT, 512], BF16)
        self.d_qaT = dscr("d_qaT", [512, NT], BF16)
        self.d_kaT = dscr("d_kaT", [512, NT], BF16)
        self.d_va = dscr("d_va", [NT, 512], BF16)
        self.d_ob = dscr("d_ob", [NT, 512], F32)
        self.d_mix = dscr("d_mix", [NT, D], BF16)
        self.d_x1 = dscr("d_x1", [NT, D], F32)
        self.d_hnT = dscr("d_hnT", [D, NT], BF16)
        self.d_gate = dscr("d_gate", [NT, 16], F32)

        with contextlib.ExitStack() as es:
            WORDS = 51 * 1024
            arena_t = es.enter_context(nc.sbuf_tensor("arena", [128, WORDS], F32))
            self.A = Arena(arena_t[:], WORDS)
            self.psall = es.enter_context(nc.psum_tensor("psall", [128, 4096], F32))[:]
            self.ps = [self.psall[:, i * 512:(i + 1) * 512] for i in range(8)]
            self.psb = [p.bitcast(BF16) for p in self.ps]
            self.setup_consts()
            base = self.A.off
            toff = 0
            gtile = 0
            self.phaseA_weights()
            for S_ in self.seqs:
                self.phaseA_seq(toff, S_, gtile)
                toff += S_
                gtile += S_ // 128
            self.gates_finish()
            self.S.barrier()
            self.A.off = base
            self.phaseB1()
            self.S.barrier()
            self.A.off = base
            self.phaseB2()
            self.S.barrier()
            self.A.off = base
            self.phaseC1()
            self.S.barrier()
            self.A.off = base
            self.phaseC2()
            self.S.barrier()
            self.A.off = base
            self.phaseC3()
            self.S.emit()
        return nc

    def setup_consts(self):
        A = self.A
        ntl = self.NT // 128
        self.identf = A.f32(128)
        self.ident = A.bf16(128)
        self.mhalf = A.f32(16)
        self.UI = A.f32(128)
        self.US = A.f32(128)
        self.LI = A.f32(128)
        self.LS = A.f32(128)
        self.BD = A.f32(128)
        self.onesA = A.f32(128)
        self.onesB = A.f32(128)
        self.rope = A.f32(32 * 16)
        self.dtb = A.f32(8)
        self.nega = A.f32(8)
        self.gqk = A.f32(128)
        self.dng = A.f32(128)
        self.subg = A.f32(128)
        self.lamt = A.f32(256)
        self.lam = A.f32(4)
        self.cw = A.f32(12 * 5)
        self.fw = A.f32(22 * 3)
        self.fb = A.f32(22)
        self.g1 = A.f32(8)
        self.g2 = A.f32(8)
        self.g3 = A.f32(8)
        self.gx = A.f32(ntl * 8)
        self.gb = A.f32(ntl * 8)
        C = ["const"]
        pool = "pool"
        ms = self.memset
        S = self.S
        ms(pool, self.identf, 0.0, C)
        S.add(pool, lambda e: e.affine_select(out=self.identf, in_=self.identf, pattern=[[-1, 128]], compare_op=ALU.not_equal,
                                              fill=1.0, base=0, channel_multiplier=1), C, C)
        self.cp("dve", self.ident, self.identf, C, C)
        ms(pool, self.mhalf, -0.5, C)
        ms(pool, self.BD, 0.0, C)
        ms(pool, self.BD[0:64, 0:64], 1.0, C)
        ms(pool, self.BD[64:128, 64:128], 1.0, C)
        ms(pool, self.onesA, 0.0, C)
        ms(pool, self.onesA[0:64, :], 1.0, C)
        ms(pool, self.onesB, 0.0, C)
        ms(pool, self.onesB[64:128, :], 1.0, C)
        for m, pat, cm, base_, op in ((self.UI, 1, -1, 0, ALU.is_ge), (self.US, 1, -1, 0, ALU.is_gt),
                                      (self.LI, -1, 1, 0, ALU.is_ge), (self.LS, -1, 1, 0, ALU.is_gt)):
            S.add(pool, lambda e, m=m, pat=pat, cm=cm, base_=base_, op=op: e.affine_select(
                out=m, in_=self.BD, pattern=[[pat, 128]], compare_op=op, fill=0.0, base=base_, channel_multiplier=cm), C, C)
        d = self.dma
        d(self.rope, self.rope_cs[:, :], (), C)
        d(self.dtb, self.dn_dt_bias[0:1, :].partition_broadcast(128), (), C)
        d(self.nega, self.dn_a_log[0:1, :].partition_broadcast(128), (), C)
        d(self.gqk, self.da_qk_norm_g[0:1, :].partition_broadcast(128), (), C)
        d(self.dng, self.dn_norm_g[0:1, :].partition_broadcast(128), (), C)
        d(self.subg, self.da_subln_g[0:1, :].partition_broadcast(128), (), C)
        d(self.lamt, self.da_lambda[0:1, :].partition_broadcast(128), (), C)
        d(self.cw, self.cw_l[:, :], (), C)
        d(self.fw, self.fw_l[:, :], (), C)
        d(self.fb, self.fb_l[:, :], (), C)
        d(self.g1, self.ln1_g[:, :], (), C)
        d(self.g2, self.ln2_g[:, :], (), C)
        d(self.g3, self.ple_gate_norm_g[:, :], (), C)
        self.act(self.nega, self.nega, AF.Exp, C, C)
        self.ts("dve", self.nega, self.nega, -1.0, None, ALU.mult, None, C, C)
        self.ts("dve", self.gqk[:, 0:64], self.gqk[:, 0:64], 0.125, None, ALU.mult, None, C, C)
        self.ts("dve", self.subg, self.subg, 1.0 - LAM_INIT, None, ALU.mult, None, C, C)
        lt = self.lamt.rearrange("p (a d) -> p a d", d=64)
        self.tt("dve", lt[:, 0, :], lt[:, 0, :], lt[:, 1, :], ALU.mult, C, C)
        self.tt("dve", lt[:, 2, :], lt[:, 2, :], lt[:, 3, :], ALU.mult, C, C)
        self.red(self.lam[:, 0:1], lt[:, 0, :], C, C)
        self.red(self.lam[:, 1:2], lt[:, 2, :], C, C)
        self.act(self.lam[:, 0:2], self.lam[:, 0:2], AF.Exp, C, C)
        self.tt("dve", self.lam[:, 2:3], self.lam[:, 0:1], self.lam[:, 1:2], ALU.subtract, C, C)
        self.ts("dve", self.lam[:, 3:4], self.lam[:, 2:3], LAM_INIT, -1.0, ALU.add, ALU.mult, C, C)

    def load_weight_bf(self, dst, src, kc_n, ncols, gain, tagw, stage):
        CH = stage[0].shape[-1]
        i = 0
        for kc in range(kc_n):
            for c0 in range(0, ncols, CH):
                c1 = min(ncols, c0 + CH)
                st = stage[i % 2]
                tk = "wstage%d" % (i % 2)
                i += 1
                self.dma(st[:, 0:c1 - c0], src[kc * 128:(kc + 1) * 128, c0:c1], (), [tk])
                eng = "dve" if (i % 2) else "pool"
                if gain is None:
                    self.cp(eng, dst[:, kc, c0:c1], st[:, 0:c1 - c0], [tk], [tagw])
                else:
                    self.ts(eng, dst[:, kc, c0:c1], st[:, 0:c1 - c0], gain[:, kc:kc + 1], None, ALU.mult, None, [tk, "const"], [tagw])

    def phaseA_weights(self):
        A = self.A
        self.win = A.bf16(8 * NCOL).rearrange("p (k n) -> p k n", k=8)
        mark = A.off
        self.stageA = [A.f32(1800), A.f32(1800)]
        self.load_weight_bf(self.win, self.w_in, 8, NCOL, self.g1, "win", self.stageA)
        self.S.barrier()
        A.off = mark
        Smax = max(self.seqs)
        self.XW = Smax + 4
        self.xnT = A.bf16(8 * self.XW).rearrange("p (k n) -> p k n", k=8)
        self.xin = [A.f32(1024), A.f32(1024)]
        self.xnb = [A.bf16(1024), A.bf16(1024)]
        self.junk = A.bf16(1024)
        self.sm = A.f32(80)
        self.tmA = A.f32(1024)
        self.tmB = A.f32(512)
        self.qkb = A.bf16(1024)
        self.trs = A.bf16(1024)
        self.dzb = A.bf16(512)
        self.avb = A.bf16(512)
        self.presb = [A.f32(516), A.f32(516)]
        self.accb = [A.f32(512), A.f32(512)]
        self.fmb = [A.bf16(512), A.bf16(512)]
        self.sq2 = A.f32(1024)
        self.trs2 = [A.bf16(512), A.bf16(512)]
        self.tok = A.bf16(4 * 1536).rearrange("p (t c) -> p t c", t=4)
        self.nrmb = [A.bf16(1024), A.bf16(1024)]
        self.rq8 = [A.f32(8), A.f32(8)]
        self.sq = A.f32(1024)

    def phaseA_seq(self, T0, S_, gt0):
        ps, psb = self.ps, self.psb
        nt = S_ // 128
        xnT = self.xnT
        XT = lambda j: ("xnT", j)
        prog = {"a0": 0}
        self.memset("pool", xnT[:, :, 0:2], 0.0, ["xnTpadL"])
        self.memset("pool", xnT[:, :, 2 + S_:2 + S_ + 2], 0.0, ["xnTpadR"] + [XT(j) for j in range(nt, nt + 1)])

        def a0_a(i):
            b = i % 2
            xin, xnb = self.xin[b], self.xnb[b]
            tx, tn = "xin%d" % b, "xnb%d" % b
            self.dma(xin, self.x[T0 + i * 128:T0 + (i + 1) * 128, :], (), [tx], q="act")
            ss = self.sm[:, b:b + 1]
            tss = "ssA%d" % b
            self.act(self.junk, xin, AF.Square, [tx], ["junk", tss], accum_out=ss)
            rs = self.sm[:, 2 + b:3 + b]
            trs = "rsA%d" % b
            self.rsqrt_small(rs, ss, 1.0 / D, [tss], [trs])
            self.ts("dve", xnb, xin, rs, None, ALU.mult, None, [tx, trs], [tn])

        def a0_b(i):
            b = i % 2
            xnb, tn = self.xnb[b], "xnb%d" % b
            for kc in range(8):
                self.tr(psb[0][:, kc * 128:(kc + 1) * 128], xnb[:, kc * 128:(kc + 1) * 128], self.ident, [tn, "const"], ["ps0"])
            self.cp("act", xnT[:, :, 2 + i * 128:2 + (i + 1) * 128], psb[0][:, 0:1024].rearrange("p (k n) -> p k n", k=8),
                    ["ps0"], [XT(i)])
            prog["a0"] = i + 1

        def gen_a0():
            a0_a(0)
            yield
            for i in range(nt):
                if i + 1 < nt:
                    a0_a(i + 1)
                a0_b(i)
                yield

        def tm_s1(i):
            gt = gt0 + i
            cols = slice(2 + i * 128, 2 + (i + 1) * 128)
            rows = slice(T0 + i * 128, T0 + (i + 1) * 128)

            def proj(c0, n, out, tok):
                for kc in range(8):
                    self.mm(out, xnT[:, kc, cols], self.win[:, kc, c0:c0 + n], kc == 0, kc == 7, [XT(i), "win"], [tok])
            proj(2064, 512, ps[2][:, 0:512], "ps2")
            proj(2576, 512, ps[3][:, 0:512], "ps3")
            proj(2048, 16, ps[5][:, 0:16], "ps5")
            self.tt("dve", self.gx[:, gt * 8:(gt + 1) * 8], ps[5][:, 0:8], self.dtb, ALU.add, ["ps5", "const"], ["gx"])
            self.cp("dve", self.gb[:, gt * 8:(gt + 1) * 8], ps[5][:, 8:16], ["ps5"], ["gb"])
            proj(1536, 512, ps[1][:, 0:512], "ps1")
            self.cp("act", self.dzb, ps[1][:, 0:512], ["ps1"], ["dzb"])
            self.dma(self.d_dz[rows, :], self.dzb, ["dzb"], [])
            proj(3088, 512, ps[1][:, 0:512], "ps1")
            self.cp("act", self.avb, ps[1][:, 0:512], ["ps1"], ["avb"])
            self.dma(self.d_va[rows, :], self.avb, ["avb"], [])

        def tm_s2(i):
            tmA = self.tmA
            self.act(self.sq[:, 0:512], ps[2][:, 0:512], AF.Square, ["ps2"], ["sq"])
            self.act(self.sq[:, 512:1024], ps[3][:, 0:512], AF.Square, ["ps3"], ["sq"])
            ssq = self.sm[:, 8:24]
            self.red(ssq, self.sq.rearrange("p (g d) -> p g d", d=64), ["sq"], ["ssq"])
            rq = self.sm[:, 24:40]
            self.rsqrt_small(rq, ssq, 1.0 / 64, ["ssq"], ["rq"])
            for half, bk in ((0, 2), (1, 3)):
                o3 = tmA[:, half * 512:(half + 1) * 512].rearrange("p (g d) -> p g d", d=64)
                self.tt("dve", o3, ps[bk][:, 0:512].rearrange("p (g d) -> p g d", d=64),
                        rq[:, half * 8:(half + 1) * 8].unsqueeze(2).to_broadcast([128, 8, 64]), ALU.mult, ["ps%d" % bk, "rq"], ["tmA%d" % half])
                self.tt("pool", o3, o3, self.gqk[:, half * 64:(half + 1) * 64].unsqueeze(1).to_broadcast([128, 8, 64]), ALU.mult,
                        ["tmA%d" % half, "const"], ["tmA%d" % half])

        def tm_s3(i):
            rows = slice(T0 + i * 128, T0 + (i + 1) * 128)
            tmA = self.tmA
            a3 = tmA.rearrange("p (g d) -> p g d", d=64)
            cosb = self.rope[:, i * 16:i * 16 + 8].unsqueeze(1).to_broadcast([128, 16, 8])
            sinb = self.rope[:, i * 16 + 8:i * 16 + 16].unsqueeze(1).to_broadcast([128, 16, 8])
            tB = self.tmB
            t1 = tB[:, 0:128].rearrange("p (g d) -> p g d", d=8)
            t2 = tB[:, 128:256].rearrange("p (g d) -> p g d", d=8)
            t3 = tB[:, 256:384].rearrange("p (g d) -> p g d", d=8)
            t4 = tB[:, 384:512].rearrange("p (g d) -> p g d", d=8)
            x1, x2 = a3[:, :, 0:8], a3[:, :, 8:16]
            tA = ["tmA0", "tmA1"]
            self.tt("dve", t1, x1, cosb, ALU.mult, tA + ["const"], ["tB1"])
            self.tt("dve", t2, x2, sinb, ALU.mult, tA + ["const"], ["tB2"])
            self.tt("dve", t3, x2, cosb, ALU.mult, tA + ["const"], ["tB3"])
            self.tt("dve", t4, x1, sinb, ALU.mult, tA + ["const"], ["tB4"])
            self.cp("act", self.qkb, tmA, tA, ["qkb"])
            q3 = self.qkb.rearrange("p (g d) -> p g d", d=64)
            self.tt("dve", q3[:, :, 0:8], t1, t2, ALU.subtract, ["tB1", "tB2", "qkb"], ["qkb"])
            self.tt("dve", q3[:, :, 8:16], t3, t4, ALU.add, ["tB3", "tB4", "qkb"], ["qkb"])
            for j in range(8):
                self.tr(psb[4][:, j * 128:(j + 1) * 128], self.qkb[:, j * 128:(j + 1) * 128], self.ident, ["qkb", "const"], ["ps4"])
            self.cp("act", self.trs, psb[4][:, 0:1024], ["ps4"], ["trs"])
            t3d = self.trs.rearrange("p (j n) -> p j n", j=8)
            self.dma(self.d_qaT[:, rows].rearrange("(h p) n -> p h n", p=128), t3d[:, 0:4, :], ["trs"], [])
            self.dma(self.d_kaT[:, rows].rearrange("(h p) n -> p h n", p=128), t3d[:, 4:8, :], ["trs"], [])

        def gen_tm():
            while prog["a0"] < 1:
                yield
            tm_s1(0)
            yield
            for i in range(nt):
                tm_s2(i)
                yield
                if i + 1 < nt:
                    while prog["a0"] < i + 2:
                        yield
                    tm_s1(i + 1)
                    yield
                tm_s3(i)
                yield

        def gen_fm():
            BL = min(512, S_)
            ntb = BL // 128
            for blk in range(S_ // BL):
                t0 = blk * BL
                need = min(nt, blk * ntb + ntb + 1)
                while prog["a0"] < need:
                    yield
                rd = [XT(j) for j in range(max(0, blk * ntb - 1), min(nt, blk * ntb + ntb + 1))] + ["xnTpadL", "xnTpadR", "win"]

                def fm_mm(c):
                    bk = 6 + c % 2
                    for kc in range(8):
                        self.mm(ps[bk][:, 0:BL], self.win[:, kc, c * 128:(c + 1) * 128], xnT[:, kc, 2 + t0:2 + t0 + BL], kc == 0, kc == 7,
                                rd, ["ps%d" % bk])
                    for kc in range(8):
                        self.mm(ps[5][:, 16:18], self.win[:, kc, c * 128:(c + 1) * 128], xnT[:, kc, t0:t0 + 2], kc == 0, kc == 7, rd, ["ps5"])
                    for kc in range(8):
                        self.mm(ps[5][:, 18:20], self.win[:, kc, c * 128:(c + 1) * 128], xnT[:, kc, t0 + BL + 2:t0 + BL + 4], kc == 0, kc == 7,
                                rd, ["ps5"])
                    pre = self.presb[c % 2]
                    tpre = "pre%d" % (c % 2)
                    self.cp("act", pre[:, 2:2 + BL], ps[bk][:, 0:BL], ["ps%d" % bk], [tpre])
                    self.cp("dve", pre[:, 0:2], ps[5][:, 16:18], ["ps5"], [tpre])
                    self.cp("dve", pre[:, BL + 2:BL + 4], ps[5][:, 18:20], ["ps5"], [tpre])

                fm_mm(0)
                yield
                for c in range(12):
                    if c + 1 < 12:
                        fm_mm(c + 1)
                    pre, acc, fm = self.presb[c % 2], self.accb[c % 2], self.fmb[c % 2]
                    tpre, tacc, tfm = "pre%d" % (c % 2), "acc%d" % (c % 2), "fm%d" % (c % 2)
                    a_ = acc[:, 0:BL]
                    self.ts("dve", a_, pre[:, 0:BL], self.cw[:, c * 5:c * 5 + 1], None, ALU.mult, None, [tpre, "const"], [tacc])
                    for j in range(1, 5):
                        self.stt(a_, pre[:, j:j + BL], self.cw[:, c * 5 + j:c * 5 + j + 1], a_, ALU.mult, ALU.add, [tpre, tacc, "const"], [tacc])
                    self.act(fm[:, 0:BL], a_, AF.Silu, [tacc], [tfm])
                    yield
                    for t in range(ntb):
                        self.tr(psb[4][:, t * 128:(t + 1) * 128], fm[:, t * 128:(t + 1) * 128], self.ident, [tfm, "const"], ["ps4"])
                    self.cp("act", self.tok[:, 0:ntb, c * 128:(c + 1) * 128],
                            psb[4][:, 0:ntb * 128].rearrange("p (t n) -> p t n", t=ntb), ["ps4"], ["tok"])
                    yield

                def l2_a(t):
                    tk = self.tok[:, t, :]
                    nb_ = t % 2
                    nrm = self.nrmb[nb_]
                    self.act(self.sq2, tk[:, 0:1024], AF.Square, ["tok"], ["sq2"])
                    ssq = self.sm[:, 40 + nb_ * 8:48 + nb_ * 8]
                    self.red(ssq, self.sq2.rearrange("p (g d) -> p g d", d=128), ["sq2"], ["ssq2%d" % nb_])
                    rq = self.sm[:, 56 + nb_ * 4:60 + nb_ * 4]
                    rk = self.sm[:, 24 + 40:28 + 40] if False else None
                    rq8 = self.rq8[nb_]
                    self.ts("dve", rq8, ssq, EPS, None, ALU.add, None, ["ssq2%d" % nb_], ["rq2a%d" % nb_])
                    self.tt("pool", rq8, rq8, self.mhalf[:, 0:8], ALU.pow, ["rq2a%d" % nb_, "const"], ["rq2%d" % nb_])
                    self.ts("dve", rq8[:, 0:4], rq8[:, 0:4], 128.0 ** -0.5, None, ALU.mult, None, ["rq2%d" % nb_], ["rq2%d" % nb_])
                    self.tt("dve", nrm.rearrange("p (g d) -> p g d", d=128), tk[:, 0:1024].rearrange("p (g d) -> p g d", d=128),
                            rq8.unsqueeze(2).to_broadcast([128, 8, 128]), ALU.mult, ["tok", "rq2%d" % nb_], ["nrm%d" % nb_])

                def l2_b(t):
                    rows = slice(T0 + t0 + t * 128, T0 + t0 + (t + 1) * 128)
                    tk = self.tok[:, t, :]
                    nb_ = t % 2
                    nrm = self.nrmb[nb_]
                    tn_ = "nrm%d" % nb_
                    self.dma(self.d_kn[rows, :], nrm[:, 512:1024], [tn_], [])
                    self.dma(self.d_v[rows, :], tk[:, 1024:1536], ["tok"], [])
                    for half, dst in ((0, self.d_qT), (1, self.d_kT)):
                        for j in range(4):
                            self.tr(psb[4][:, j * 128:(j + 1) * 128], nrm[:, half * 512 + j * 128:half * 512 + (j + 1) * 128],
                                    self.ident, [tn_, "const"], ["ps4"])
                        tr2 = self.trs2[half]
                        self.cp("act", tr2, psb[4][:, 0:512], ["ps4"], ["trs2%d" % half])
                        self.dma(dst[:, rows].rearrange("(h p) n -> p h n", p=128), tr2.rearrange("p (j n) -> p j n", j=4), ["trs2%d" % half], [])

                l2_a(0)
                yield
                for t in range(ntb):
                    if t + 1 < ntb:
                        l2_a(t + 1)
                    l2_b(t)
                    yield

        g0, g1, g2 = gen_a0(), gen_tm(), gen_fm()
        alive = {0: g0, 1: g1, 2: g2}
        plan = (0, 2, 2, 1, 2, 2, 1, 2, 2, 1, 2, 2)
        while alive:
            for k in plan:
                g = alive.get(k)
                if g is None:
                    continue
                try:
                    next(g)
                except StopIteration:
                    del alive[k]

    def gates_finish(self):
        ntl = self.NT // 128
        gx3 = self.gx.rearrange("p (t c) -> p t c", c=8)
        self.act(self.gx, self.gx, AF.Exp, ["gx"], ["gx"])
        self.act(self.gx, self.gx, AF.Ln, ["gx"], ["gx"], bias=1.0)
        self.tt("dve", gx3, gx3, self.nega.unsqueeze(1).to_broadcast([128, ntl, 8]), ALU.mult, ["gx", "const"], ["gx"])
        self.act(self.gb, self.gb, AF.Tanh, ["gb"], ["gb"], scale=0.5)
        self.ts("dve", self.gb, self.gb, 0.5, 0.5, ALU.mult, ALU.add, ["gb"], ["gb"])
        if self.debug:
            for t in range(ntl):
                self.dma(self.d_gate[t * 128:(t + 1) * 128, 0:8], self.gx[:, t * 8:(t + 1) * 8], ["gx"], [])
                self.dma(self.d_gate[t * 128:(t + 1) * 128, 8:16], self.gb[:, t * 8:(t + 1) * 8], ["gb"], [])

    def phaseB1(self):
        if self.debug == "nodn":
            dn_in = self.nc.dram_tensor("dn_in", [self.NT, 512], BF16, kind="ExternalInput").ap()
            for i in range(self.NT // 128):
                self.dma(self.d_mix[i * 128:(i + 1) * 128, 0:512], dn_in[i * 128:(i + 1) * 128, :], (), [])
            return
        A = self.A
        self.d_of = self.nc.dram_tensor("d_of", [self.NT, 512], F32, kind="Internal").ap()
        v3 = lambda a: a.rearrange("p (h d) -> p h d", h=4)
        identbc = self.identf.unsqueeze(1).to_broadcast([128, 4, 128])

        def mkbufs(sid):
            B = {}
            for nm in ("kn0", "kn1", "vt0", "vt1", "kT0", "kT1", "qT0", "qT1", "kb", "kbT", "kbg", "vb", "kdec", "P0", "P1", "Q0", "Q1",
                       "Rb", "qkT", "wT", "vnew", "Sb"):
                B[nm] = A.bf16(512)
            for nm in ("Rg", "E", "Es", "Xf", "Rf", "usb", "tmp", "ot0", "ot1", "Sf"):
                B[nm] = A.f32(512)
            B["gsm"] = A.f32(24)
            B["e20"] = A.f32(24)
            B["bg"] = A.f32(8)
            return B

        def dn_pass(sid, T0, S_, gt0, dr, B, bk):
            ps = [self.ps[b] for b in bk]
            psb = [self.psb[b] for b in bk]
            pt = ["ps%d" % b for b in bk]
            T = lambda nm: "%s_%d" % (nm, sid)
            nt = S_ // 128
            if dr == 0:
                CUM, ARGL, RMASK, MINCL, MSTR = self.UI, self.LS, self.UI, self.UI, self.US
                tiles = range(nt)
                chunks = (0, 64)
            else:
                CUM, ARGL, RMASK, MINCL, MSTR = self.LI, self.US, self.LI, self.LI, self.LS
                tiles = range(nt - 1, -1, -1)
                chunks = (64, 0)
            Sf, Sb = B["Sf"], B["Sb"]
            self.memset("pool", Sf, 0.0, [T("Sf")])
            self.memset("pool", Sb, 0.0, [T("Sb")])
            cnt = 0
            for i in tiles:
                b = cnt % 2
                cnt += 1
                gt = gt0 + i
                rows = slice(T0 + i * 128, T0 + (i + 1) * 128)
                kn_t, v_t, kT_t, qT_t = B["kn%d" % b], B["vt%d" % b], B["kT%d" % b], B["qT%d" % b]
                tkn, tv, tkT, tqT = T("kn%d" % b), T("vt%d" % b), T("kT%d" % b), T("qT%d" % b)
                self.dma(kn_t, self.d_kn[rows, :], (), [tkn], q="act")
                self.dma(v_t, self.d_v[rows, :], (), [tv], q="act")
                self.dma(v3(kT_t), self.d_kT[:, rows].rearrange("(h p) n -> p h n", p=128), (), [tkT], q="act")
                self.dma(v3(qT_t), self.d_qT[:, rows].rearrange("(h p) n -> p h n", p=128), (), [tqT], q="act")
                g4 = self.gx[:, gt * 8 + dr * 4:gt * 8 + dr * 4 + 4]
                be4 = self.gb[:, gt * 8 + dr * 4:gt * 8 + dr * 4 + 4]
                bch = lambda a, n=128: a.unsqueeze(2).to_broadcast([n, 4, 128])
                gsm, e20, bg = B["gsm"], B["e20"], B["bg"]
                kb, kbT, kbg, vb, kdec = B["kb"], B["kbT"], B["kbg"], B["vb"], B["kdec"]
                Rg, E, Es, Xf, Rf, Rb, qkT = B["Rg"], B["E"], B["Es"], B["Xf"], B["Rf"], B["Rb"], B["qkT"]
                usb, wT, vnew, tmp = B["usb"], B["wT"], B["vnew"], B["tmp"]
                self.mm(ps[3][:, 0:4], CUM, g4, True, True, ["const", "gx"], [pt[3]])
                self.mm(ps[3][:, 4:8], self.BD, g4, True, True, ["const", "gx"], [pt[3]])
                self.mm(ps[3][:, 8:12], self.onesA, g4, True, True, ["const", "gx"], [pt[3]])
                self.mm(ps[3][:, 12:16], self.onesB, g4, True, True, ["const", "gx"], [pt[3]])
                self.cp("dve", gsm[:, 0:16], ps[3][:, 0:16], [pt[3]], [T("gsm")])
                self.tt("dve", gsm[:, 16:20], gsm[:, 4:8], gsm[:, 0:4], ALU.subtract, [T("gsm")], [T("gsm")])
                self.act(e20[:, 0:20], gsm[:, 0:20], AF.Exp, [T("gsm")], [T("e20")])
                egc, etA, etB, erest = e20[:, 0:4], e20[:, 8:12], e20[:, 12:16], e20[:, 16:20]
                self.tt("dve", bg[:, 0:4], be4, egc, ALU.mult, ["gb", T("e20")], [T("bg")])
                yield
                self.tt("dve", v3(kb), v3(kn_t), bch(be4), ALU.mult, [tkn, "gb"], [T("kb")])
                for h in range(4):
                    self.tr(psb[3][:, 512 + h * 128:512 + (h + 1) * 128], kb[:, h * 128:(h + 1) * 128], self.ident, [T("kb"), "const"], [pt[3]])
                self.cp("act", kbT, psb[3][:, 512:1024], [pt[3]], [T("kbT")])
                self.tt("dve", v3(kbg), v3(kn_t), bch(bg[:, 0:4]), ALU.mult, [tkn, T("bg")], [T("kbg")])
                self.tt("pool", v3(vb), v3(v_t), bch(be4), ALU.mult, [tv, "gb"], [T("vb")])
                self.tt("pool", v3(kdec), v3(kn_t), bch(erest), ALU.mult, [tkn, T("e20")], [T("kdec")])
                self.tt("pool", v3(Rg), RMASK.unsqueeze(1).to_broadcast([128, 4, 128]), bch(g4), ALU.mult, ["const", "gx"], [T("Rg")])
                self.mm(ps[2][:, 0:512], ARGL, Rg, True, True, ["const", T("Rg")], [pt[2]])
                for h in range(4):
                    hs = slice(h * 128, (h + 1) * 128)
                    self.mm(ps[0][:, hs], kT_t[:, hs], kbT[:, hs], True, True, [tkT, T("kbT")], [pt[0]])
                    self.mm(ps[1][:, hs], kT_t[:, hs], qT_t[:, hs], True, True, [tkT, tqT], [pt[1]])
                self.act(E, ps[2][:, 0:512], AF.Exp, [pt[2]], [T("E")])
                self.tt("pool", v3(E), v3(E), MINCL.unsqueeze(1).to_broadcast([128, 4, 128]), ALU.mult, [T("E"), "const"], [T("E")])
                self.tt("pool", v3(Es), v3(E), MSTR.unsqueeze(1).to_broadcast([128, 4, 128]), ALU.mult, [T("E"), "const"], [T("Es")])
                self.tt("dve", Xf, ps[0][:, 0:512], Es, ALU.mult, [pt[0], T("Es")], [T("Xf")])
                P, Q = B["P0"], B["Q0"]
                self.cp("act", P, Xf, [T("Xf")], [T("P0")])
                self.tt("pool", v3(Rf), identbc, v3(Xf), ALU.subtract, ["const", T("Xf")], [T("Rf")])
                self.cp("act", Rb, Rf, [T("Rf")], [T("Rb")])
                self.tt("dve", qkT, ps[1][:, 0:512], E, ALU.mult, [pt[1], T("E")], [T("qkT")])
                for h in range(4):
                    self.tr(psb[3][:, 512 + h * 128:512 + (h + 1) * 128], P[:, h * 128:(h + 1) * 128], self.ident, [T("P0"), "const"], [pt[3]])
                self.cp("act", Q, psb[3][:, 512:1024], [pt[3]], [T("Q0")])
                yield
                pi = 0
                for k in range(1, 6):
                    tP, tQ = T("P%d" % pi), T("Q%d" % pi)
                    pn = 1 - pi
                    tPn, tQn = T("P%d" % pn), T("Q%d" % pn)
                    Pn, Qn = B["P%d" % pn], B["Q%d" % pn]
                    for h in range(4):
                        hs = slice(h * 128, (h + 1) * 128)
                        self.mm(ps[1][:, hs], P[:, hs], Q[:, hs], True, True, [tP, tQ], [pt[1]])
                    if k < 5:
                        for h in range(4):
                            hs = slice(h * 128, (h + 1) * 128)
                            self.mm(ps[0][:, hs], Q[:, hs], P[:, hs], True, True, [tP, tQ], [pt[0]])
                    self.cp("dve", Qn, ps[1][:, 0:512], [pt[1]], [tQn])
                    if k < 5:
                        self.cp("act", Pn, ps[0][:, 0:512], [pt[0]], [tPn])
                    for h in range(4):
                        hs = slice(h * 128, (h + 1) * 128)
                        self.mm(ps[2][:, hs], Qn[:, hs], Rb[:, hs], True, True, [tQn, T("Rb")], [pt[2]])
                    self.tt("dve", Rf, Rf, ps[2][:, 0:512], ALU.add, [T("Rf"), pt[2]], [T("Rf")])
                    self.cp("act", Rb, Rf, [T("Rf")], [T("Rb")])
                    P, Q, pi = Pn, Qn, pn
                    yield
                for h in range(4):
                    hs = slice(h * 128, (h + 1) * 128)
                    self.mm(ps[2][:, hs], Rb[:, hs], vb[:, hs], True, True, [T("Rb"), T("vb")], [pt[2]])
                    self.mm(ps[0][:, hs], kbg[:, hs], Rb[:, hs], True, True, [T("Rb"), T("kbg")], [pt[0]])
                self.cp("dve", usb, ps[2][:, 0:512], [pt[2]], [T("usb")])
                self.cp("act", wT, ps[0][:, 0:512], [pt[0]], [T("wT")])
                yield
                ob_ = B["ot%d" % b]
                tot = T("ot%d" % b)
                for r0 in chunks:
                    rs = slice(r0, r0 + 64)
                    etc = etA if r0 == 0 else etB
                    for h in range(4):
                        hs = slice(h * 128, (h + 1) * 128)
                        self.mm(ps[1][rs, hs], wT[:, h * 128 + r0:h * 128 + r0 + 64], Sb[:, hs], True, True, [T("wT"), T("Sb")], [pt[1]])
                    for h in range(4):
                        hs = slice(h * 128, (h + 1) * 128)
                        self.mm(ps[2][rs, hs], qT_t[:, h * 128 + r0:h * 128 + r0 + 64], Sb[:, hs], True, True, [tqT, T("Sb")], [pt[2]])
                    self.tt("dve", vnew[rs, :], usb[rs, :], ps[1][rs, 0:512], ALU.subtract, [T("usb"), pt[1]], [T("vnew")])
                    for h in range(4):
                        hs = slice(h * 128, (h + 1) * 128)
                        self.mm(ps[0][:, hs], kdec[rs, hs], vnew[rs, hs], True, True, [T("kdec"), T("vnew")], [pt[0]])
                    for h in range(4):
                        hs = slice(h * 128, (h + 1) * 128)
                        self.mm(ps[3][rs, hs], qkT[rs, h * 128 + r0:h * 128 + r0 + 64], vnew[rs, hs], True, True, [T("qkT"), T("vnew")], [pt[3]])
                    for h in range(4):
                        hs = slice(h * 128, (h + 1) * 128)
                        self.stt(Sf[:, hs], Sf[:, hs], etc[:, h:h + 1], ps[0][:, hs], ALU.mult, ALU.add, [T("Sf"), T("e20"), pt[0]], [T("Sf")])
                    self.cp("act", Sb, Sf, [T("Sf")], [T("Sb")])
                    self.tt("dve", v3(tmp)[rs], v3(ps[2][:, 0:512])[rs], bch(egc[rs, :], 64), ALU.mult, [pt[2], T("e20")], [T("tmpB")])
                    self.tt("dve", ob_[rs, :], tmp[rs, :], ps[3][rs, 0:512], ALU.add, [T("tmpB"), pt[3]], [tot])
                    yield
                dst = self.d_ob if dr == 1 else self.d_of
                self.dma(dst[rows, :], ob_, [tot], [("od%d" % dr, gt)])

        base = A.off
        BS = [mkbufs(k) for k in range(4)]
        offs = []
        T0 = 0
        gt0 = 0
        for S_ in self.seqs:
            offs.append((T0, S_, gt0))
            T0 += S_
            gt0 += S_ // 128
        groups = []
        i = 0
        while i < len(offs):
            if i + 1 < len(offs) and offs[i][1] == offs[i + 1][1]:
                groups.append([offs[i], offs[i + 1]])
                i += 2
            else:
                groups.append([offs[i]])
                i += 1
        for grp in groups:
            gens = []
            for k, (T0_, S_, gt0_) in enumerate(grp):
                gens.append(dn_pass(2 * k, T0_, S_, gt0_, 1, BS[2 * k], (0, 1, 2, 3)))
                gens.append(dn_pass(2 * k + 1, T0_, S_, gt0_, 0, BS[2 * k + 1], (4, 5, 6, 7)))
            for k, g in enumerate(gens):
                for _ in range((len(gens) - 1 - k) * (10 // len(gens))):
                    next(g)
            alive = list(gens)
            while alive:
                for g in list(alive):
                    try:
                        next(g)
                    except StopIteration:
                        alive.remove(g)
        self.S.barrier()
        A.off = base
        obt = [A.f32(512), A.f32(512)]
        oft = [A.f32(512), A.f32(512)]
        dzt = [A.bf16(512), A.bf16(512)]
        osum = [A.f32(512), A.f32(512)]
        sq = A.f32(512)
        th = [A.f32(512), A.f32(512)]
        mixo = [A.bf16(512), A.bf16(512)]
        smc = [A.f32(8), A.f32(8)]
        bch = lambda a, n=128: a.unsqueeze(2).to_broadcast([n, 4, 128])
        for gt in range(self.NT // 128):
            b = gt % 2
            rows = slice(gt * 128, (gt + 1) * 128)
            X = lambda nm: "%s%d" % (nm, b)
            self.dma(obt[b], self.d_ob[rows, :], [("od1", gt)], [X("obt")], q="act")
            self.dma(oft[b], self.d_of[rows, :], [("od0", gt)], [X("oft")], q="act")
            self.dma(dzt[b], self.d_dz[rows, :], (), [X("dzt")], q="act")
            self.tt("pool", osum[b], oft[b], obt[b], ALU.add, [X("oft"), X("obt")], [X("osum")])
            self.act(sq, osum[b], AF.Square, [X("osum")], ["sqB"])
            self.red(smc[b][:, 0:4], v3(sq), ["sqB"], [X("ssD")])
            self.rsqrt_small(smc[b][:, 4:8], smc[b][:, 0:4], 1.0 / 128, [X("ssD")], [X("rsD")])
            self.tt("dve", v3(osum[b]), v3(osum[b]), bch(smc[b][:, 4:8]), ALU.mult, [X("osum"), X("rsD")], [X("osum")])
            self.tt("pool", v3(osum[b]), v3(osum[b]), self.dng.unsqueeze(1).to_broadcast([128, 4, 128]), ALU.mult, [X("osum"), "const"], [X("osum")])
            self.act(th[b], dzt[b], AF.Tanh, [X("dzt")], [X("thB")], scale=0.5)
            self.stt(th[b], th[b], 1.0, dzt[b], ALU.add, ALU.mult, [X("thB"), X("dzt")], [X("thB")])
            self.stt(mixo[b], osum[b], 0.5, th[b], ALU.mult, ALU.mult, [X("osum"), X("thB")], [X("mixo")])
            self.dma(self.d_mix[rows, 0:512], mixo[b], [X("mixo")], [])

    def phaseB2(self):
        A, ps = self.A, self.ps
        Smax = max(self.seqs)
        ntm = Smax // 128
        kaT = [A.bf16(Smax) for _ in range(3)]
        va = [A.bf16(ntm * 130).rearrange("p (t d) -> p t d", d=130) for _ in range(3)]
        qa = [A.bf16(512), A.bf16(512), A.bf16(512)]
        PT = [A.bf16(1024).rearrange("p (c n) -> p c n", c=2) for _ in range(3)]
        o1 = A.f32(128)
        oo = A.f32(128)
        ob = [A.bf16(128), A.bf16(128)]
        sm = A.f32(16)
        sqo = A.f32(128)
        steps = []
        T0 = 0
        for S_ in self.seqs:
            nt = S_ // 128
            QB = min(512, S_)
            for h in range(4):
                for qb in range(S_ // QB):
                    for kt in range(nt):
                        steps.append((T0, S_, h, qb, kt))
            T0 += S_
        state = {"hkey": None, "hcount": -1, "qkey": None, "qcount": -1}

        def prep_qk(n):
            T0, S_, h, qb, kt = steps[n]
            nt = S_ // 128
            QB = min(512, S_)
            if state["hkey"] != (T0, h):
                state["hkey"] = (T0, h)
                state["hcount"] += 1
                hb = state["hcount"] % 3
                tk, tv = "kaT%d" % hb, "va%d" % hb
                self.dma(kaT[hb][:, 0:S_], self.d_kaT[h * 128:(h + 1) * 128, T0:T0 + S_], (), [tk])
                self.dma(va[hb][:, 0:nt, 0:128], self.d_va[T0:T0 + S_, h * 128:(h + 1) * 128].rearrange("(t p) d -> p t d", p=128), (), [tv])
                self.memset("pool", va[hb][:, 0:nt, 128:129], 1.0, [tv])
            if state["qkey"] != (T0, h, qb):
                state["qkey"] = (T0, h, qb)
                state["qcount"] += 1
                qq = state["qcount"] % 3
                self.dma(qa[qq][:, 0:QB], self.d_qaT[h * 128:(h + 1) * 128, T0 + qb * QB:T0 + (qb + 1) * QB], (), ["qa%d" % qq])
            hb = state["hcount"] % 3
            qq = state["qcount"] % 3
            a = 2 * (n % 2)
            for comp in range(2):
                r0 = comp * 64
                self.mm(ps[a + comp][:, 0:QB], kaT[hb][r0:r0 + 64, kt * 128:(kt + 1) * 128], qa[qq][r0:r0 + 64, 0:QB], True, True,
                        ["kaT%d" % hb, "qa%d" % qq], ["ps%d" % (a + comp)])
            return hb

        oc = {"n": 0, "q": 0}
        accS = [A.f32(3 * 512).rearrange("p (b n) -> p b n", b=3) for _ in range(2)]

        def do_exp(n):
            T0, S_, h, qb, kt = steps[n]
            QB = min(512, S_)
            a = 2 * (n % 2)
            pb = n % 3
            tp = "PT%d" % pb
            if QB == 512:
                self.act(PT[pb].rearrange("p c n -> p (c n)"), self.psall[:, a * 512:(a + 2) * 512], AF.Exp, ["ps%d" % a, "ps%d" % (a + 1)], [tp])
            else:
                for comp in range(2):
                    self.act(PT[pb][:, comp, 0:QB], ps[a + comp][:, 0:QB], AF.Exp, ["ps%d" % (a + comp)], [tp])

        def do_pv(n, hb):
            T0, S_, h, qb, kt = steps[n]
            nt = S_ // 128
            QB = min(512, S_)
            nqs = QB // 128
            pb = n % 3
            tp = "PT%d" % pb
            tv = "va%d" % hb
            place = {}
            idx = 0
            for comp in range(2):
                for qs in range(nqs):
                    place[(comp, qs)] = (4 + idx // 3, (idx % 3) * 129)
                    idx += 1
            if kt == 0:
                state["started"] = set()
            started = state["started"]
            for comp in range(2):
                for qs in range(nqs):
                    bk, col = place[(comp, qs)]
                    st = bk not in started
                    started.add(bk)
                    self.mm(ps[bk][:, col:col + 129], PT[pb][:, comp, qs * 128:(qs + 1) * 128], va[hb][:, kt, 0:129], st, kt == nt - 1,
                            [tp, tv], ["ps%d" % bk])
            if kt == nt - 1:
                qp = oc["q"] % 2
                oc["q"] += 1
                aS = accS[qp]
                ta = "accS%d" % qp
                nacc = 2 * nqs
                for bk in sorted(set(b for b, _ in place.values())):
                    ncols = 129 * min(3, nacc - 3 * (bk - 4))
                    self.cp("dve", aS[:, bk - 4, 0:ncols], ps[bk][:, 0:ncols], ["ps%d" % bk], [ta])
                for qs in range(nqs):
                    b0, c0 = place[(0, qs)]
                    b1, c1 = place[(1, qs)]
                    oq = oc["n"] % 2
                    oc["n"] += 1
                    a0_, a1_ = aS[:, b0 - 4, :], aS[:, b1 - 4, :]
                    self.S.add("dve", lambda e, a0_=a0_, c0=c0: e.reciprocal(out=sm[:, 0:1], in_=a0_[:, c0 + 128:c0 + 129]), [ta], ["smr0"])
                    self.S.add("dve", lambda e, a1_=a1_, c1=c1: e.reciprocal(out=sm[:, 1:2], in_=a1_[:, c1 + 128:c1 + 129]), [ta], ["smr1"])
                    self.tt("dve", sm[:, 2:3], sm[:, 1:2], self.lam[:, 3:4], ALU.mult, ["smr1", "const"], ["smr2"])
                    self.ts("dve", o1, a0_[:, c0:c0 + 128], sm[:, 0:1], None, ALU.mult, None, [ta, "smr0"], ["o1"])
                    self.stt(oo, a1_[:, c1:c1 + 128], sm[:, 2:3], o1, ALU.mult, ALU.add, [ta, "smr2", "o1"], ["oo"])
                    self.tt("pool", sqo, oo, oo, ALU.mult, ["oo"], ["sqo"])
                    self.red(sm[:, 4:5], sqo, ["sqo"], ["ssB"])
                    self.rsqrt_small(sm[:, 5:6], sm[:, 4:5], 1.0 / 128, ["ssB"], ["rsB"])
                    self.stt(ob[oq], oo, sm[:, 5:6], self.subg, ALU.mult, ALU.mult, ["oo", "rsB", "const"], ["ob%d" % oq])
                    r0 = T0 + qb * QB + qs * 128
                    self.dma(self.d_mix[r0:r0 + 128, 512 + h * 128:512 + (h + 1) * 128], ob[oq], ["ob%d" % oq], [])

        hbs = {}
        hbs[0] = prep_qk(0)
        for n in range(len(steps)):
            if n + 1 < len(steps):
                hbs[n + 1] = prep_qk(n + 1)
            if n >= 1:
                do_pv(n - 1, hbs[n - 1])
            do_exp(n)
        do_pv(len(steps) - 1, hbs[len(steps) - 1])

    def phaseC1(self):
        A, ps, psb = self.A, self.ps, self.psb
        wout = A.bf16(8 * D).rearrange("p (k n) -> p k n", k=8)
        stage = [A.f32(1024), A.f32(1024)]
        self.load_weight_bf(wout, self.w_out, 8, D, None, "wout", stage)
        ntl = self.NT // 128

        def c1_stream(sid, tiles, bk):
            ps_ = [ps[b] for b in bk]
            psb_ = [psb[b] for b in bk]
            pt = ["ps%d" % b for b in bk]
            N = lambda nm: "%s_s%d" % (nm, sid)
            mixb = [A.bf16(1024), A.bf16(1024)]
            mixT = A.bf16(1024).rearrange("p (k n) -> p k n", k=8)
            xin = [A.f32(1024), A.f32(1024)]
            x1 = [A.f32(1024), A.f32(1024)]
            hnb = [A.bf16(1024), A.bf16(1024)]
            hnT = [A.bf16(1024).rearrange("p (k n) -> p k n", k=8) for _ in range(2)]
            jk = A.bf16(1024)
            sm = [A.f32(8), A.f32(8)]

            def p1(j):
                i = tiles[j]
                b = j % 2
                rows = slice(i * 128, (i + 1) * 128)
                tm, tx, t1 = N("mixb%d" % b), N("xinC%d" % b), N("x1C%d" % b)
                self.dma(mixb[b], self.d_mix[rows, :], (), [tm], q="act")
                self.dma(xin[b], self.x[rows, :], (), [tx], q="act")
                for kc in range(8):
                    self.tr(psb_[0][:, kc * 128:(kc + 1) * 128], mixb[b][:, kc * 128:(kc + 1) * 128], self.ident, [tm, "const"], [pt[0]])
                self.cp("act", mixT, psb_[0][:, 0:1024].rearrange("p (k n) -> p k n", k=8), [pt[0]], [N("mixT")])
                for nb in range(2):
                    for kc in range(8):
                        self.mm(ps_[1 + nb][:, 0:512], mixT[:, kc, :], wout[:, kc, nb * 512:(nb + 1) * 512], kc == 0, kc == 7, [N("mixT"), "wout"], [pt[1 + nb]])
                    self.tt("dve", x1[b][:, nb * 512:(nb + 1) * 512], ps_[1 + nb][:, 0:512], xin[b][:, nb * 512:(nb + 1) * 512], ALU.add,
                            [pt[1 + nb], tx], [t1])
                self.dma(self.d_x1[rows, :], x1[b], [t1], [("x1d", i)])
                self.act(jk, x1[b], AF.Square, [t1], [N("jkC"), N("ssC%d" % b)], accum_out=sm[b][:, 0:1])
                self.rsqrt_small(sm[b][:, 1:2], sm[b][:, 0:1], 1.0 / D, [N("ssC%d" % b)], [N("rsC%d" % b)])
                self.ts("dve", hnb[b], x1[b], sm[b][:, 1:2], None, ALU.mult, None, [t1, N("rsC%d" % b)], [N("hnb%d" % b)])

            def p2(j):
                i = tiles[j]
                b = j % 2
                rows = slice(i * 128, (i + 1) * 128)
                for kc in range(8):
                    self.tr(psb_[3][:, kc * 128:(kc + 1) * 128], hnb[b][:, kc * 128:(kc + 1) * 128], self.ident, [N("hnb%d" % b), "const"], [pt[3]])
                self.cp("act", hnT[b], psb_[3][:, 0:1024].rearrange("p (k n) -> p k n", k=8), [pt[3]], [N("hnT%d" % b)])
                self.dma(self.d_hnT[:, rows].rearrange("(k p) n -> p k n", p=128), hnT[b], [N("hnT%d" % b)], [("hnTd", i)])

            p1(0)
            yield
            for j in range(len(tiles)):
                if j + 1 < len(tiles):
                    p1(j + 1)
                    yield
                p2(j)
                yield

        gens = [c1_stream(k, list(range(k, ntl, 4)), (0, 1, 2, 3) if k % 2 == 0 else (4, 5, 6, 7)) for k in range(4)]
        for k, g in enumerate(gens):
            for _ in range(3 - k):
                next(g)
        while gens:
            for g in list(gens):
                try:
                    next(g)
                except StopIteration:
                    gens.remove(g)

    def phaseC2(self):
        A, ps = self.A, self.ps
        wup = A.bf16(8 * 2 * DFF).rearrange("p (k n) -> p k n", k=8)
        wdn = A.bf16(22 * D).rearrange("p (k n) -> p k n", k=22)
        mark = A.off
        stage = [A.f32(1408), A.f32(1408)]
        self.load_weight_bf(wup, self.w_up, 8, 2 * DFF, self.g2, "wup", stage)
        self.load_weight_bf(wdn, self.w_down, 22, D, None, "wdn", stage)
        self.S.barrier()
        A.off = mark
        BLM = min(512, max(self.seqs))
        hb = A.bf16(8 * (BLM + 2)).rearrange("p (k n) -> p k n", k=8)
        actT = A.bf16(22 * BLM).rearrange("p (c n) -> p c n", c=22)
        gsb = A.f32(2 * (BLM + 2))
        acc = A.f32(BLM)
        sl = A.f32(BLM)
        x1 = A.f32(1024)
        T0 = 0
        for S_ in self.seqs:
            BL = min(512, S_)
            for blk in range(S_ // BL):
                t0 = T0 + blk * BL
                ntb = BL // 128
                first, last = blk == 0, blk == S_ // BL - 1
                lo = 1 if first else 0
                hi = BL + 1 if last else BL + 2
                c_lo, c_hi = t0 - 1 + lo, t0 - 1 + hi
                rd = [("hnTd", j) for j in range(c_lo // 128, (c_hi - 1) // 128 + 1)]
                self.dma(hb[:, :, lo:hi], self.d_hnT[:, c_lo:c_hi].rearrange("(k p) n -> p k n", p=128), rd, ["hb"])
                if first:
                    self.memset("pool", hb[:, :, 0:1], 0.0, ["hb"])
                if last:
                    self.memset("pool", hb[:, :, BL + 1:BL + 2], 0.0, ["hb"])
                for c in range(22):
                    bg = c % 2
                    bu = 2 + c % 2
                    for kc in range(8):
                        self.mm(ps[bg][:, 0:BL], wup[:, kc, c * 128:(c + 1) * 128], hb[:, kc, 1:BL + 1], kc == 0, kc == 7, ["hb", "wup"], ["ps%d" % bg])
                    for kc in range(8):
                        self.mm(ps[4][:, 0:2], wup[:, kc, c * 128:(c + 1) * 128], hb[:, kc, 0:BL + 2:BL + 1], kc == 0, kc == 7, ["hb", "wup"], ["ps4"])
                    for kc in range(8):
                        self.mm(ps[bu][:, 0:BL], wup[:, kc, DFF + c * 128:DFF + (c + 1) * 128], hb[:, kc, 1:BL + 1], kc == 0, kc == 7,
                                ["hb", "wup"], ["ps%d" % bu])
                    self.cp("act", gsb[:, 1:BL + 1], ps[bg][:, 0:BL], ["ps%d" % bg], ["gsb"])
                    self.cp("dve", gsb[:, 0:BL + 2:BL + 1], ps[4][:, 0:2], ["ps4"], ["gsb"])
                    self.ts("dve", acc[:, 0:BL], gsb[:, 0:BL], self.fw[:, c * 3:c * 3 + 1], self.fb[:, c:c + 1], ALU.mult, ALU.add, ["gsb", "const"], ["accC"])
                    self.stt(acc[:, 0:BL], gsb[:, 1:BL + 1], self.fw[:, c * 3 + 1:c * 3 + 2], acc[:, 0:BL], ALU.mult, ALU.add, ["gsb", "accC", "const"], ["accC"])
                    self.stt(acc[:, 0:BL], gsb[:, 2:BL + 2], self.fw[:, c * 3 + 2:c * 3 + 3], acc[:, 0:BL], ALU.mult, ALU.add, ["gsb", "accC", "const"], ["accC"])
                    self.act(sl[:, 0:BL], acc[:, 0:BL], AF.Silu, ["accC"], ["sl"])
                    self.tt("dve", actT[:, c, 0:BL], sl[:, 0:BL], ps[bu][:, 0:BL], ALU.mult, ["sl", "ps%d" % bu], ["actT"])
                for j in range(ntb):
                    ti = t0 // 128 + j
                    rows = slice(t0 + j * 128, t0 + (j + 1) * 128)
                    self.dma(x1, self.d_x1[rows, :], [("x1d", ti)], ["x1F"])
                    for nb in range(2):
                        bk = 5 + nb
                        for c in range(22):
                            self.mm(ps[bk][:, 0:512], actT[:, c, j * 128:(j + 1) * 128], wdn[:, c, nb * 512:(nb + 1) * 512], c == 0, c == 21,
                                    ["actT", "wdn"], ["ps%d" % bk])
                        self.tt("dve", x1[:, nb * 512:(nb + 1) * 512], ps[bk][:, 0:512], x1[:, nb * 512:(nb + 1) * 512], ALU.add, ["ps%d" % bk, "x1F"], ["x1F"])
                    self.dma(self.d_x1[rows, :], x1, ["x1F"], [("x1d", ti)])
            T0 += S_

    def phaseC3(self):
        A, ps, psb = self.A, self.ps, self.psb
        wg = A.bf16(8 * D).rearrange("p (k n) -> p k n", k=8)
        pp = A.bf16(2 * D).rearrange("p (k n) -> p k n", k=2)
        stage = [A.f32(1024), A.f32(1024)]
        self.load_weight_bf(wg, self.w_ple_gate, 8, D, self.g3, "wg", stage)
        self.load_weight_bf(pp, self.ple_proj, 2, D, None, "pp", stage)
        pleg = A.f32(1024)
        self.dma(pleg, self.ple_norm_g[0:1, :].partition_broadcast(128), (), ["pleg"])
        ntl = self.NT // 128

        def c3_stream(sid, tiles, bk):
            ps_ = [ps[b] for b in bk]
            psb_ = [psb[b] for b in bk]
            pt = ["ps%d" % b for b in bk]
            N = lambda nm: "%s_s%d" % (nm, sid)
            x2 = [A.f32(1024), A.f32(1024)]
            pin = [A.f32(256), A.f32(256)]
            pbf = A.bf16(256)
            peT = A.bf16(256).rearrange("p (k n) -> p k n", k=2)
            ee = [A.f32(1024), A.f32(1024)]
            xnb = [A.bf16(1024), A.bf16(1024)]
            xnT = A.bf16(1024).rearrange("p (k n) -> p k n", k=8)
            th = A.f32(1024)
            yb = [A.f32(1024), A.f32(1024)]
            jk = A.bf16(1024)
            sm = [A.f32(8), A.f32(8)]

            def q1(j):
                i = tiles[j]
                b = j % 2
                rows = slice(i * 128, (i + 1) * 128)
                X = lambda nm: N("%s%d" % (nm, b))
                tx, tp = X("x2_"), X("pin")
                s_ = sm[b]
                self.dma(x2[b], self.d_x1[rows, :], [("x1d", i)], [tx], q="act")
                self.dma(pin[b], self.p[rows, :], (), [tp], q="act")
                self.cp("dve", pbf, pin[b], [tp], [N("pbf")])
                for k in range(2):
                    self.tr(psb_[0][:, k * 128:(k + 1) * 128], pbf[:, k * 128:(k + 1) * 128], self.ident, [N("pbf"), "const"], [pt[0]])
                self.cp("act", peT, psb_[0][:, 0:256].rearrange("p (k n) -> p k n", k=2), [pt[0]], [N("peT")])
                for nb in range(2):
                    for k in range(2):
                        self.mm(ps_[1 + nb][:, 0:512], peT[:, k, :], pp[:, k, nb * 512:(nb + 1) * 512], k == 0, k == 1, [N("peT"), "pp"], [pt[1 + nb]])
                    self.act(jk[:, 0:512], ps_[1 + nb][:, 0:512], AF.Square, [pt[1 + nb]], [N("jk3"), X("sse%d_" % nb)], accum_out=s_[:, nb:nb + 1])
                self.tt("dve", s_[:, 2:3], s_[:, 0:1], s_[:, 1:2], ALU.add, [X("sse0_"), X("sse1_")], [X("sse")])
                self.rsqrt_small(s_[:, 3:4], s_[:, 2:3], 1.0 / D, [X("sse")], [X("rse")])
                for nb in range(2):
                    self.stt(ee[b][:, nb * 512:(nb + 1) * 512], ps_[1 + nb][:, 0:512], s_[:, 3:4], pleg[:, nb * 512:(nb + 1) * 512], ALU.mult, ALU.mult,
                             [pt[1 + nb], X("rse"), "pleg"], [X("ee")])
                self.act(jk, x2[b], AF.Square, [tx], [N("jk3"), X("ssx")], accum_out=s_[:, 4:5])
                self.rsqrt_small(s_[:, 5:6], s_[:, 4:5], 1.0 / D, [X("ssx")], [X("rsx")])
                self.ts("dve", xnb[b], x2[b], s_[:, 5:6], None, ALU.mult, None, [tx, X("rsx")], [X("xnb3")])

            def q2(j):
                i = tiles[j]
                b = j % 2
                rows = slice(i * 128, (i + 1) * 128)
                X = lambda nm: N("%s%d" % (nm, b))
                tx, ty = X("x2_"), X("yb")
                for kc in range(8):
                    self.tr(psb_[3][:, kc * 128:(kc + 1) * 128], xnb[b][:, kc * 128:(kc + 1) * 128], self.ident, [X("xnb3"), "const"], [pt[3]])
                self.cp("act", xnT, psb_[3][:, 0:1024].rearrange("p (k n) -> p k n", k=8), [pt[3]], [N("xnT3")])
                for nb in range(2):
                    for kc in range(8):
                        self.mm(ps_[1 + nb][:, 0:512], xnT[:, kc, :], wg[:, kc, nb * 512:(nb + 1) * 512], kc == 0, kc == 7, [N("xnT3"), "wg"], [pt[1 + nb]])
                    self.act(th[:, nb * 512:(nb + 1) * 512], ps_[1 + nb][:, 0:512], AF.Tanh, [pt[1 + nb]], [N("th")], scale=0.5)
                self.stt(th, th, 1.0, ee[b], ALU.add, ALU.mult, [N("th"), X("ee")], [N("th")])
                self.stt(yb[b], th, 0.5, x2[b], ALU.mult, ALU.add, [N("th"), tx], [ty])
                self.dma(self.y[rows, :], yb[b], [ty], [])

            q1(0)
            yield
            for j in range(len(tiles)):
                if j + 1 < len(tiles):
                    q1(j + 1)
                    yield
                q2(j)
                yield

        gens = [c3_stream(k, list(range(k, ntl, 4)), (0, 1, 2, 3) if k % 2 == 0 else (4, 5, 6, 7)) for k in range(4)]
        for k, g in enumerate(gens):
            for _ in range(3 - k):
                next(g)
        while gens:
            for g in list(gens):
                try:
                    next(g)
                except StopIteration:
                    gens.remove(g)


def rope_table():
    inv = ROPE_THETA ** (-np.arange(0, 16, 2, dtype=np.float32) / 16.0)
    pos = np.arange(4096, dtype=np.float32)
    ang = (pos[:, None] * inv[None, :].astype(np.float32)).astype(np.float32)
    cs = np.concatenate([np.cos(ang), np.sin(ang)], axis=1).astype(np.float32)
    return np.ascontiguousarray(cs.reshape(32, 128, 16).transpose(1, 0, 2).reshape(128, 512))


def host_layout(inp):
    f = lambda a: np.ascontiguousarray(np.asarray(a, dtype=np.float32))
    m = {}
    m["w_in"] = f(inp["w_in"][0]); m["w_out"] = f(inp["w_out"][0]); m["w_up"] = f(inp["w_up"][0])
    m["w_down"] = f(inp["w_down"][0]); m["ple_proj"] = f(inp["ple_proj"][0]); m["w_ple_gate"] = f(inp["w_ple_gate"][0])
    m["ln1_g"] = f(np.asarray(inp["ln1_g"][0]).reshape(8, 128).T)
    m["ln2_g"] = f(np.asarray(inp["ln2_g"][0]).reshape(8, 128).T)
    m["ple_gate_norm_g"] = f(np.asarray(inp["ple_gate_norm_g"][0]).reshape(8, 128).T)
    m["ple_norm_g"] = f(np.asarray(inp["ple_norm_g"][0]).reshape(1, D))
    m["cw_l"] = f(np.asarray(inp["dn_conv_w"][0]).reshape(5, 12, 128).transpose(2, 1, 0).reshape(128, 60))
    m["fw_l"] = f(np.asarray(inp["ffn_conv_w"][0]).reshape(3, 22, 128).transpose(2, 1, 0).reshape(128, 66))
    m["fb_l"] = f(np.asarray(inp["ffn_conv_b"][0]).reshape(22, 128).T)
    m["dn_a_log"] = f(np.asarray(inp["dn_a_log"][0]).reshape(1, 8))
    m["dn_dt_bias"] = f(np.asarray(inp["dn_dt_bias"][0]).reshape(1, 8))
    m["dn_norm_g"] = f(np.asarray(inp["dn_norm_g"][0]).reshape(1, 128))
    m["da_qk_norm_g"] = f(np.asarray(inp["da_qk_norm_g"][0]).reshape(1, 128))
    m["da_lambda"] = f(np.asarray(inp["da_lambda"][0]).reshape(1, 256))
    m["da_subln_g"] = f(np.asarray(inp["da_subln_g"][0]).reshape(1, 128))
    m["rope_cs"] = rope_table()
    return m


def kernel(**inp):
    xp = np.asarray(inp["x_prompt"], dtype=np.float32)
    xs = np.asarray(inp["x_sample"], dtype=np.float32)
    pp = np.asarray(inp["p_prompt"], dtype=np.float32)[0]
    psm = np.asarray(inp["p_sample"], dtype=np.float32)[0]
    nb, SP = xp.shape[0], xp.shape[1]
    SS = xs.shape[1]
    per = xs.shape[0] // nb
    seqs = [SP] + [SS] * per
    common = host_layout(inp)
    in_maps = []
    for c in range(nb):
        m = dict(common)
        m["x"] = np.ascontiguousarray(np.concatenate([xp[c]] + [xs[c * per + j] for j in range(per)], axis=0))
        m["p"] = np.ascontiguousarray(np.concatenate([pp[c]] + [psm[c * per + j] for j in range(per)], axis=0))
        in_maps.append(m)
    nc = Builder(seqs).build()
    res = run_bass_kernel_spmd(nc, in_maps, core_ids=list(range(nb)))
    yp = np.stack([res.results[c]["y"][0:SP] for c in range(nb)], axis=0)
    ys = np.stack([res.results[c]["y"][SP + j * SS:SP + (j + 1) * SS] for c in range(nb) for j in range(per)], axis=0)
    return (yp.astype(np.float32), ys.astype(np.float32))
```

```python
import contextlib
import math

import numpy as np
import concourse.bass as bass
import concourse.mybir as mybir
from concourse.bass_utils import run_bass_kernel_spmd

F32 = mybir.dt.float32
BF16 = mybir.dt.bfloat16
AF = mybir.ActivationFunctionType
ALU = mybir.AluOpType
AX = mybir.AxisListType

D = 1024
NCOL = 3600
DFF = 2816
PLE = 256
EPS = 1e-6
LAM_INIT = 0.8 - 0.6 * math.exp(-0.3 * 0)
ROPE_THETA = 500000.0

ENGS = ("pe", "act", "dve", "pool", "sp")
NDMASEM = 16


class Op:
    __slots__ = ("eng", "fn", "reads", "writes", "dma", "idx", "waits", "sig", "sem", "val")

    def __init__(self, eng, fn, reads, writes, dma):
        self.eng = eng
        self.fn = fn
        self.reads = reads
        self.writes = writes
        self.dma = dma
        self.waits = []
        self.sig = False
        self.sem = None
        self.val = 0


class Sched:
    def __init__(self, nc):
        self.nc = nc
        self.ops = []
        self.last_w = {}
        self.readers = {}
        self.last_eng = {}
        self.recent_dma = {e: [] for e in ENGS}

    def add(self, eng, fn, reads=(), writes=(), dma=False):
        op = Op(eng, fn, tuple(reads), tuple(writes), dma)
        op.idx = len(self.ops)
        deps = {}
        for t in op.reads:
            w = self.last_w.get(t)
            if w is not None:
                deps[w.idx] = w
        for t in op.writes:
            w = self.last_w.get(t)
            if w is not None:
                deps[w.idx] = w
            for r in self.readers.get(t, ()):
                deps[r.idx] = r
        for d in deps.values():
            if self._needs_sync(d, op):
                op.waits.append(d)
                d.sig = True
        for t in op.writes:
            self.last_w[t] = op
            self.readers[t] = []
        for t in op.reads:
            if t not in op.writes:
                self.readers.setdefault(t, []).append(op)
        self.ops.append(op)
        if dma:
            lst = self.recent_dma[eng]
            lst.append(op)
            if len(lst) > NDMASEM:
                lst.pop(0)
        else:
            self.last_eng[eng] = op
        return op

    def barrier(self):
        prods = list(self.last_eng.values())
        for e in ENGS:
            prods.extend(self.recent_dma[e])
        new = []
        for e in ENGS:
            op = Op(e, None, (), (), False)
            op.idx = len(self.ops)
            for d in prods:
                op.waits.append(d)
                d.sig = True
            self.ops.append(op)
            new.append(op)
        for op in new:
            self.last_eng[op.eng] = op
        self.last_w = {}
        self.readers = {}

    def _needs_sync(self, prod, cons):
        if prod.dma or cons.dma:
            return True
        if prod.eng != cons.eng:
            return True
        if prod.eng == "pe":
            return False
        for t in cons.reads:
            if t in prod.writes:
                return True
        for t in cons.writes:
            if t in prod.writes:
                return True
        return False

    def emit(self):
        nc = self.nc
        with contextlib.ExitStack() as es:
            csem = {e: es.enter_context(nc.semaphore("c_" + e)) for e in ENGS}
            dsem = {e: [es.enter_context(nc.semaphore("d_%s%d" % (e, i))) for i in range(NDMASEM)] for e in ENGS}
            ccount = {e: 0 for e in ENGS}
            dcount = {e: [0] * NDMASEM for e in ENGS}
            dn = {e: 0 for e in ENGS}
            prevdma = {}
            for op in self.ops:
                if op.fn is None:
                    op.sig = False
                    continue
                if op.dma:
                    k = dn[op.eng] % NDMASEM
                    dn[op.eng] += 1
                    dcount[op.eng][k] += 16
                    op.sem = dsem[op.eng][k]
                    op.val = dcount[op.eng][k]
                    op.sig = True
                    prev = prevdma.get((op.eng, k))
                    if prev is not None:
                        op.waits.append(prev)
                    prevdma[(op.eng, k)] = op
                elif op.sig:
                    ccount[op.eng] += 1
                    op.sem = csem[op.eng]
                    op.val = ccount[op.eng]
            per = {e: [o for o in self.ops if o.eng == e] for e in ENGS}
            finals = {}
            for op in self.ops:
                if op.dma:
                    key = id(op.sem)
                    if key not in finals or finals[key][1] < op.val:
                        finals[key] = (op.sem, op.val)

            def run(e, eng):
                waited = {}
                for op in per[e]:
                    for d in op.waits:
                        if d.sem is None:
                            continue
                        key = id(d.sem)
                        if waited.get(key, 0) >= d.val:
                            continue
                        eng.wait_ge(d.sem, d.val)
                        waited[key] = d.val
                    if op.fn is None:
                        continue
                    ins = op.fn(eng)
                    if op.sig:
                        ins.then_inc(op.sem, 16 if op.dma else 1)
                if e == "sp":
                    for sem, val in finals.values():
                        eng.wait_ge(sem, val)

            with nc.Block() as block:
                @block.tensor
                def _(eng):
                    run("pe", eng)

                @block.scalar
                def _(eng):
                    run("act", eng)

                @block.vector
                def _(eng):
                    run("dve", eng)

                @block.gpsimd
                def _(eng):
                    run("pool", eng)

                @block.sync
                def _(eng):
                    run("sp", eng)


class Arena:
    def __init__(self, ap, words):
        self.ap = ap
        self.words = words
        self.off = 0

    def f32(self, n):
        n2 = (n + 7) // 8 * 8
        assert self.off + n2 <= self.words, ("SBUF arena overflow", self.off, n2, self.words)
        a = self.ap[:, self.off:self.off + n]
        self.off += n2
        return a

    def bf16(self, n):
        w = (n + 1) // 2
        w2 = (w + 7) // 8 * 8
        assert self.off + w2 <= self.words, ("SBUF arena overflow", self.off, w2, self.words)
        a = self.ap[:, self.off:self.off + w].bitcast(BF16)[:, 0:n]
        self.off += w2
        return a


class Builder:
    def __init__(self, seqs, debug=False):
        self.seqs = list(seqs)
        self.NT = sum(self.seqs)
        self.debug = debug
        self.nc = bass.Bass("TRN2", target_bir_lowering=False)
        self.S = Sched(self.nc)
        self.uid = 0

    def u(self, base):
        self.uid += 1
        return "%s#%d" % (base, self.uid)

    def act(self, out, in_, func, r, w, **kw):
        return self.S.add("act", lambda e: e.activation(out=out, in_=in_, func=func, **kw), r, w)

    def tt(self, eng, out, in0, in1, op, r, w):
        return self.S.add(eng, lambda e: e.tensor_tensor(out=out, in0=in0, in1=in1, op=op), r, w)

    def ts(self, eng, out, in0, s1, s2, op0, op1, r, w, **kw):
        if op1 is None:
            return self.S.add(eng, lambda e: e.tensor_scalar(out=out, in0=in0, scalar1=s1, scalar2=None, op0=op0, **kw), r, w)
        return self.S.add(eng, lambda e: e.tensor_scalar(out=out, in0=in0, scalar1=s1, scalar2=s2, op0=op0, op1=op1, **kw), r, w)

    def stt(self, out, in0, scalar, in1, op0, op1, r, w):
        return self.S.add("dve", lambda e: e.scalar_tensor_tensor(out=out, in0=in0, scalar=scalar, in1=in1, op0=op0, op1=op1), r, w)

    def cp(self, eng, out, in_, r, w):
        if eng == "act":
            return self.S.add("act", lambda e: e.copy(out=out, in_=in_), r, w)
        return self.S.add(eng, lambda e: e.tensor_copy(out=out, in_=in_), r, w)

    def red(self, out, in_, r, w, op=ALU.add):
        return self.S.add("dve", lambda e: e.tensor_reduce(out=out, in_=in_, axis=AX.X, op=op), r, w)

    def mm(self, out, lhsT, rhs, start, stop, r, w):
        return self.S.add("pe", lambda e: e.matmul(out, lhsT=lhsT, rhs=rhs, start=start, stop=stop, skip_group_check=True), r, w)

    def tr(self, out, in_, ident, r, w):
        return self.S.add("pe", lambda e: e.transpose(out=out, in_=in_, identity=ident), r, w)

    def dma(self, out, in_, r, w, q="sp"):
        return self.S.add(q, lambda e: e.dma_start(out=out, in_=in_), r, w, dma=True)

    def memset(self, eng, ap, val, w):
        return self.S.add(eng, lambda e: e.memset(ap, val), (), w)

    def rsqrt_small(self, out, in_, scale, r, w):
        n = in_.shape[-1]
        tmp = self.u("rs_tmp")
        self.ts("dve", out, in_, scale, EPS, ALU.mult, ALU.add, r, [tmp] + list(w))
        self.tt("pool", out, out, self.mhalf[:, 0:n], ALU.pow, [tmp] + list(w) + ["const"], w)

    def build(self):
        nc = self.nc
        NT = self.NT
        dbg = "ExternalOutput" if self.debug else "Internal"

        def din(name, shape, dt=F32):
            return nc.dram_tensor(name, list(shape), dt, kind="ExternalInput").ap()

        def dscr(name, shape, dt):
            return nc.dram_tensor(name, list(shape), dt, kind=dbg).ap()

        self.x = din("x", [NT, D])
        self.p = din("p", [NT, PLE])
        self.y = nc.dram_tensor("y", [NT, D], F32, kind="ExternalOutput").ap()
        self.w_in = din("w_in", [D, NCOL])
        self.w_out = din("w_out", [D, D])
        self.w_up = din("w_up", [D, 2 * DFF])
        self.w_down = din("w_down", [DFF, D])
        self.ple_proj = din("ple_proj", [PLE, D])
        self.w_ple_gate = din("w_ple_gate", [D, D])
        self.ln1_g = din("ln1_g", [128, 8])
        self.ln2_g = din("ln2_g", [128, 8])
        self.ple_norm_g = din("ple_norm_g", [1, D])
        self.ple_gate_norm_g = din("ple_gate_norm_g", [128, 8])
        self.cw_l = din("cw_l", [128, 60])
        self.dn_a_log = din("dn_a_log", [1, 8])
        self.dn_dt_bias = din("dn_dt_bias", [1, 8])
        self.dn_norm_g = din("dn_norm_g", [1, 128])
        self.da_qk_norm_g = din("da_qk_norm_g", [1, 128])
        self.da_lambda = din("da_lambda", [1, 256])
        self.da_subln_g = din("da_subln_g", [1, 128])
        self.fw_l = din("fw_l", [128, 66])
        self.fb_l = din("fb_l", [128, 22])
        self.rope_cs = din("rope_cs", [128, 32 * 16])

        self.d_kn = dscr("d_kn", [NT, 512], BF16)
        self.d_v = dscr("d_v", [NT, 512], BF16)
        self.d_kT = dscr("d_kT", [512, NT], BF16)
        self.d_qT = dscr("d_qT", [512, NT], BF16)
        self.d_dz = dscr("d_dz", [NT, 512], BF16)
        self.d_qaT = dscr("d_qaT", [512, NT], BF16)
        self.d_kaT = dscr("d_kaT", [512, NT], BF16)
        self.d_va = dscr("d_va", [NT, 512], BF16)
        self.d_ob = dscr("d_ob", [NT, 512], F32)
        self.d_mix = dscr("d_mix", [NT, D], BF16)
        self.d_x1 = dscr("d_x1", [NT, D], F32)
        self.d_hnT = dscr("d_hnT", [D, NT], BF16)
        self.d_gate = dscr("d_gate", [NT, 16], F32)

        with contextlib.ExitStack() as es:
            WORDS = 51 * 1024
            arena_t = es.enter_context(nc.sbuf_tensor("arena", [128, WORDS], F32))
            self.A = Arena(arena_t[:], WORDS)
            self.psall = es.enter_context(nc.psum_tensor("psall", [128, 4096], F32))[:]
            self.ps = [self.psall[:, i * 512:(i + 1) * 512] for i in range(8)]
            self.psb = [p.bitcast(BF16) for p in self.ps]
            self.setup_consts()
            base = self.A.off
            toff = 0
            gtile = 0
            self.phaseA_weights()
            for S_ in self.seqs:
                self.phaseA_seq(toff, S_, gtile)
                toff += S_
                gtile += S_ // 128
            self.gates_finish()
            self.S.barrier()
            self.A.off = base
            self.phaseB1()
            self.S.barrier()
            self.A.off = base
            self.phaseB2()
            self.S.barrier()
            self.A.off = base
            self.phaseC1()
            self.S.barrier()
            self.A.off = base
            self.phaseC2()
            self.S.barrier()
            self.A.off = base
            self.phaseC3()
            self.S.emit()
        return nc

    def setup_consts(self):
        A = self.A
        ntl = self.NT // 128
        self.identf = A.f32(128)
        self.ident = A.bf16(128)
        self.mhalf = A.f32(16)
        self.UI = A.f32(128)
        self.US = A.f32(128)
        self.LI = A.f32(128)
        self.LS = A.f32(128)
        self.BD = A.f32(128)
        self.onesA = A.f32(128)
        self.onesB = A.f32(128)
        self.rope = A.f32(32 * 16)
        self.dtb = A.f32(8)
        self.nega = A.f32(8)
        self.gqk = A.f32(128)
        self.dng = A.f32(128)
        self.subg = A.f32(128)
        self.lamt = A.f32(256)
        self.lam = A.f32(4)
        self.cw = A.f32(12 * 5)
        self.fw = A.f32(22 * 3)
        self.fb = A.f32(22)
        self.g1 = A.f32(8)
        self.g2 = A.f32(8)
        self.g3 = A.f32(8)
        self.gx = A.f32(ntl * 8)
        self.gb = A.f32(ntl * 8)
        C = ["const"]
        pool = "pool"
        ms = self.memset
        S = self.S
        ms(pool, self.identf, 0.0, C)
        S.add(pool, lambda e: e.affine_select(out=self.identf, in_=self.identf, pattern=[[-1, 128]], compare_op=ALU.not_equal,
                                              fill=1.0, base=0, channel_multiplier=1), C, C)
        self.cp("dve", self.ident, self.identf, C, C)
        ms(pool, self.mhalf, -0.5, C)
        ms(pool, self.BD, 0.0, C)
        ms(pool, self.BD[0:64, 0:64], 1.0, C)
        ms(pool, self.BD[64:128, 64:128], 1.0, C)
        ms(pool, self.onesA, 0.0, C)
        ms(pool, self.onesA[0:64, :], 1.0, C)
        ms(pool, self.onesB, 0.0, C)
        ms(pool, self.onesB[64:128, :], 1.0, C)
        for m, pat, cm, base_, op in ((self.UI, 1, -1, 0, ALU.is_ge), (self.US, 1, -1, 0, ALU.is_gt),
                                      (self.LI, -1, 1, 0, ALU.is_ge), (self.LS, -1, 1, 0, ALU.is_gt)):
            S.add(pool, lambda e, m=m, pat=pat, cm=cm, base_=base_, op=op: e.affine_select(
                out=m, in_=self.BD, pattern=[[pat, 128]], compare_op=op, fill=0.0, base=base_, channel_multiplier=cm), C, C)
        d = self.dma
        d(self.rope, self.rope_cs[:, :], (), C)
        d(self.dtb, self.dn_dt_bias[0:1, :].partition_broadcast(128), (), C)
        d(self.nega, self.dn_a_log[0:1, :].partition_broadcast(128), (), C)
        d(self.gqk, self.da_qk_norm_g[0:1, :].partition_broadcast(128), (), C)
        d(self.dng, self.dn_norm_g[0:1, :].partition_broadcast(128), (), C)
        d(self.subg, self.da_subln_g[0:1, :].partition_broadcast(128), (), C)
        d(self.lamt, self.da_lambda[0:1, :].partition_broadcast(128), (), C)
        d(self.cw, self.cw_l[:, :], (), C)
        d(self.fw, self.fw_l[:, :], (), C)
        d(self.fb, self.fb_l[:, :], (), C)
        d(self.g1, self.ln1_g[:, :], (), C)
        d(self.g2, self.ln2_g[:, :], (), C)
        d(self.g3, self.ple_gate_norm_g[:, :], (), C)
        self.act(self.nega, self.nega, AF.Exp, C, C)
        self.ts("dve", self.nega, self.nega, -1.0, None, ALU.mult, None, C, C)
        self.ts("dve", self.gqk[:, 0:64], self.gqk[:, 0:64], 0.125, None, ALU.mult, None, C, C)
        self.ts("dve", self.subg, self.subg, 1.0 - LAM_INIT, None, ALU.mult, None, C, C)
        lt = self.lamt.rearrange("p (a d) -> p a d", d=64)
        self.tt("dve", lt[:, 0, :], lt[:, 0, :], lt[:, 1, :], ALU.mult, C, C)
        self.tt("dve", lt[:, 2, :], lt[:, 2, :], lt[:, 3, :], ALU.mult, C, C)
        self.red(self.lam[:, 0:1], lt[:, 0, :], C, C)
        self.red(self.lam[:, 1:2], lt[:, 2, :], C, C)
        self.act(self.lam[:, 0:2], self.lam[:, 0:2], AF.Exp, C, C)
        self.tt("dve", self.lam[:, 2:3], self.lam[:, 0:1], self.lam[:, 1:2], ALU.subtract, C, C)
        self.ts("dve", self.lam[:, 3:4], self.lam[:, 2:3], LAM_INIT, -1.0, ALU.add, ALU.mult, C, C)

    def load_weight_bf(self, dst, src, kc_n, ncols, gain, tagw, stage):
        CH = stage[0].shape[-1]
        i = 0
        for kc in range(kc_n):
            for c0 in range(0, ncols, CH):
                c1 = min(ncols, c0 + CH)
                st = stage[i % 2]
                tk = "wstage%d" % (i % 2)
                i += 1
                self.dma(st[:, 0:c1 - c0], src[kc * 128:(kc + 1) * 128, c0:c1], (), [tk])
                eng = "dve" if (i % 2) else "act"
                if gain is None:
                    self.cp(eng, dst[:, kc, c0:c1], st[:, 0:c1 - c0], [tk], [tagw])
                elif eng == "act":
                    self.act(dst[:, kc, c0:c1], st[:, 0:c1 - c0], AF.Identity, [tk, "const"], [tagw], scale=gain[:, kc:kc + 1])
                else:
                    self.ts(eng, dst[:, kc, c0:c1], st[:, 0:c1 - c0], gain[:, kc:kc + 1], None, ALU.mult, None, [tk, "const"], [tagw])

    def phaseA_weights(self):
        A = self.A
        self.win = A.bf16(8 * NCOL).rearrange("p (k n) -> p k n", k=8)
        mark = A.off
        self.stageA = [A.f32(1800), A.f32(1800)]
        self.load_weight_bf(self.win, self.w_in, 8, NCOL, self.g1, "win", self.stageA)
        self.S.barrier()
        A.off = mark
        Smax = max(self.seqs)
        self.XW = Smax + 4
        self.xnT = A.bf16(8 * self.XW).rearrange("p (k n) -> p k n", k=8)
        self.xin = [A.f32(1024), A.f32(1024)]
        self.xnb = [A.bf16(1024), A.bf16(1024)]
        self.junk = A.bf16(1024)
        self.sm = A.f32(80)
        self.tmA = A.f32(1024)
        self.tmB = A.f32(512)
        self.qkb = A.bf16(1024)
        self.trs = A.bf16(1024)
        self.dzb = A.bf16(512)
        self.avb = A.bf16(512)
        self.presb = [A.f32(516), A.f32(516)]
        self.accb = [A.f32(512), A.f32(512)]
        self.fmb = [A.bf16(512), A.bf16(512)]
        self.sq2 = A.f32(1024)
        self.trs2 = [A.bf16(512), A.bf16(512)]
        self.tok = A.bf16(4 * 1536).rearrange("p (t c) -> p t c", t=4)
        self.nrmb = [A.bf16(1024), A.bf16(1024)]
        self.rq8 = [A.f32(8), A.f32(8)]
        self.sq = A.f32(1024)

    def phaseA_seq(self, T0, S_, gt0):
        ps, psb = self.ps, self.psb
        nt = S_ // 128
        xnT = self.xnT
        XT = lambda j: ("xnT", j)
        prog = {"a0": 0}
        self.memset("pool", xnT[:, :, 0:2], 0.0, ["xnTpadL"])
        self.memset("pool", xnT[:, :, 2 + S_:2 + S_ + 2], 0.0, ["xnTpadR"] + [XT(j) for j in range(nt, nt + 1)])

        def a0_a(i):
            b = i % 2
            xin, xnb = self.xin[b], self.xnb[b]
            tx, tn = "xin%d" % b, "xnb%d" % b
            self.dma(xin, self.x[T0 + i * 128:T0 + (i + 1) * 128, :], (), [tx], q="act")
            ss = self.sm[:, b:b + 1]
            tss = "ssA%d" % b
            self.act(self.junk, xin, AF.Square, [tx], ["junk", tss], accum_out=ss)
            rs = self.sm[:, 2 + b:3 + b]
            trs = "rsA%d" % b
            self.rsqrt_small(rs, ss, 1.0 / D, [tss], [trs])
            self.ts("dve", xnb, xin, rs, None, ALU.mult, None, [tx, trs], [tn])

        def a0_b(i):
            b = i % 2
            xnb, tn = self.xnb[b], "xnb%d" % b
            for kc in range(8):
                self.tr(psb[0][:, kc * 128:(kc + 1) * 128], xnb[:, kc * 128:(kc + 1) * 128], self.ident, [tn, "const"], ["ps0"])
            self.cp("act", xnT[:, :, 2 + i * 128:2 + (i + 1) * 128], psb[0][:, 0:1024].rearrange("p (k n) -> p k n", k=8),
                    ["ps0"], [XT(i)])
            prog["a0"] = i + 1

        def gen_a0():
            a0_a(0)
            yield
            for i in range(nt):
                if i + 1 < nt:
                    a0_a(i + 1)
                a0_b(i)
                yield

        def tm_s1(i):
            gt = gt0 + i
            cols = slice(2 + i * 128, 2 + (i + 1) * 128)
            rows = slice(T0 + i * 128, T0 + (i + 1) * 128)

            def proj(c0, n, out, tok):
                for kc in range(8):
                    self.mm(out, xnT[:, kc, cols], self.win[:, kc, c0:c0 + n], kc == 0, kc == 7, [XT(i), "win"], [tok])
            proj(2064, 512, ps[2][:, 0:512], "ps2")
            proj(2576, 512, ps[3][:, 0:512], "ps3")
            proj(2048, 16, ps[5][:, 0:16], "ps5")
            self.tt("dve", self.gx[:, gt * 8:(gt + 1) * 8], ps[5][:, 0:8], self.dtb, ALU.add, ["ps5", "const"], ["gx"])
            self.cp("dve", self.gb[:, gt * 8:(gt + 1) * 8], ps[5][:, 8:16], ["ps5"], ["gb"])
            proj(1536, 512, ps[1][:, 0:512], "ps1")
            self.cp("act", self.dzb, ps[1][:, 0:512], ["ps1"], ["dzb"])
            self.dma(self.d_dz[rows, :], self.dzb, ["dzb"], [])
            proj(3088, 512, ps[1][:, 0:512], "ps1")
            self.cp("act", self.avb, ps[1][:, 0:512], ["ps1"], ["avb"])
            self.dma(self.d_va[rows, :], self.avb, ["avb"], [])

        def tm_s2(i):
            tmA = self.tmA
            self.act(self.sq[:, 0:512], ps[2][:, 0:512], AF.Square, ["ps2"], ["sq"])
            self.act(self.sq[:, 512:1024], ps[3][:, 0:512], AF.Square, ["ps3"], ["sq"])
            ssq = self.sm[:, 8:24]
            self.red(ssq, self.sq.rearrange("p (g d) -> p g d", d=64), ["sq"], ["ssq"])
            rq = self.sm[:, 24:40]
            self.rsqrt_small(rq, ssq, 1.0 / 64, ["ssq"], ["rq"])
            for half, bk in ((0, 2), (1, 3)):
                o3 = tmA[:, half * 512:(half + 1) * 512].rearrange("p (g d) -> p g d", d=64)
                self.tt("dve", o3, ps[bk][:, 0:512].rearrange("p (g d) -> p g d", d=64),
                        rq[:, half * 8:(half + 1) * 8].unsqueeze(2).to_broadcast([128, 8, 64]), ALU.mult, ["ps%d" % bk, "rq"], ["tmA%d" % half])
                self.tt("pool", o3, o3, self.gqk[:, half * 64:(half + 1) * 64].unsqueeze(1).to_broadcast([128, 8, 64]), ALU.mult,
                        ["tmA%d" % half, "const"], ["tmA%d" % half])

        def tm_s3(i):
            rows = slice(T0 + i * 128, T0 + (i + 1) * 128)
            tmA = self.tmA
            a3 = tmA.rearrange("p (g d) -> p g d", d=64)
            cosb = self.rope[:, i * 16:i * 16 + 8].unsqueeze(1).to_broadcast([128, 16, 8])
            sinb = self.rope[:, i * 16 + 8:i * 16 + 16].unsqueeze(1).to_broadcast([128, 16, 8])
            tB = self.tmB
            t1 = tB[:, 0:128].rearrange("p (g d) -> p g d", d=8)
            t2 = tB[:, 128:256].rearrange("p (g d) -> p g d", d=8)
            t3 = tB[:, 256:384].rearrange("p (g d) -> p g d", d=8)
            t4 = tB[:, 384:512].rearrange("p (g d) -> p g d", d=8)
            x1, x2 = a3[:, :, 0:8], a3[:, :, 8:16]
            tA = ["tmA0", "tmA1"]
            self.tt("dve", t1, x1, cosb, ALU.mult, tA + ["const"], ["tB1"])
            self.tt("dve", t2, x2, sinb, ALU.mult, tA + ["const"], ["tB2"])
            self.tt("dve", t3, x2, cosb, ALU.mult, tA + ["const"], ["tB3"])
            self.tt("dve", t4, x1, sinb, ALU.mult, tA + ["const"], ["tB4"])
            self.cp("act", self.qkb, tmA, tA, ["qkb"])
            q3 = self.qkb.rearrange("p (g d) -> p g d", d=64)
            self.tt("dve", q3[:, :, 0:8], t1, t2, ALU.subtract, ["tB1", "tB2", "qkb"], ["qkb"])
            self.tt("dve", q3[:, :, 8:16], t3, t4, ALU.add, ["tB3", "tB4", "qkb"], ["qkb"])
            for j in range(8):
                self.tr(psb[4][:, j * 128:(j + 1) * 128], self.qkb[:, j * 128:(j + 1) * 128], self.ident, ["qkb", "const"], ["ps4"])
            self.cp("act", self.trs, psb[4][:, 0:1024], ["ps4"], ["trs"])
            t3d = self.trs.rearrange("p (j n) -> p j n", j=8)
            self.dma(self.d_qaT[:, rows].rearrange("(h p) n -> p h n", p=128), t3d[:, 0:4, :], ["trs"], [])
            self.dma(self.d_kaT[:, rows].rearrange("(h p) n -> p h n", p=128), t3d[:, 4:8, :], ["trs"], [])

        def gen_tm():
            while prog["a0"] < 1:
                yield
            tm_s1(0)
            yield
            for i in range(nt):
                tm_s2(i)
                yield
                if i + 1 < nt:
                    while prog["a0"] < i + 2:
                        yield
                    tm_s1(i + 1)
                    yield
                tm_s3(i)
                yield

        def gen_fm():
            BL = min(512, S_)
            ntb = BL // 128
            for blk in range(S_ // BL):
                t0 = blk * BL
                need = min(nt, blk * ntb + ntb + 1)
                while prog["a0"] < need:
                    yield
                rd = [XT(j) for j in range(max(0, blk * ntb - 1), min(nt, blk * ntb + ntb + 1))] + ["xnTpadL", "xnTpadR", "win"]

                def fm_mm(c):
                    bk = 6 + c % 2
                    for kc in range(8):
                        self.mm(ps[bk][:, 0:BL], self.win[:, kc, c * 128:(c + 1) * 128], xnT[:, kc, 2 + t0:2 + t0 + BL], kc == 0, kc == 7,
                                rd, ["ps%d" % bk])
                    for kc in range(8):
                        self.mm(ps[5][:, 16:18], self.win[:, kc, c * 128:(c + 1) * 128], xnT[:, kc, t0:t0 + 2], kc == 0, kc == 7, rd, ["ps5"])
                    for kc in range(8):
                        self.mm(ps[5][:, 18:20], self.win[:, kc, c * 128:(c + 1) * 128], xnT[:, kc, t0 + BL + 2:t0 + BL + 4], kc == 0, kc == 7,
                                rd, ["ps5"])
                    pre = self.presb[c % 2]
                    tpre = "pre%d" % (c % 2)
                    self.cp("act", pre[:, 2:2 + BL], ps[bk][:, 0:BL], ["ps%d" % bk], [tpre])
                    self.cp("dve", pre[:, 0:2], ps[5][:, 16:18], ["ps5"], [tpre])
                    self.cp("dve", pre[:, BL + 2:BL + 4], ps[5][:, 18:20], ["ps5"], [tpre])

                fm_mm(0)
                yield
                for c in range(12):
                    if c + 1 < 12:
                        fm_mm(c + 1)
                    pre, acc, fm = self.presb[c % 2], self.accb[c % 2], self.fmb[c % 2]
                    tpre, tacc, tfm = "pre%d" % (c % 2), "acc%d" % (c % 2), "fm%d" % (c % 2)
                    a_ = acc[:, 0:BL]
                    self.ts("dve", a_, pre[:, 0:BL], self.cw[:, c * 5:c * 5 + 1], None, ALU.mult, None, [tpre, "const"], [tacc])
                    for j in range(1, 5):
                        self.stt(a_, pre[:, j:j + BL], self.cw[:, c * 5 + j:c * 5 + j + 1], a_, ALU.mult, ALU.add, [tpre, tacc, "const"], [tacc])
                    self.act(fm[:, 0:BL], a_, AF.Silu, [tacc], [tfm])
                    yield
                    for t in range(ntb):
                        self.tr(psb[4][:, t * 128:(t + 1) * 128], fm[:, t * 128:(t + 1) * 128], self.ident, [tfm, "const"], ["ps4"])
                    self.cp("act", self.tok[:, 0:ntb, c * 128:(c + 1) * 128],
                            psb[4][:, 0:ntb * 128].rearrange("p (t n) -> p t n", t=ntb), ["ps4"], ["tok"])
                    yield

                def l2_a(t):
                    tk = self.tok[:, t, :]
                    nb_ = t % 2
                    nrm = self.nrmb[nb_]
                    self.act(self.sq2, tk[:, 0:1024], AF.Square, ["tok"], ["sq2"])
                    ssq = self.sm[:, 40 + nb_ * 8:48 + nb_ * 8]
                    self.red(ssq, self.sq2.rearrange("p (g d) -> p g d", d=128), ["sq2"], ["ssq2%d" % nb_])
                    rq = self.sm[:, 56 + nb_ * 4:60 + nb_ * 4]
                    rk = self.sm[:, 24 + 40:28 + 40] if False else None
                    rq8 = self.rq8[nb_]
                    self.ts("dve", rq8, ssq, EPS, None, ALU.add, None, ["ssq2%d" % nb_], ["rq2a%d" % nb_])
                    self.tt("pool", rq8, rq8, self.mhalf[:, 0:8], ALU.pow, ["rq2a%d" % nb_, "const"], ["rq2%d" % nb_])
                    self.ts("dve", rq8[:, 0:4], rq8[:, 0:4], 128.0 ** -0.5, None, ALU.mult, None, ["rq2%d" % nb_], ["rq2%d" % nb_])
                    self.tt("dve", nrm.rearrange("p (g d) -> p g d", d=128), tk[:, 0:1024].rearrange("p (g d) -> p g d", d=128),
                            rq8.unsqueeze(2).to_broadcast([128, 8, 128]), ALU.mult, ["tok", "rq2%d" % nb_], ["nrm%d" % nb_])

                def l2_b(t):
                    rows = slice(T0 + t0 + t * 128, T0 + t0 + (t + 1) * 128)
                    tk = self.tok[:, t, :]
                    nb_ = t % 2
                    nrm = self.nrmb[nb_]
                    tn_ = "nrm%d" % nb_
                    self.dma(self.d_kn[rows, :], nrm[:, 512:1024], [tn_], [])
                    self.dma(self.d_v[rows, :], tk[:, 1024:1536], ["tok"], [])
                    for half, dst in ((0, self.d_qT), (1, self.d_kT)):
                        for j in range(4):
                            self.tr(psb[4][:, j * 128:(j + 1) * 128], nrm[:, half * 512 + j * 128:half * 512 + (j + 1) * 128],
                                    self.ident, [tn_, "const"], ["ps4"])
                        tr2 = self.trs2[half]
                        self.cp("act", tr2, psb[4][:, 0:512], ["ps4"], ["trs2%d" % half])
                        self.dma(dst[:, rows].rearrange("(h p) n -> p h n", p=128), tr2.rearrange("p (j n) -> p j n", j=4), ["trs2%d" % half], [])

                l2_a(0)
                yield
                for t in range(ntb):
                    if t + 1 < ntb:
                        l2_a(t + 1)
                    l2_b(t)
                    yield

        g0, g1, g2 = gen_a0(), gen_tm(), gen_fm()
        alive = {0: g0, 1: g1, 2: g2}
        plan = (0, 2, 2, 1, 2, 2, 1, 2, 2, 1, 2, 2)
        while alive:
            for k in plan:
                g = alive.get(k)
                if g is None:
                    continue
                try:
                    next(g)
                except StopIteration:
                    del alive[k]

    def gates_finish(self):
        ntl = self.NT // 128
        gx3 = self.gx.rearrange("p (t c) -> p t c", c=8)
        self.act(self.gx, self.gx, AF.Exp, ["gx"], ["gx"])
        self.act(self.gx, self.gx, AF.Ln, ["gx"], ["gx"], bias=1.0)
        self.tt("dve", gx3, gx3, self.nega.unsqueeze(1).to_broadcast([128, ntl, 8]), ALU.mult, ["gx", "const"], ["gx"])
        self.act(self.gb, self.gb, AF.Tanh, ["gb"], ["gb"], scale=0.5)
        self.ts("dve", self.gb, self.gb, 0.5, 0.5, ALU.mult, ALU.add, ["gb"], ["gb"])
        if self.debug:
            for t in range(ntl):
                self.dma(self.d_gate[t * 128:(t + 1) * 128, 0:8], self.gx[:, t * 8:(t + 1) * 8], ["gx"], [])
                self.dma(self.d_gate[t * 128:(t + 1) * 128, 8:16], self.gb[:, t * 8:(t + 1) * 8], ["gb"], [])

    def phaseB1(self):
        if self.debug == "nodn":
            dn_in = self.nc.dram_tensor("dn_in", [self.NT, 512], BF16, kind="ExternalInput").ap()
            for i in range(self.NT // 128):
                self.dma(self.d_mix[i * 128:(i + 1) * 128, 0:512], dn_in[i * 128:(i + 1) * 128, :], (), [])
            return
        A = self.A
        self.d_of = self.nc.dram_tensor("d_of", [self.NT, 512], F32, kind="Internal").ap()
        v3 = lambda a: a.rearrange("p (h d) -> p h d", h=4)
        identbc = self.identf.unsqueeze(1).to_broadcast([128, 4, 128])

        def mkbufs(sid):
            B = {}
            for nm in ("kn0", "kn1", "vt0", "vt1", "kT0", "kT1", "qT0", "qT1", "kb", "kbT", "kbg", "vb", "kdec", "P0", "P1", "Q0", "Q1",
                       "Rb", "qkT", "wT", "vnew", "Sb"):
                B[nm] = A.bf16(512)
            for nm in ("Rg", "E", "Es", "Xf", "Rf", "usb", "tmp", "ot0", "ot1", "Sf"):
                B[nm] = A.f32(512)
            B["gsm"] = A.f32(24)
            B["e20"] = A.f32(24)
            B["bg"] = A.f32(8)
            return B

        def dn_pass(sid, T0, S_, gt0, dr, B, bk):
            ps = [self.ps[b] for b in bk]
            psb = [self.psb[b] for b in bk]
            pt = ["ps%d" % b for b in bk]
            T = lambda nm: "%s_%d" % (nm, sid)
            nt = S_ // 128
            if dr == 0:
                CUM, ARGL, RMASK, MINCL, MSTR = self.UI, self.LS, self.UI, self.UI, self.US
                tiles = range(nt)
                chunks = (0, 64)
            else:
                CUM, ARGL, RMASK, MINCL, MSTR = self.LI, self.US, self.LI, self.LI, self.LS
                tiles = range(nt - 1, -1, -1)
                chunks = (64, 0)
            Sf, Sb = B["Sf"], B["Sb"]
            self.memset("pool", Sf, 0.0, [T("Sf")])
            self.memset("pool", Sb, 0.0, [T("Sb")])
            cnt = 0
            for i in tiles:
                b = cnt % 2
                cnt += 1
                gt = gt0 + i
                rows = slice(T0 + i * 128, T0 + (i + 1) * 128)
                kn_t, v_t, kT_t, qT_t = B["kn%d" % b], B["vt%d" % b], B["kT%d" % b], B["qT%d" % b]
                tkn, tv, tkT, tqT = T("kn%d" % b), T("vt%d" % b), T("kT%d" % b), T("qT%d" % b)
                self.dma(kn_t, self.d_kn[rows, :], (), [tkn], q="act")
                self.dma(v_t, self.d_v[rows, :], (), [tv], q="act")
                self.dma(v3(kT_t), self.d_kT[:, rows].rearrange("(h p) n -> p h n", p=128), (), [tkT], q="act")
                self.dma(v3(qT_t), self.d_qT[:, rows].rearrange("(h p) n -> p h n", p=128), (), [tqT], q="act")
                g4 = self.gx[:, gt * 8 + dr * 4:gt * 8 + dr * 4 + 4]
                be4 = self.gb[:, gt * 8 + dr * 4:gt * 8 + dr * 4 + 4]
                bch = lambda a, n=128: a.unsqueeze(2).to_broadcast([n, 4, 128])
                gsm, e20, bg = B["gsm"], B["e20"], B["bg"]
                kb, kbT, kbg, vb, kdec = B["kb"], B["kbT"], B["kbg"], B["vb"], B["kdec"]
                Rg, E, Es, Xf, Rf, Rb, qkT = B["Rg"], B["E"], B["Es"], B["Xf"], B["Rf"], B["Rb"], B["qkT"]
                usb, wT, vnew, tmp = B["usb"], B["wT"], B["vnew"], B["tmp"]
                self.mm(ps[3][:, 0:4], CUM, g4, True, True, ["const", "gx"], [pt[3]])
                self.mm(ps[3][:, 4:8], self.BD, g4, True, True, ["const", "gx"], [pt[3]])
                self.mm(ps[3][:, 8:12], self.onesA, g4, True, True, ["const", "gx"], [pt[3]])
                self.mm(ps[3][:, 12:16], self.onesB, g4, True, True, ["const", "gx"], [pt[3]])
                self.cp("dve", gsm[:, 0:16], ps[3][:, 0:16], [pt[3]], [T("gsm")])
                self.tt("dve", gsm[:, 16:20], gsm[:, 4:8], gsm[:, 0:4], ALU.subtract, [T("gsm")], [T("gsm")])
                self.act(e20[:, 0:20], gsm[:, 0:20], AF.Exp, [T("gsm")], [T("e20")])
                egc, etA, etB, erest = e20[:, 0:4], e20[:, 8:12], e20[:, 12:16], e20[:, 16:20]
                self.tt("dve", bg[:, 0:4], be4, egc, ALU.mult, ["gb", T("e20")], [T("bg")])
                yield
                self.tt("dve", v3(kb), v3(kn_t), bch(be4), ALU.mult, [tkn, "gb"], [T("kb")])
                for h in range(4):
                    self.tr(psb[3][:, 512 + h * 128:512 + (h + 1) * 128], kb[:, h * 128:(h + 1) * 128], self.ident, [T("kb"), "const"], [pt[3]])
                self.cp("act", kbT, psb[3][:, 512:1024], [pt[3]], [T("kbT")])
                self.tt("dve", v3(kbg), v3(kn_t), bch(bg[:, 0:4]), ALU.mult, [tkn, T("bg")], [T("kbg")])
                self.tt("pool", v3(vb), v3(v_t), bch(be4), ALU.mult, [tv, "gb"], [T("vb")])
                self.tt("pool", v3(kdec), v3(kn_t), bch(erest), ALU.mult, [tkn, T("e20")], [T("kdec")])
                self.tt("pool", v3(Rg), RMASK.unsqueeze(1).to_broadcast([128, 4, 128]), bch(g4), ALU.mult, ["const", "gx"], [T("Rg")])
                self.mm(ps[2][:, 0:512], ARGL, Rg, True, True, ["const", T("Rg")], [pt[2]])
                for h in range(4):
                    hs = slice(h * 128, (h + 1) * 128)
                    self.mm(ps[0][:, hs], kT_t[:, hs], kbT[:, hs], True, True, [tkT, T("kbT")], [pt[0]])
                    self.mm(ps[1][:, hs], kT_t[:, hs], qT_t[:, hs], True, True, [tkT, tqT], [pt[1]])
                self.act(E, ps[2][:, 0:512], AF.Exp, [pt[2]], [T("E")])
                self.tt("pool", v3(E), v3(E), MINCL.unsqueeze(1).to_broadcast([128, 4, 128]), ALU.mult, [T("E"), "const"], [T("E")])
                self.tt("pool", v3(Es), v3(E), MSTR.unsqueeze(1).to_broadcast([128, 4, 128]), ALU.mult, [T("E"), "const"], [T("Es")])
                self.tt("dve", Xf, ps[0][:, 0:512], Es, ALU.mult, [pt[0], T("Es")], [T("Xf")])
                P, Q = B["P0"], B["Q0"]
                self.cp("act", P, Xf, [T("Xf")], [T("P0")])
                self.tt("pool", v3(Rf), identbc, v3(Xf), ALU.subtract, ["const", T("Xf")], [T("Rf")])
                self.cp("act", Rb, Rf, [T("Rf")], [T("Rb")])
                self.tt("dve", qkT, ps[1][:, 0:512], E, ALU.mult, [pt[1], T("E")], [T("qkT")])
                for h in range(4):
                    self.tr(psb[3][:, 512 + h * 128:512 + (h + 1) * 128], P[:, h * 128:(h + 1) * 128], self.ident, [T("P0"), "const"], [pt[3]])
                self.cp("act", Q, psb[3][:, 512:1024], [pt[3]], [T("Q0")])
                yield
                pi = 0
                for k in range(1, 6):
                    tP, tQ = T("P%d" % pi), T("Q%d" % pi)
                    pn = 1 - pi
                    tPn, tQn = T("P%d" % pn), T("Q%d" % pn)
                    Pn, Qn = B["P%d" % pn], B["Q%d" % pn]
                    for h in range(4):
                        hs = slice(h * 128, (h + 1) * 128)
                        self.mm(ps[1][:, hs], P[:, hs], Q[:, hs], True, True, [tP, tQ], [pt[1]])
                    if k < 5:
                        for h in range(4):
                            hs = slice(h * 128, (h + 1) * 128)
                            self.mm(ps[0][:, hs], Q[:, hs], P[:, hs], True, True, [tP, tQ], [pt[0]])
                    self.cp("dve", Qn, ps[1][:, 0:512], [pt[1]], [tQn])
                    if k < 5:
                        self.cp("act", Pn, ps[0][:, 0:512], [pt[0]], [tPn])
                    for h in range(4):
                        hs = slice(h * 128, (h + 1) * 128)
                        self.mm(ps[2][:, hs], Qn[:, hs], Rb[:, hs], True, True, [tQn, T("Rb")], [pt[2]])
                    self.tt("dve", Rf, Rf, ps[2][:, 0:512], ALU.add, [T("Rf"), pt[2]], [T("Rf")])
                    self.cp("act", Rb, Rf, [T("Rf")], [T("Rb")])
                    P, Q, pi = Pn, Qn, pn
                    yield
                for h in range(4):
                    hs = slice(h * 128, (h + 1) * 128)
                    self.mm(ps[2][:, hs], Rb[:, hs], vb[:, hs], True, True, [T("Rb"), T("vb")], [pt[2]])
                    self.mm(ps[0][:, hs], kbg[:, hs], Rb[:, hs], True, True, [T("Rb"), T("kbg")], [pt[0]])
                self.cp("dve", usb, ps[2][:, 0:512], [pt[2]], [T("usb")])
                self.cp("act", wT, ps[0][:, 0:512], [pt[0]], [T("wT")])
                yield
                ob_ = B["ot%d" % b]
                tot = T("ot%d" % b)
                for r0 in chunks:
                    rs = slice(r0, r0 + 64)
                    etc = etA if r0 == 0 else etB
                    for h in range(4):
                        hs = slice(h * 128, (h + 1) * 128)
                        self.mm(ps[1][rs, hs], wT[:, h * 128 + r0:h * 128 + r0 + 64], Sb[:, hs], True, True, [T("wT"), T("Sb")], [pt[1]])
                    for h in range(4):
                        hs = slice(h * 128, (h + 1) * 128)
                        self.mm(ps[2][rs, hs], qT_t[:, h * 128 + r0:h * 128 + r0 + 64], Sb[:, hs], True, True, [tqT, T("Sb")], [pt[2]])
                    self.tt("dve", vnew[rs, :], usb[rs, :], ps[1][rs, 0:512], ALU.subtract, [T("usb"), pt[1]], [T("vnew")])
                    for h in range(4):
                        hs = slice(h * 128, (h + 1) * 128)
                        self.mm(ps[0][:, hs], kdec[rs, hs], vnew[rs, hs], True, True, [T("kdec"), T("vnew")], [pt[0]])
                    for h in range(4):
                        hs = slice(h * 128, (h + 1) * 128)
                        self.mm(ps[3][rs, hs], qkT[rs, h * 128 + r0:h * 128 + r0 + 64], vnew[rs, hs], True, True, [T("qkT"), T("vnew")], [pt[3]])
                    for h in range(4):
                        hs = slice(h * 128, (h + 1) * 128)
                        self.stt(Sf[:, hs], Sf[:, hs], etc[:, h:h + 1], ps[0][:, hs], ALU.mult, ALU.add, [T("Sf"), T("e20"), pt[0]], [T("Sf")])
                    self.cp("act", Sb, Sf, [T("Sf")], [T("Sb")])
                    self.tt("dve", v3(tmp)[rs], v3(ps[2][:, 0:512])[rs], bch(egc[rs, :], 64), ALU.mult, [pt[2], T("e20")], [T("tmpB")])
                    self.tt("dve", ob_[rs, :], tmp[rs, :], ps[3][rs, 0:512], ALU.add, [T("tmpB"), pt[3]], [tot])
                    yield
                dst = self.d_ob if dr == 1 else self.d_of
                self.dma(dst[rows, :], ob_, [tot], [("od%d" % dr, gt)])

        base = A.off
        BS = [mkbufs(k) for k in range(4)]
        offs = []
        T0 = 0
        gt0 = 0
        for S_ in self.seqs:
            offs.append((T0, S_, gt0))
            T0 += S_
            gt0 += S_ // 128
        groups = []
        i = 0
        while i < len(offs):
            if i + 1 < len(offs) and offs[i][1] == offs[i + 1][1]:
                groups.append([offs[i], offs[i + 1]])
                i += 2
            else:
                groups.append([offs[i]])
                i += 1
        for grp in groups:
            gens = []
            for k, (T0_, S_, gt0_) in enumerate(grp):
                gens.append(dn_pass(2 * k, T0_, S_, gt0_, 1, BS[2 * k], (0, 1, 2, 3)))
                gens.append(dn_pass(2 * k + 1, T0_, S_, gt0_, 0, BS[2 * k + 1], (4, 5, 6, 7)))
            for k, g in enumerate(gens):
                for _ in range((len(gens) - 1 - k) * (10 // len(gens))):
                    next(g)
            alive = list(gens)
            while alive:
                for g in list(alive):
                    try:
                        next(g)
                    except StopIteration:
                        alive.remove(g)
        self.S.barrier()
        A.off = base
        obt = [A.f32(512), A.f32(512)]
        oft = [A.f32(512), A.f32(512)]
        dzt = [A.bf16(512), A.bf16(512)]
        osum = [A.f32(512), A.f32(512)]
        sq = A.f32(512)
        th = [A.f32(512), A.f32(512)]
        mixo = [A.bf16(512), A.bf16(512)]
        smc = [A.f32(8), A.f32(8)]
        bch = lambda a, n=128: a.unsqueeze(2).to_broadcast([n, 4, 128])
        for gt in range(self.NT // 128):
            b = gt % 2
            rows = slice(gt * 128, (gt + 1) * 128)
            X = lambda nm: "%s%d" % (nm, b)
            self.dma(obt[b], self.d_ob[rows, :], [("od1", gt)], [X("obt")], q="act")
            self.dma(oft[b], self.d_of[rows, :], [("od0", gt)], [X("oft")], q="act")
            self.dma(dzt[b], self.d_dz[rows, :], (), [X("dzt")], q="act")
            self.tt("pool", osum[b], oft[b], obt[b], ALU.add, [X("oft"), X("obt")], [X("osum")])
            self.act(sq, osum[b], AF.Square, [X("osum")], ["sqB"])
            self.red(smc[b][:, 0:4], v3(sq), ["sqB"], [X("ssD")])
            self.rsqrt_small(smc[b][:, 4:8], smc[b][:, 0:4], 1.0 / 128, [X("ssD")], [X("rsD")])
            self.tt("dve", v3(osum[b]), v3(osum[b]), bch(smc[b][:, 4:8]), ALU.mult, [X("osum"), X("rsD")], [X("osum")])
            self.tt("pool", v3(osum[b]), v3(osum[b]), self.dng.unsqueeze(1).to_broadcast([128, 4, 128]), ALU.mult, [X("osum"), "const"], [X("osum")])
            self.act(th[b], dzt[b], AF.Tanh, [X("dzt")], [X("thB")], scale=0.5)
            self.stt(th[b], th[b], 1.0, dzt[b], ALU.add, ALU.mult, [X("thB"), X("dzt")], [X("thB")])
            self.stt(mixo[b], osum[b], 0.5, th[b], ALU.mult, ALU.mult, [X("osum"), X("thB")], [X("mixo")])
            self.dma(self.d_mix[rows, 0:512], mixo[b], [X("mixo")], [])

    def phaseB2(self):
        A, ps = self.A, self.ps
        Smax = max(self.seqs)
        ntm = Smax // 128
        kaT = [A.bf16(Smax) for _ in range(3)]
        va = [A.bf16(ntm * 130).rearrange("p (t d) -> p t d", d=130) for _ in range(3)]
        qa = [A.bf16(512), A.bf16(512), A.bf16(512)]
        PT = [A.bf16(1024).rearrange("p (c n) -> p c n", c=2) for _ in range(3)]
        o1 = A.f32(128)
        oo = A.f32(128)
        ob = [A.bf16(128), A.bf16(128)]
        sm = A.f32(16)
        sqo = A.f32(128)
        steps = []
        T0 = 0
        for S_ in self.seqs:
            nt = S_ // 128
            QB = min(512, S_)
            for h in range(4):
                for qb in range(S_ // QB):
                    for kt in range(nt):
                        steps.append((T0, S_, h, qb, kt))
            T0 += S_
        state = {"hkey": None, "hcount": -1, "qkey": None, "qcount": -1}

        def prep_qk(n):
            T0, S_, h, qb, kt = steps[n]
            nt = S_ // 128
            QB = min(512, S_)
            if state["hkey"] != (T0, h):
                state["hkey"] = (T0, h)
                state["hcount"] += 1
                hb = state["hcount"] % 3
                tk, tv = "kaT%d" % hb, "va%d" % hb
                self.dma(kaT[hb][:, 0:S_], self.d_kaT[h * 128:(h + 1) * 128, T0:T0 + S_], (), [tk])
                self.dma(va[hb][:, 0:nt, 0:128], self.d_va[T0:T0 + S_, h * 128:(h + 1) * 128].rearrange("(t p) d -> p t d", p=128), (), [tv])
                self.memset("pool", va[hb][:, 0:nt, 128:129], 1.0, [tv])
            if state["qkey"] != (T0, h, qb):
                state["qkey"] = (T0, h, qb)
                state["qcount"] += 1
                qq = state["qcount"] % 3
                self.dma(qa[qq][:, 0:QB], self.d_qaT[h * 128:(h + 1) * 128, T0 + qb * QB:T0 + (qb + 1) * QB], (), ["qa%d" % qq])
            hb = state["hcount"] % 3
            qq = state["qcount"] % 3
            a = 2 * (n % 2)
            for comp in range(2):
                r0 = comp * 64
                self.mm(ps[a + comp][:, 0:QB], kaT[hb][r0:r0 + 64, kt * 128:(kt + 1) * 128], qa[qq][r0:r0 + 64, 0:QB], True, True,
                        ["kaT%d" % hb, "qa%d" % qq], ["ps%d" % (a + comp)])
            return hb

        oc = {"n": 0, "q": 0}
        accS = [A.f32(3 * 512).rearrange("p (b n) -> p b n", b=3) for _ in range(2)]

        def do_exp(n):
            T0, S_, h, qb, kt = steps[n]
            QB = min(512, S_)
            a = 2 * (n % 2)
            pb = n % 3
            tp = "PT%d" % pb
            if QB == 512:
                self.act(PT[pb].rearrange("p c n -> p (c n)"), self.psall[:, a * 512:(a + 2) * 512], AF.Exp, ["ps%d" % a, "ps%d" % (a + 1)], [tp])
            else:
                for comp in range(2):
                    self.act(PT[pb][:, comp, 0:QB], ps[a + comp][:, 0:QB], AF.Exp, ["ps%d" % (a + comp)], [tp])

        def do_pv(n, hb):
            T0, S_, h, qb, kt = steps[n]
            nt = S_ // 128
            QB = min(512, S_)
            nqs = QB // 128
            pb = n % 3
            tp = "PT%d" % pb
            tv = "va%d" % hb
            place = {}
            idx = 0
            for comp in range(2):
                for qs in range(nqs):
                    place[(comp, qs)] = (4 + idx // 3, (idx % 3) * 129)
                    idx += 1
            if kt == 0:
                state["started"] = set()
            started = state["started"]
            for comp in range(2):
                for qs in range(nqs):
                    bk, col = place[(comp, qs)]
                    st = bk not in started
                    started.add(bk)
                    self.mm(ps[bk][:, col:col + 129], PT[pb][:, comp, qs * 128:(qs + 1) * 128], va[hb][:, kt, 0:129], st, kt == nt - 1,
                            [tp, tv], ["ps%d" % bk])
            if kt == nt - 1:
                qp = oc["q"] % 2
                oc["q"] += 1
                aS = accS[qp]
                ta = "accS%d" % qp
                nacc = 2 * nqs
                for bk in sorted(set(b for b, _ in place.values())):
                    ncols = 129 * min(3, nacc - 3 * (bk - 4))
                    self.cp("dve", aS[:, bk - 4, 0:ncols], ps[bk][:, 0:ncols], ["ps%d" % bk], [ta])
                for qs in range(nqs):
                    b0, c0 = place[(0, qs)]
                    b1, c1 = place[(1, qs)]
                    oq = oc["n"] % 2
                    oc["n"] += 1
                    a0_, a1_ = aS[:, b0 - 4, :], aS[:, b1 - 4, :]
                    self.S.add("dve", lambda e, a0_=a0_, c0=c0: e.reciprocal(out=sm[:, 0:1], in_=a0_[:, c0 + 128:c0 + 129]), [ta], ["smr0"])
                    self.S.add("dve", lambda e, a1_=a1_, c1=c1: e.reciprocal(out=sm[:, 1:2], in_=a1_[:, c1 + 128:c1 + 129]), [ta], ["smr1"])
                    self.tt("dve", sm[:, 2:3], sm[:, 1:2], self.lam[:, 3:4], ALU.mult, ["smr1", "const"], ["smr2"])
                    self.ts("dve", o1, a0_[:, c0:c0 + 128], sm[:, 0:1], None, ALU.mult, None, [ta, "smr0"], ["o1"])
                    self.stt(oo, a1_[:, c1:c1 + 128], sm[:, 2:3], o1, ALU.mult, ALU.add, [ta, "smr2", "o1"], ["oo"])
                    self.tt("pool", sqo, oo, oo, ALU.mult, ["oo"], ["sqo"])
                    self.red(sm[:, 4:5], sqo, ["sqo"], ["ssB"])
                    self.rsqrt_small(sm[:, 5:6], sm[:, 4:5], 1.0 / 128, ["ssB"], ["rsB"])
                    self.stt(ob[oq], oo, sm[:, 5:6], self.subg, ALU.mult, ALU.mult, ["oo", "rsB", "const"], ["ob%d" % oq])
                    r0 = T0 + qb * QB + qs * 128
                    self.dma(self.d_mix[r0:r0 + 128, 512 + h * 128:512 + (h + 1) * 128], ob[oq], ["ob%d" % oq], [])

        hbs = {}
        hbs[0] = prep_qk(0)
        for n in range(len(steps)):
            if n + 1 < len(steps):
                hbs[n + 1] = prep_qk(n + 1)
            if n >= 1:
                do_pv(n - 1, hbs[n - 1])
            do_exp(n)
        do_pv(len(steps) - 1, hbs[len(steps) - 1])

    def phaseC1(self):
        A, ps, psb = self.A, self.ps, self.psb
        wout = A.bf16(8 * D).rearrange("p (k n) -> p k n", k=8)
        stage = [A.f32(1024), A.f32(1024)]
        self.load_weight_bf(wout, self.w_out, 8, D, None, "wout", stage)
        ntl = self.NT // 128

        def c1_stream(sid, tiles, bk):
            ps_ = [ps[b] for b in bk]
            psb_ = [psb[b] for b in bk]
            pt = ["ps%d" % b for b in bk]
            N = lambda nm: "%s_s%d" % (nm, sid)
            mixb = [A.bf16(1024), A.bf16(1024)]
            mixT = A.bf16(1024).rearrange("p (k n) -> p k n", k=8)
            xin = [A.f32(1024), A.f32(1024)]
            x1 = [A.f32(1024), A.f32(1024)]
            hnb = [A.bf16(1024), A.bf16(1024)]
            hnT = [A.bf16(1024).rearrange("p (k n) -> p k n", k=8) for _ in range(2)]
            jk = A.bf16(1024)
            sm = [A.f32(8), A.f32(8)]

            def p1(j):
                i = tiles[j]
                b = j % 2
                rows = slice(i * 128, (i + 1) * 128)
                tm, tx, t1 = N("mixb%d" % b), N("xinC%d" % b), N("x1C%d" % b)
                self.dma(mixb[b], self.d_mix[rows, :], (), [tm], q="act")
                self.dma(xin[b], self.x[rows, :], (), [tx], q="act")
                for kc in range(8):
                    self.tr(psb_[0][:, kc * 128:(kc + 1) * 128], mixb[b][:, kc * 128:(kc + 1) * 128], self.ident, [tm, "const"], [pt[0]])
                self.cp("act", mixT, psb_[0][:, 0:1024].rearrange("p (k n) -> p k n", k=8), [pt[0]], [N("mixT")])
                for nb in range(2):
                    for kc in range(8):
                        self.mm(ps_[1 + nb][:, 0:512], mixT[:, kc, :], wout[:, kc, nb * 512:(nb + 1) * 512], kc == 0, kc == 7, [N("mixT"), "wout"], [pt[1 + nb]])
                    self.tt("dve", x1[b][:, nb * 512:(nb + 1) * 512], ps_[1 + nb][:, 0:512], xin[b][:, nb * 512:(nb + 1) * 512], ALU.add,
                            [pt[1 + nb], tx], [t1])
                self.dma(self.d_x1[rows, :], x1[b], [t1], [("x1d", i)])
                self.act(jk, x1[b], AF.Square, [t1], [N("jkC"), N("ssC%d" % b)], accum_out=sm[b][:, 0:1])
                self.rsqrt_small(sm[b][:, 1:2], sm[b][:, 0:1], 1.0 / D, [N("ssC%d" % b)], [N("rsC%d" % b)])
                self.ts("dve", hnb[b], x1[b], sm[b][:, 1:2], None, ALU.mult, None, [t1, N("rsC%d" % b)], [N("hnb%d" % b)])

            def p2(j):
                i = tiles[j]
                b = j % 2
                rows = slice(i * 128, (i + 1) * 128)
                for kc in range(8):
                    self.tr(psb_[3][:, kc * 128:(kc + 1) * 128], hnb[b][:, kc * 128:(kc + 1) * 128], self.ident, [N("hnb%d" % b), "const"], [pt[3]])
                self.cp("act", hnT[b], psb_[3][:, 0:1024].rearrange("p (k n) -> p k n", k=8), [pt[3]], [N("hnT%d" % b)])
                self.dma(self.d_hnT[:, rows].rearrange("(k p) n -> p k n", p=128), hnT[b], [N("hnT%d" % b)], [("hnTd", i)])

            p1(0)
            yield
            for j in range(len(tiles)):
                if j + 1 < len(tiles):
                    p1(j + 1)
                    yield
                p2(j)
                yield

        gens = [c1_stream(k, list(range(k, ntl, 4)), (0, 1, 2, 3) if k % 2 == 0 else (4, 5, 6, 7)) for k in range(4)]
        for k, g in enumerate(gens):
            for _ in range(3 - k):
                next(g)
        while gens:
            for g in list(gens):
                try:
                    next(g)
                except StopIteration:
                    gens.remove(g)

    def phaseC2(self):
        A, ps = self.A, self.ps
        wup = A.bf16(8 * 2 * DFF).rearrange("p (k n) -> p k n", k=8)
        wdn = A.bf16(22 * D).rearrange("p (k n) -> p k n", k=22)
        mark = A.off
        stage = [A.f32(1408), A.f32(1408)]
        self.load_weight_bf(wup, self.w_up, 8, 2 * DFF, self.g2, "wup", stage)
        self.load_weight_bf(wdn, self.w_down, 22, D, None, "wdn", stage)
        self.S.barrier()
        A.off = mark
        BLM = min(512, max(self.seqs))
        hb = A.bf16(8 * (BLM + 2)).rearrange("p (k n) -> p k n", k=8)
        actT = A.bf16(22 * BLM).rearrange("p (c n) -> p c n", c=22)
        gsb = A.f32(2 * (BLM + 2))
        acc = A.f32(BLM)
        sl = A.f32(BLM)
        x1 = A.f32(1024)
        T0 = 0
        for S_ in self.seqs:
            BL = min(512, S_)
            for blk in range(S_ // BL):
                t0 = T0 + blk * BL
                ntb = BL // 128
                first, last = blk == 0, blk == S_ // BL - 1
                lo = 1 if first else 0
                hi = BL + 1 if last else BL + 2
                c_lo, c_hi = t0 - 1 + lo, t0 - 1 + hi
                rd = [("hnTd", j) for j in range(c_lo // 128, (c_hi - 1) // 128 + 1)]
                self.dma(hb[:, :, lo:hi], self.d_hnT[:, c_lo:c_hi].rearrange("(k p) n -> p k n", p=128), rd, ["hb"])
                if first:
                    self.memset("pool", hb[:, :, 0:1], 0.0, ["hb"])
                if last:
                    self.memset("pool", hb[:, :, BL + 1:BL + 2], 0.0, ["hb"])
                for c in range(22):
                    bg = c % 2
                    bu = 2 + c % 2
                    for kc in range(8):
                        self.mm(ps[bg][:, 0:BL], wup[:, kc, c * 128:(c + 1) * 128], hb[:, kc, 1:BL + 1], kc == 0, kc == 7, ["hb", "wup"], ["ps%d" % bg])
                    for kc in range(8):
                        self.mm(ps[4][:, 0:2], wup[:, kc, c * 128:(c + 1) * 128], hb[:, kc, 0:BL + 2:BL + 1], kc == 0, kc == 7, ["hb", "wup"], ["ps4"])
                    for kc in range(8):
                        self.mm(ps[bu][:, 0:BL], wup[:, kc, DFF + c * 128:DFF + (c + 1) * 128], hb[:, kc, 1:BL + 1], kc == 0, kc == 7,
                                ["hb", "wup"], ["ps%d" % bu])
                    self.cp("act", gsb[:, 1:BL + 1], ps[bg][:, 0:BL], ["ps%d" % bg], ["gsb"])
                    self.cp("dve", gsb[:, 0:BL + 2:BL + 1], ps[4][:, 0:2], ["ps4"], ["gsb"])
                    self.ts("dve", acc[:, 0:BL], gsb[:, 0:BL], self.fw[:, c * 3:c * 3 + 1], self.fb[:, c:c + 1], ALU.mult, ALU.add, ["gsb", "const"], ["accC"])
                    self.stt(acc[:, 0:BL], gsb[:, 1:BL + 1], self.fw[:, c * 3 + 1:c * 3 + 2], acc[:, 0:BL], ALU.mult, ALU.add, ["gsb", "accC", "const"], ["accC"])
                    self.stt(acc[:, 0:BL], gsb[:, 2:BL + 2], self.fw[:, c * 3 + 2:c * 3 + 3], acc[:, 0:BL], ALU.mult, ALU.add, ["gsb", "accC", "const"], ["accC"])
                    self.act(sl[:, 0:BL], acc[:, 0:BL], AF.Silu, ["accC"], ["sl"])
                    self.tt("dve", actT[:, c, 0:BL], sl[:, 0:BL], ps[bu][:, 0:BL], ALU.mult, ["sl", "ps%d" % bu], ["actT"])
                for j in range(ntb):
                    ti = t0 // 128 + j
                    rows = slice(t0 + j * 128, t0 + (j + 1) * 128)
                    self.dma(x1, self.d_x1[rows, :], [("x1d", ti)], ["x1F"])
                    for nb in range(2):
                        bk = 5 + nb
                        for c in range(22):
                            self.mm(ps[bk][:, 0:512], actT[:, c, j * 128:(j + 1) * 128], wdn[:, c, nb * 512:(nb + 1) * 512], c == 0, c == 21,
                                    ["actT", "wdn"], ["ps%d" % bk])
                        self.tt("dve", x1[:, nb * 512:(nb + 1) * 512], ps[bk][:, 0:512], x1[:, nb * 512:(nb + 1) * 512], ALU.add, ["ps%d" % bk, "x1F"], ["x1F"])
                    self.dma(self.d_x1[rows, :], x1, ["x1F"], [("x1d", ti)])
            T0 += S_

    def phaseC3(self):
        A, ps, psb = self.A, self.ps, self.psb
        wg = A.bf16(8 * D).rearrange("p (k n) -> p k n", k=8)
        pp = A.bf16(2 * D).rearrange("p (k n) -> p k n", k=2)
        stage = [A.f32(1024), A.f32(1024)]
        self.load_weight_bf(wg, self.w_ple_gate, 8, D, self.g3, "wg", stage)
        self.load_weight_bf(pp, self.ple_proj, 2, D, None, "pp", stage)
        pleg = A.f32(1024)
        self.dma(pleg, self.ple_norm_g[0:1, :].partition_broadcast(128), (), ["pleg"])
        ntl = self.NT // 128

        def c3_stream(sid, tiles, bk):
            ps_ = [ps[b] for b in bk]
            psb_ = [psb[b] for b in bk]
            pt = ["ps%d" % b for b in bk]
            N = lambda nm: "%s_s%d" % (nm, sid)
            x2 = [A.f32(1024), A.f32(1024)]
            pin = [A.f32(256), A.f32(256)]
            pbf = A.bf16(256)
            peT = A.bf16(256).rearrange("p (k n) -> p k n", k=2)
            ee = [A.f32(1024), A.f32(1024)]
            xnb = [A.bf16(1024), A.bf16(1024)]
            xnT = A.bf16(1024).rearrange("p (k n) -> p k n", k=8)
            th = A.f32(1024)
            yb = [A.f32(1024), A.f32(1024)]
            jk = A.bf16(1024)
            sm = [A.f32(8), A.f32(8)]

            def q1(j):
                i = tiles[j]
                b = j % 2
                rows = slice(i * 128, (i + 1) * 128)
                X = lambda nm: N("%s%d" % (nm, b))
                tx, tp = X("x2_"), X("pin")
                s_ = sm[b]
                self.dma(x2[b], self.d_x1[rows, :], [("x1d", i)], [tx], q="act")
                self.dma(pin[b], self.p[rows, :], (), [tp], q="act")
                self.cp("dve", pbf, pin[b], [tp], [N("pbf")])
                for k in range(2):
                    self.tr(psb_[0][:, k * 128:(k + 1) * 128], pbf[:, k * 128:(k + 1) * 128], self.ident, [N("pbf"), "const"], [pt[0]])
                self.cp("act", peT, psb_[0][:, 0:256].rearrange("p (k n) -> p k n", k=2), [pt[0]], [N("peT")])
                for nb in range(2):
                    for k in range(2):
                        self.mm(ps_[1 + nb][:, 0:512], peT[:, k, :], pp[:, k, nb * 512:(nb + 1) * 512], k == 0, k == 1, [N("peT"), "pp"], [pt[1 + nb]])
                    self.act(jk[:, 0:512], ps_[1 + nb][:, 0:512], AF.Square, [pt[1 + nb]], [N("jk3"), X("sse%d_" % nb)], accum_out=s_[:, nb:nb + 1])
                self.tt("dve", s_[:, 2:3], s_[:, 0:1], s_[:, 1:2], ALU.add, [X("sse0_"), X("sse1_")], [X("sse")])
                self.rsqrt_small(s_[:, 3:4], s_[:, 2:3], 1.0 / D, [X("sse")], [X("rse")])
                for nb in range(2):
                    self.stt(ee[b][:, nb * 512:(nb + 1) * 512], ps_[1 + nb][:, 0:512], s_[:, 3:4], pleg[:, nb * 512:(nb + 1) * 512], ALU.mult, ALU.mult,
                             [pt[1 + nb], X("rse"), "pleg"], [X("ee")])
                self.act(jk, x2[b], AF.Square, [tx], [N("jk3"), X("ssx")], accum_out=s_[:, 4:5])
                self.rsqrt_small(s_[:, 5:6], s_[:, 4:5], 1.0 / D, [X("ssx")], [X("rsx")])
                self.ts("dve", xnb[b], x2[b], s_[:, 5:6], None, ALU.mult, None, [tx, X("rsx")], [X("xnb3")])

            def q2(j):
                i = tiles[j]
                b = j % 2
                rows = slice(i * 128, (i + 1) * 128)
                X = lambda nm: N("%s%d" % (nm, b))
                tx, ty = X("x2_"), X("yb")
                for kc in range(8):
                    self.tr(psb_[3][:, kc * 128:(kc + 1) * 128], xnb[b][:, kc * 128:(kc + 1) * 128], self.ident, [X("xnb3"), "const"], [pt[3]])
                self.cp("act", xnT, psb_[3][:, 0:1024].rearrange("p (k n) -> p k n", k=8), [pt[3]], [N("xnT3")])
                for nb in range(2):
                    for kc in range(8):
                        self.mm(ps_[1 + nb][:, 0:512], xnT[:, kc, :], wg[:, kc, nb * 512:(nb + 1) * 512], kc == 0, kc == 7, [N("xnT3"), "wg"], [pt[1 + nb]])
                    self.act(th[:, nb * 512:(nb + 1) * 512], ps_[1 + nb][:, 0:512], AF.Tanh, [pt[1 + nb]], [N("th")], scale=0.5)
                self.stt(th, th, 1.0, ee[b], ALU.add, ALU.mult, [N("th"), X("ee")], [N("th")])
                self.stt(yb[b], th, 0.5, x2[b], ALU.mult, ALU.add, [N("th"), tx], [ty])
                self.dma(self.y[rows, :], yb[b], [ty], [])

            q1(0)
            yield
            for j in range(len(tiles)):
                if j + 1 < len(tiles):
                    q1(j + 1)
                    yield
                q2(j)
                yield

        gens = [c3_stream(k, list(range(k, ntl, 4)), (0, 1, 2, 3) if k % 2 == 0 else (4, 5, 6, 7)) for k in range(4)]
        for k, g in enumerate(gens):
            for _ in range(3 - k):
                next(g)
        while gens:
            for g in list(gens):
                try:
                    next(g)
                except StopIteration:
                    gens.remove(g)


def rope_table():
    inv = ROPE_THETA ** (-np.arange(0, 16, 2, dtype=np.float32) / 16.0)
    pos = np.arange(4096, dtype=np.float32)
    ang = (pos[:, None] * inv[None, :].astype(np.float32)).astype(np.float32)
    cs = np.concatenate([np.cos(ang), np.sin(ang)], axis=1).astype(np.float32)
    return np.ascontiguousarray(cs.reshape(32, 128, 16).transpose(1, 0, 2).reshape(128, 512))


def host_layout(inp):
    f = lambda a: np.ascontiguousarray(np.asarray(a, dtype=np.float32))
    m = {}
    m["w_in"] = f(inp["w_in"][0]); m["w_out"] = f(inp["w_out"][0]); m["w_up"] = f(inp["w_up"][0])
    m["w_down"] = f(inp["w_down"][0]); m["ple_proj"] = f(inp["ple_proj"][0]); m["w_ple_gate"] = f(inp["w_ple_gate"][0])
    m["ln1_g"] = f(np.asarray(inp["ln1_g"][0]).reshape(8, 128).T)
    m["ln2_g"] = f(np.asarray(inp["ln2_g"][0]).reshape(8, 128).T)
    m["ple_gate_norm_g"] = f(np.asarray(inp["ple_gate_norm_g"][0]).reshape(8, 128).T)
    m["ple_norm_g"] = f(np.asarray(inp["ple_norm_g"][0]).reshape(1, D))
    m["cw_l"] = f(np.asarray(inp["dn_conv_w"][0]).reshape(5, 12, 128).transpose(2, 1, 0).reshape(128, 60))
    m["fw_l"] = f(np.asarray(inp["ffn_conv_w"][0]).reshape(3, 22, 128).transpose(2, 1, 0).reshape(128, 66))
    m["fb_l"] = f(np.asarray(inp["ffn_conv_b"][0]).reshape(22, 128).T)
    m["dn_a_log"] = f(np.asarray(inp["dn_a_log"][0]).reshape(1, 8))
    m["dn_dt_bias"] = f(np.asarray(inp["dn_dt_bias"][0]).reshape(1, 8))
    m["dn_norm_g"] = f(np.asarray(inp["dn_norm_g"][0]).reshape(1, 128))
    m["da_qk_norm_g"] = f(np.asarray(inp["da_qk_norm_g"][0]).reshape(1, 128))
    m["da_lambda"] = f(np.asarray(inp["da_lambda"][0]).reshape(1, 256))
    m["da_subln_g"] = f(np.asarray(inp["da_subln_g"][0]).reshape(1, 128))
    m["rope_cs"] = rope_table()
    return m


def kernel(**inp):
    xp = np.asarray(inp["x_prompt"], dtype=np.float32)
    xs = np.asarray(inp["x_sample"], dtype=np.float32)
    pp = np.asarray(inp["p_prompt"], dtype=np.float32)[0]
    psm = np.asarray(inp["p_sample"], dtype=np.float32)[0]
    nb, SP = xp.shape[0], xp.shape[1]
    SS = xs.shape[1]
    per = xs.shape[0] // nb
    seqs = [SP] + [SS] * per
    common = host_layout(inp)
    in_maps = []
    for c in range(nb):
        m = dict(common)
        m["x"] = np.ascontiguousarray(np.concatenate([xp[c]] + [xs[c * per + j] for j in range(per)], axis=0))
        m["p"] = np.ascontiguousarray(np.concatenate([pp[c]] + [psm[c * per + j] for j in range(per)], axis=0))
        in_maps.append(m)
    nc = Builder(seqs).build()
    res = run_bass_kernel_spmd(nc, in_maps, core_ids=list(range(nb)))
    yp = np.stack([res.results[c]["y"][0:SP] for c in range(nb)], axis=0)
    ys = np.stack([res.results[c]["y"][SP + j * SS:SP + (j + 1) * SS] for c in range(nb) for j in range(per)], axis=0)
    return (yp.astype(np.float32), ys.astype(np.float32))
```

```python
import contextlib
import math

import numpy as np
import concourse.bass as bass
import concourse.mybir as mybir
from concourse.bass_utils import run_bass_kernel_spmd

F32 = mybir.dt.float32
BF16 = mybir.dt.bfloat16
AF = mybir.ActivationFunctionType
ALU = mybir.AluOpType
AX = mybir.AxisListType

D = 1024
NCOL = 3600
DFF = 2816
PLE = 256
EPS = 1e-6
LAM_INIT = 0.8 - 0.6 * math.exp(-0.3 * 0)
ROPE_THETA = 500000.0

ENGS = ("pe", "act", "dve", "pool", "sp")
NDMASEM = 16


class Op:
    __slots__ = ("eng", "fn", "reads", "writes", "dma", "idx", "waits", "sig", "sem", "val")

    def __init__(self, eng, fn, reads, writes, dma):
        self.eng = eng
        self.fn = fn
        self.reads = reads
        self.writes = writes
        self.dma = dma
        self.waits = []
        self.sig = False
        self.sem = None
        self.val = 0


class Sched:
    def __init__(self, nc):
        self.nc = nc
        self.ops = []
        self.last_w = {}
        self.readers = {}
        self.last_eng = {}
        self.recent_dma = {e: [] for e in ENGS}

    def add(self, eng, fn, reads=(), writes=(), dma=False):
        op = Op(eng, fn, tuple(reads), tuple(writes), dma)
        op.idx = len(self.ops)
        deps = {}
        for t in op.reads:
            w = self.last_w.get(t)
            if w is not None:
                deps[w.idx] = w
        for t in op.writes:
            w = self.last_w.get(t)
            if w is not None:
                deps[w.idx] = w
            for r in self.readers.get(t, ()):
                deps[r.idx] = r
        for d in deps.values():
            if self._needs_sync(d, op):
                op.waits.append(d)
                d.sig = True
        for t in op.writes:
            self.last_w[t] = op
            self.readers[t] = []
        for t in op.reads:
            if t not in op.writes:
                self.readers.setdefault(t, []).append(op)
        self.ops.append(op)
        if dma:
            lst = self.recent_dma[eng]
            lst.append(op)
            if len(lst) > NDMASEM:
                lst.pop(0)
        else:
            self.last_eng[eng] = op
        return op

    def barrier(self):
        prods = list(self.last_eng.values())
        for e in ENGS:
            prods.extend(self.recent_dma[e])
        new = []
        for e in ENGS:
            op = Op(e, None, (), (), False)
            op.idx = len(self.ops)
            for d in prods:
                op.waits.append(d)
                d.sig = True
            self.ops.append(op)
            new.append(op)
        for op in new:
            self.last_eng[op.eng] = op
        self.last_w = {}
        self.readers = {}

    def _needs_sync(self, prod, cons):
        if prod.dma or cons.dma:
            return True
        if prod.eng != cons.eng:
            return True
        if prod.eng == "pe":
            return False
        for t in cons.reads:
            if t in prod.writes:
                return True
        for t in cons.writes:
            if t in prod.writes:
                return True
        return False

    def emit(self):
        nc = self.nc
        with contextlib.ExitStack() as es:
            csem = {e: es.enter_context(nc.semaphore("c_" + e)) for e in ENGS}
            dsem = {e: [es.enter_context(nc.semaphore("d_%s%d" % (e, i))) for i in range(NDMASEM)] for e in ENGS}
            ccount = {e: 0 for e in ENGS}
            dcount = {e: [0] * NDMASEM for e in ENGS}
            dn = {e: 0 for e in ENGS}
            prevdma = {}
            for op in self.ops:
                if op.fn is None:
                    op.sig = False
                    continue
                if op.dma:
                    k = dn[op.eng] % NDMASEM
                    dn[op.eng] += 1
                    dcount[op.eng][k] += 16
                    op.sem = dsem[op.eng][k]
                    op.val = dcount[op.eng][k]
                    op.sig = True
                    prev = prevdma.get((op.eng, k))
                    if prev is not None:
                        op.waits.append(prev)
                    prevdma[(op.eng, k)] = op
                elif op.sig:
                    ccount[op.eng] += 1
                    op.sem = csem[op.eng]
                    op.val = ccount[op.eng]
            per = {e: [o for o in self.ops if o.eng == e] for e in ENGS}
            finals = {}
            for op in self.ops:
                if op.dma:
                    key = id(op.sem)
                    if key not in finals or finals[key][1] < op.val:
                        finals[key] = (op.sem, op.val)

            def run(e, eng):
                waited = {}
                for op in per[e]:
                    for d in op.waits:
                        if d.sem is None:
                            continue
                        key = id(d.sem)
                        if waited.get(key, 0) >= d.val:
                            continue
                        eng.wait_ge(d.sem, d.val)
                        waited[key] = d.val
                    if op.fn is None:
                        continue
                    ins = op.fn(eng)
                    if op.sig:
                        ins.then_inc(op.sem, 16 if op.dma else 1)
                if e == "sp":
                    for sem, val in finals.values():
                        eng.wait_ge(sem, val)

            with nc.Block() as block:
                @block.tensor
                def _(eng):
                    run("pe", eng)

                @block.scalar
                def _(eng):
                    run("act", eng)

                @block.vector
                def _(eng):
                    run("dve", eng)

                @block.gpsimd
                def _(eng):
                    run("pool", eng)

                @block.sync
                def _(eng):
                    run("sp", eng)


class Arena:
    def __init__(self, ap, words):
        self.ap = ap
        self.words = words
        self.off = 0

    def f32(self, n):
        n2 = (n + 7) // 8 * 8
        assert self.off + n2 <= self.words, ("SBUF arena overflow", self.off, n2, self.words)
        a = self.ap[:, self.off:self.off + n]
        self.off += n2
        return a

    def bf16(self, n):
        w = (n + 1) // 2
        w2 = (w + 7) // 8 * 8
        assert self.off + w2 <= self.words, ("SBUF arena overflow", self.off, w2, self.words)
        a = self.ap[:, self.off:self.off + w].bitcast(BF16)[:, 0:n]
        self.off += w2
        return a


class Builder:
    def __init__(self, seqs, debug=False):
        self.seqs = list(seqs)
        self.NT = sum(self.seqs)
        self.debug = debug
        self.nc = bass.Bass("TRN2", target_bir_lowering=False)
        self.S = Sched(self.nc)
        self.uid = 0

    def u(self, base):
        self.uid += 1
        return "%s#%d" % (base, self.uid)

    def act(self, out, in_, func, r, w, **kw):
        return self.S.add("act", lambda e: e.activation(out=out, in_=in_, func=func, **kw), r, w)

    def tt(self, eng, out, in0, in1, op, r, w):
        return self.S.add(eng, lambda e: e.tensor_tensor(out=out, in0=in0, in1=in1, op=op), r, w)

    def ts(self, eng, out, in0, s1, s2, op0, op1, r, w, **kw):
        if op1 is None:
            return self.S.add(eng, lambda e: e.tensor_scalar(out=out, in0=in0, scalar1=s1, scalar2=None, op0=op0, **kw), r, w)
        return self.S.add(eng, lambda e: e.tensor_scalar(out=out, in0=in0, scalar1=s1, scalar2=s2, op0=op0, op1=op1, **kw), r, w)

    def stt(self, out, in0, scalar, in1, op0, op1, r, w):
        return self.S.add("dve", lambda e: e.scalar_tensor_tensor(out=out, in0=in0, scalar=scalar, in1=in1, op0=op0, op1=op1), r, w)

    def cp(self, eng, out, in_, r, w):
        if eng == "act":
            return self.S.add("act", lambda e: e.copy(out=out, in_=in_), r, w)
        return self.S.add(eng, lambda e: e.tensor_copy(out=out, in_=in_), r, w)

    def red(self, out, in_, r, w, op=ALU.add):
        return self.S.add("dve", lambda e: e.tensor_reduce(out=out, in_=in_, axis=AX.X, op=op), r, w)

    def mm(self, out, lhsT, rhs, start, stop, r, w):
        return self.S.add("pe", lambda e: e.matmul(out, lhsT=lhsT, rhs=rhs, start=start, stop=stop, skip_group_check=True), r, w)

    def tr(self, out, in_, ident, r, w):
        return self.S.add("pe", lambda e: e.transpose(out=out, in_=in_, identity=ident), r, w)

    def dma(self, out, in_, r, w, q="sp"):
        return self.S.add(q, lambda e: e.dma_start(out=out, in_=in_), r, w, dma=True)

    def memset(self, eng, ap, val, w):
        return self.S.add(eng, lambda e: e.memset(ap, val), (), w)

    def rsqrt_small(self, out, in_, scale, r, w):
        n = in_.shape[-1]
        tmp = self.u("rs_tmp")
        self.ts("dve", out, in_, scale, EPS, ALU.mult, ALU.add, r, [tmp] + list(w))
        self.tt("pool", out, out, self.mhalf[:, 0:n], ALU.pow, [tmp] + list(w) + ["const"], w)

    def build(self):
        nc = self.nc
        NT = self.NT
        dbg = "ExternalOutput" if self.debug else "Internal"

        def din(name, shape, dt=F32):
            return nc.dram_tensor(name, list(shape), dt, kind="ExternalInput").ap()

        def dscr(name, shape, dt):
            return nc.dram_tensor(name, list(shape), dt, kind=dbg).ap()

        self.x = din("x", [NT, D])
        self.p = din("p", [NT, PLE])
        self.y = nc.dram_tensor("y", [NT, D], F32, kind="ExternalOutput").ap()
        self.w_in = din("w_in", [D, NCOL])
        self.w_out = din("w_out", [D, D])
        self.w_up = din("w_up", [D, 2 * DFF])
        self.w_down = din("w_down", [DFF, D])
        self.ple_proj = din("ple_proj", [PLE, D])
        self.w_ple_gate = din("w_ple_gate", [D, D])
        self.ln1_g = din("ln1_g", [128, 8])
        self.ln2_g = din("ln2_g", [128, 8])
        self.ple_norm_g = din("ple_norm_g", [1, D])
        self.ple_gate_norm_g = din("ple_gate_norm_g", [128, 8])
        self.cw_l = din("cw_l", [128, 60])
        self.dn_a_log = din("dn_a_log", [1, 8])
        self.dn_dt_bias = din("dn_dt_bias", [1, 8])
        self.dn_norm_g = din("dn_norm_g", [1, 128])
        self.da_qk_norm_g = din("da_qk_norm_g", [1, 128])
        self.da_lambda = din("da_lambda", [1, 256])
        self.da_subln_g = din("da_subln_g", [1, 128])
        self.fw_l = din("fw_l", [128, 66])
        self.fb_l = din("fb_l", [128, 22])
        self.rope_cs = din("rope_cs", [128, 32 * 16])

        self.d_kn = dscr("d_kn", [NT, 512], BF16)
        self.d_v = dscr("d_v", [NT, 512], BF16)
        self.d_kT = dscr("d_kT", [512, NT], BF16)
        self.d_qT = dscr("d_qT", [512, NT], BF16)
        self.d_dz = dscr("d_dz", [NT, 512], BF16)
        self.d_qaT = dscr("d_qaT", [512, NT], BF16)
        self.d_kaT = dscr("d_kaT", [512, NT], BF16)
        self.d_va = dscr("d_va", [NT, 512], BF16)
        self.d_ob = dscr("d_ob", [NT, 512], F32)
        self.d_mix = dscr("d_mix", [NT, D], BF16)
        self.d_x1 = dscr("d_x1", [NT, D], F32)
        self.d_hnT = dscr("d_hnT", [D, NT], BF16)
        self.d_gate = dscr("d_gate", [NT, 16], F32)

        with contextlib.ExitStack() as es:
            WORDS = 51 * 1024
            arena_t = es.enter_context(nc.sbuf_tensor("arena", [128, WORDS], F32))
            self.A = Arena(arena_t[:], WORDS)
            self.psall = es.enter_context(nc.psum_tensor("psall", [128, 4096], F32))[:]
            self.ps = [self.psall[:, i * 512:(i + 1) * 512] for i in range(8)]
            self.psb = [p.bitcast(BF16) for p in self.ps]
            self.setup_consts()
            base = self.A.off
            toff = 0
            gtile = 0
            self.phaseA_weights()
            for S_ in self.seqs:
                self.phaseA_seq(toff, S_, gtile)
                toff += S_
                gtile += S_ // 128
            self.gates_finish()
            self.S.barrier()
            self.A.off = base
            self.phaseB1()
            self.S.barrier()
            self.A.off = base
            self.phaseB2()
            self.S.barrier()
            self.A.off = base
            self.phaseC1()
            self.S.barrier()
            self.A.off = base
            self.phaseC2()
            self.S.barrier()
            self.A.off = base
            self.phaseC3()
            self.S.emit()
        return nc

    def setup_consts(self):
        A = self.A
        ntl = self.NT // 128
        self.identf = A.f32(128)
        self.ident = A.bf16(128)
        self.mhalf = A.f32(16)
        self.UI = A.f32(128)
        self.US = A.f32(128)
        self.LI = A.f32(128)
        self.LS = A.f32(128)
        self.BD = A.f32(128)
        self.onesA = A.f32(128)
        self.onesB = A.f32(128)
        self.rope = A.f32(32 * 16)
        self.dtb = A.f32(8)
        self.nega = A.f32(8)
        self.gqk = A.f32(128)
        self.dng = A.f32(128)
        self.subg = A.f32(128)
        self.lamt = A.f32(256)
        self.lam = A.f32(4)
        self.cw = A.f32(12 * 5)
        self.fw = A.f32(22 * 3)
        self.fb = A.f32(22)
        self.g1 = A.f32(8)
        self.g2 = A.f32(8)
        self.g3 = A.f32(8)
        self.gx = A.f32(ntl * 8)
        self.gb = A.f32(ntl * 8)
        C = ["const"]
        pool = "pool"
        ms = self.memset
        S = self.S
        ms(pool, self.identf, 0.0, C)
        S.add(pool, lambda e: e.affine_select(out=self.identf, in_=self.identf, pattern=[[-1, 128]], compare_op=ALU.not_equal,
                                              fill=1.0, base=0, channel_multiplier=1), C, C)
        self.cp("dve", self.ident, self.identf, C, C)
        ms(pool, self.mhalf, -0.5, C)
        ms(pool, self.BD, 0.0, C)
        ms(pool, self.BD[0:64, 0:64], 1.0, C)
        ms(pool, self.BD[64:128, 64:128], 1.0, C)
        ms(pool, self.onesA, 0.0, C)
        ms(pool, self.onesA[0:64, :], 1.0, C)
        ms(pool, self.onesB, 0.0, C)
        ms(pool, self.onesB[64:128, :], 1.0, C)
        for m, pat, cm, base_, op in ((self.UI, 1, -1, 0, ALU.is_ge), (self.US, 1, -1, 0, ALU.is_gt),
                                      (self.LI, -1, 1, 0, ALU.is_ge), (self.LS, -1, 1, 0, ALU.is_gt)):
            S.add(pool, lambda e, m=m, pat=pat, cm=cm, base_=base_, op=op: e.affine_select(
                out=m, in_=self.BD, pattern=[[pat, 128]], compare_op=op, fill=0.0, base=base_, channel_multiplier=cm), C, C)
        d = self.dma
        d(self.rope, self.rope_cs[:, :], (), C)
        d(self.dtb, self.dn_dt_bias[0:1, :].partition_broadcast(128), (), C)
        d(self.nega, self.dn_a_log[0:1, :].partition_broadcast(128), (), C)
        d(self.gqk, self.da_qk_norm_g[0:1, :].partition_broadcast(128), (), C)
        d(self.dng, self.dn_norm_g[0:1, :].partition_broadcast(128), (), C)
        d(self.subg, self.da_subln_g[0:1, :].partition_broadcast(128), (), C)
        d(self.lamt, self.da_lambda[0:1, :].partition_broadcast(128), (), C)
        d(self.cw, self.cw_l[:, :], (), C)
        d(self.fw, self.fw_l[:, :], (), C)
        d(self.fb, self.fb_l[:, :], (), C)
        d(self.g1, self.ln1_g[:, :], (), C)
        d(self.g2, self.ln2_g[:, :], (), C)
        d(self.g3, self.ple_gate_norm_g[:, :], (), C)
        self.act(self.nega, self.nega, AF.Exp, C, C)
        self.ts("dve", self.nega, self.nega, -1.0, None, ALU.mult, None, C, C)
        self.ts("dve", self.gqk[:, 0:64], self.gqk[:, 0:64], 0.125, None, ALU.mult, None, C, C)
        self.ts("dve", self.subg, self.subg, 1.0 - LAM_INIT, None, ALU.mult, None, C, C)
        lt = self.lamt.rearrange("p (a d) -> p a d", d=64)
        self.tt("dve", lt[:, 0, :], lt[:, 0, :], lt[:, 1, :], ALU.mult, C, C)
        self.tt("dve", lt[:, 2, :], lt[:, 2, :], lt[:, 3, :], ALU.mult, C, C)
        self.red(self.lam[:, 0:1], lt[:, 0, :], C, C)
        self.red(self.lam[:, 1:2], lt[:, 2, :], C, C)
        self.act(self.lam[:, 0:2], self.lam[:, 0:2], AF.Exp, C, C)
        self.tt("dve", self.lam[:, 2:3], self.lam[:, 0:1], self.lam[:, 1:2], ALU.subtract, C, C)
        self.ts("dve", self.lam[:, 3:4], self.lam[:, 2:3], LAM_INIT, -1.0, ALU.add, ALU.mult, C, C)

    def load_weight_bf(self, dst, src, kc_n, ncols, gain, tagw, stage):
        CH = stage[0].shape[-1]
        i = 0
        for kc in range(kc_n):
            for c0 in range(0, ncols, CH):
                c1 = min(ncols, c0 + CH)
                st = stage[i % 2]
                tk = "wstage%d" % (i % 2)
                i += 1
                self.dma(st[:, 0:c1 - c0], src[kc * 128:(kc + 1) * 128, c0:c1], (), [tk])
                eng = "dve" if (i % 2) else "act"
                if gain is None:
                    self.cp(eng, dst[:, kc, c0:c1], st[:, 0:c1 - c0], [tk], [tagw])
                elif eng == "act":
                    self.act(dst[:, kc, c0:c1], st[:, 0:c1 - c0], AF.Identity, [tk, "const"], [tagw], scale=gain[:, kc:kc + 1])
                else:
                    self.ts(eng, dst[:, kc, c0:c1], st[:, 0:c1 - c0], gain[:, kc:kc + 1], None, ALU.mult, None, [tk, "const"], [tagw])

    def phaseA_weights(self):
        A = self.A
        self.win = A.bf16(8 * NCOL).rearrange("p (k n) -> p k n", k=8)
        mark = A.off
        self.stageA = [A.f32(1800), A.f32(1800)]
        self.load_weight_bf(self.win, self.w_in, 8, NCOL, self.g1, "win", self.stageA)
        self.S.barrier()
        A.off = mark
        Smax = max(self.seqs)
        self.XW = Smax + 4
        self.xnT = A.bf16(8 * self.XW).rearrange("p (k n) -> p k n", k=8)
        self.xin = [A.f32(1024), A.f32(1024)]
        self.xnb = [A.bf16(1024), A.bf16(1024)]
        self.junk = A.bf16(1024)
        self.sm = A.f32(80)
        self.tmA = A.f32(1024)
        self.tmB = A.f32(512)
        self.qkb = A.bf16(1024)
        self.trs = A.bf16(1024)
        self.dzb = A.bf16(512)
        self.avb = A.bf16(512)
        self.presb = [A.f32(516), A.f32(516)]
        self.accb = [A.f32(512), A.f32(512)]
        self.fmb = [A.bf16(512), A.bf16(512)]
        self.sq2 = A.f32(1024)
        self.trs2 = [A.bf16(512), A.bf16(512)]
        self.tok = A.bf16(4 * 1536).rearrange("p (t c) -> p t c", t=4)
        self.nrmb = [A.bf16(1024), A.bf16(1024)]
        self.rq8 = [A.f32(8), A.f32(8)]
        self.sq = A.f32(1024)

    def phaseA_seq(self, T0, S_, gt0):
        ps, psb = self.ps, self.psb
        nt = S_ // 128
        xnT = self.xnT
        XT = lambda j: ("xnT", j)
        prog = {"a0": 0}
        self.memset("pool", xnT[:, :, 0:2], 0.0, ["xnTpadL"])
        self.memset("pool", xnT[:, :, 2 + S_:2 + S_ + 2], 0.0, ["xnTpadR"] + [XT(j) for j in range(nt, nt + 1)])

        def a0_a(i):
            b = i % 2
            xin, xnb = self.xin[b], self.xnb[b]
            tx, tn = "xin%d" % b, "xnb%d" % b
            self.dma(xin, self.x[T0 + i * 128:T0 + (i + 1) * 128, :], (), [tx], q="act")
            ss = self.sm[:, b:b + 1]
            tss = "ssA%d" % b
            self.act(self.junk, xin, AF.Square, [tx], ["junk", tss], accum_out=ss)
            rs = self.sm[:, 2 + b:3 + b]
            trs = "rsA%d" % b
            self.rsqrt_small(rs, ss, 1.0 / D, [tss], [trs])
            self.ts("dve", xnb, xin, rs, None, ALU.mult, None, [tx, trs], [tn])

        def a0_b(i):
            b = i % 2
            xnb, tn = self.xnb[b], "xnb%d" % b
            for kc in range(8):
                self.tr(psb[0][:, kc * 128:(kc + 1) * 128], xnb[:, kc * 128:(kc + 1) * 128], self.ident, [tn, "const"], ["ps0"])
            self.cp("act", xnT[:, :, 2 + i * 128:2 + (i + 1) * 128], psb[0][:, 0:1024].rearrange("p (k n) -> p k n", k=8),
                    ["ps0"], [XT(i)])
            prog["a0"] = i + 1

        def gen_a0():
            a0_a(0)
            yield
            for i in range(nt):
                if i + 1 < nt:
                    a0_a(i + 1)
                a0_b(i)
                yield

        def tm_s1(i):
            gt = gt0 + i
            cols = slice(2 + i * 128, 2 + (i + 1) * 128)
            rows = slice(T0 + i * 128, T0 + (i + 1) * 128)

            def proj(c0, n, out, tok):
                for kc in range(8):
                    self.mm(out, xnT[:, kc, cols], self.win[:, kc, c0:c0 + n], kc == 0, kc == 7, [XT(i), "win"], [tok])
            proj(2064, 512, ps[2][:, 0:512], "ps2")
            proj(2576, 512, ps[3][:, 0:512], "ps3")
            proj(2048, 16, ps[5][:, 0:16], "ps5")
            self.tt("dve", self.gx[:, gt * 8:(gt + 1) * 8], ps[5][:, 0:8], self.dtb, ALU.add, ["ps5", "const"], ["gx"])
            self.cp("dve", self.gb[:, gt * 8:(gt + 1) * 8], ps[5][:, 8:16], ["ps5"], ["gb"])
            proj(1536, 512, ps[1][:, 0:512], "ps1")
            self.cp("act", self.dzb, ps[1][:, 0:512], ["ps1"], ["dzb"])
            self.dma(self.d_dz[rows, :], self.dzb, ["dzb"], [])
            proj(3088, 512, ps[1][:, 0:512], "ps1")
            self.cp("act", self.avb, ps[1][:, 0:512], ["ps1"], ["avb"])
            self.dma(self.d_va[rows, :], self.avb, ["avb"], [])

        def tm_s2(i):
            tmA = self.tmA
            self.act(self.sq[:, 0:512], ps[2][:, 0:512], AF.Square, ["ps2"], ["sq"])
            self.act(self.sq[:, 512:1024], ps[3][:, 0:512], AF.Square, ["ps3"], ["sq"])
            ssq = self.sm[:, 8:24]
            self.red(ssq, self.sq.rearrange("p (g d) -> p g d", d=64), ["sq"], ["ssq"])
            rq = self.sm[:, 24:40]
            self.rsqrt_small(rq, ssq, 1.0 / 64, ["ssq"], ["rq"])
            for half, bk in ((0, 2), (1, 3)):
                o3 = tmA[:, half * 512:(half + 1) * 512].rearrange("p (g d) -> p g d", d=64)
                self.tt("dve", o3, ps[bk][:, 0:512].rearrange("p (g d) -> p g d", d=64),
                        rq[:, half * 8:(half + 1) * 8].unsqueeze(2).to_broadcast([128, 8, 64]), ALU.mult, ["ps%d" % bk, "rq"], ["tmA%d" % half])
                self.tt("pool", o3, o3, self.gqk[:, half * 64:(half + 1) * 64].unsqueeze(1).to_broadcast([128, 8, 64]), ALU.mult,
                        ["tmA%d" % half, "const"], ["tmA%d" % half])

        def tm_s3(i):
            rows = slice(T0 + i * 128, T0 + (i + 1) * 128)
            tmA = self.tmA
            a3 = tmA.rearrange("p (g d) -> p g d", d=64)
            cosb = self.rope[:, i * 16:i * 16 + 8].unsqueeze(1).to_broadcast([128, 16, 8])
            sinb = self.rope[:, i * 16 + 8:i * 16 + 16].unsqueeze(1).to_broadcast([128, 16, 8])
            tB = self.tmB
            t1 = tB[:, 0:128].rearrange("p (g d) -> p g d", d=8)
            t2 = tB[:, 128:256].rearrange("p (g d) -> p g d", d=8)
            t3 = tB[:, 256:384].rearrange("p (g d) -> p g d", d=8)
            t4 = tB[:, 384:512].rearrange("p (g d) -> p g d", d=8)
            x1, x2 = a3[:, :, 0:8], a3[:, :, 8:16]
            tA = ["tmA0", "tmA1"]
            self.tt("dve", t1, x1, cosb, ALU.mult, tA + ["const"], ["tB1"])
            self.tt("dve", t2, x2, sinb, ALU.mult, tA + ["const"], ["tB2"])
            self.tt("dve", t3, x2, cosb, ALU.mult, tA + ["const"], ["tB3"])
            self.tt("dve", t4, x1, sinb, ALU.mult, tA + ["const"], ["tB4"])
            self.cp("act", self.qkb, tmA, tA, ["qkb"])
            q3 = self.qkb.rearrange("p (g d) -> p g d", d=64)
            self.tt("dve", q3[:, :, 0:8], t1, t2, ALU.subtract, ["tB1", "tB2", "qkb"], ["qkb"])
            self.tt("dve", q3[:, :, 8:16], t3, t4, ALU.add, ["tB3", "tB4", "qkb"], ["qkb"])
            for j in range(8):
                self.tr(psb[4][:, j * 128:(j + 1) * 128], self.qkb[:, j * 128:(j + 1) * 128], self.ident, ["qkb", "const"], ["ps4"])
            self.cp("act", self.trs, psb[4][:, 0:1024], ["ps4"], ["trs"])
            t3d = self.trs.rearrange("p (j n) -> p j n", j=8)
            self.dma(self.d_qaT[:, rows].rearrange("(h p) n -> p h n", p=128), t3d[:, 0:4, :], ["trs"], [])
            self.dma(self.d_kaT[:, rows].rearrange("(h p) n -> p h n", p=128), t3d[:, 4:8, :], ["trs"], [])

        def gen_tm():
            while prog["a0"] < 1:
                yield
            tm_s1(0)
            yield
            for i in range(nt):
                tm_s2(i)
                yield
                if i + 1 < nt:
                    while prog["a0"] < i + 2:
                        yield
                    tm_s1(i + 1)
                    yield
                tm_s3(i)
                yield

        def gen_fm():
            BL = min(512, S_)
            ntb = BL // 128
            for blk in range(S_ // BL):
                t0 = blk * BL
                need = min(nt, blk * ntb + ntb + 1)
                while prog["a0"] < need:
                    yield
                rd = [XT(j) for j in range(max(0, blk * ntb - 1), min(nt, blk * ntb + ntb + 1))] + ["xnTpadL", "xnTpadR", "win"]

                def fm_mm(c):
                    bk = 6 + c % 2
                    for kc in range(8):
                        self.mm(ps[bk][:, 0:BL], self.win[:, kc, c * 128:(c + 1) * 128], xnT[:, kc, 2 + t0:2 + t0 + BL], kc == 0, kc == 7,
                                rd, ["ps%d" % bk])
                    for kc in range(8):
                        self.mm(ps[5][:, 16:18], self.win[:, kc, c * 128:(c + 1) * 128], xnT[:, kc, t0:t0 + 2], kc == 0, kc == 7, rd, ["ps5"])
                    for kc in range(8):
                        self.mm(ps[5][:, 18:20], self.win[:, kc, c * 128:(c + 1) * 128], xnT[:, kc, t0 + BL + 2:t0 + BL + 4], kc == 0, kc == 7,
                                rd, ["ps5"])
                    pre = self.presb[c % 2]
                    tpre = "pre%d" % (c % 2)
                    self.cp("act", pre[:, 2:2 + BL], ps[bk][:, 0:BL], ["ps%d" % bk], [tpre])
                    self.act(self.accb[c % 2][:, 0:BL], ps[bk][:, 0:BL], AF.Identity, ["ps%d" % bk, "const"], ["acc%d" % (c % 2)],
                             scale=self.cw[:, c * 5 + 2:c * 5 + 3])
                    self.cp("dve", pre[:, 0:2], ps[5][:, 16:18], ["ps5"], [tpre])
                    self.cp("dve", pre[:, BL + 2:BL + 4], ps[5][:, 18:20], ["ps5"], [tpre])

                fm_mm(0)
                yield
                for c in range(12):
                    if c + 1 < 12:
                        fm_mm(c + 1)
                    pre, acc, fm = self.presb[c % 2], self.accb[c % 2], self.fmb[c % 2]
                    tpre, tacc, tfm = "pre%d" % (c % 2), "acc%d" % (c % 2), "fm%d" % (c % 2)
                    a_ = acc[:, 0:BL]
                    for j in (0, 1, 3, 4):
                        self.stt(a_, pre[:, j:j + BL], self.cw[:, c * 5 + j:c * 5 + j + 1], a_, ALU.mult, ALU.add, [tpre, tacc, "const"], [tacc])
                    self.act(fm[:, 0:BL], a_, AF.Silu, [tacc], [tfm])
                    yield
                    for t in range(ntb):
                        self.tr(psb[4][:, t * 128:(t + 1) * 128], fm[:, t * 128:(t + 1) * 128], self.ident, [tfm, "const"], ["ps4"])
                    self.cp("act", self.tok[:, 0:ntb, c * 128:(c + 1) * 128],
                            psb[4][:, 0:ntb * 128].rearrange("p (t n) -> p t n", t=ntb), ["ps4"], ["tok"])
                    yield

                def l2_a(t):
                    tk = self.tok[:, t, :]
                    nb_ = t % 2
                    nrm = self.nrmb[nb_]
                    self.act(self.sq2, tk[:, 0:1024], AF.Square, ["tok"], ["sq2"])
                    ssq = self.sm[:, 40 + nb_ * 8:48 + nb_ * 8]
                    self.red(ssq, self.sq2.rearrange("p (g d) -> p g d", d=128), ["sq2"], ["ssq2%d" % nb_])
                    rq = self.sm[:, 56 + nb_ * 4:60 + nb_ * 4]
                    rk = self.sm[:, 24 + 40:28 + 40] if False else None
                    rq8 = self.rq8[nb_]
                    self.ts("dve", rq8, ssq, EPS, None, ALU.add, None, ["ssq2%d" % nb_], ["rq2a%d" % nb_])
                    self.tt("pool", rq8, rq8, self.mhalf[:, 0:8], ALU.pow, ["rq2a%d" % nb_, "const"], ["rq2%d" % nb_])
                    self.ts("dve", rq8[:, 0:4], rq8[:, 0:4], 128.0 ** -0.5, None, ALU.mult, None, ["rq2%d" % nb_], ["rq2%d" % nb_])
                    self.tt("dve", nrm.rearrange("p (g d) -> p g d", d=128), tk[:, 0:1024].rearrange("p (g d) -> p g d", d=128),
                            rq8.unsqueeze(2).to_broadcast([128, 8, 128]), ALU.mult, ["tok", "rq2%d" % nb_], ["nrm%d" % nb_])

                def l2_b(t):
                    rows = slice(T0 + t0 + t * 128, T0 + t0 + (t + 1) * 128)
                    tk = self.tok[:, t, :]
                    nb_ = t % 2
                    nrm = self.nrmb[nb_]
                    tn_ = "nrm%d" % nb_
                    self.dma(self.d_kn[rows, :], nrm[:, 512:1024], [tn_], [])
                    self.dma(self.d_v[rows, :], tk[:, 1024:1536], ["tok"], [])
                    for half, dst in ((0, self.d_qT), (1, self.d_kT)):
                        for j in range(4):
                            self.tr(psb[4][:, j * 128:(j + 1) * 128], nrm[:, half * 512 + j * 128:half * 512 + (j + 1) * 128],
                                    self.ident, [tn_, "const"], ["ps4"])
                        tr2 = self.trs2[half]
                        self.cp("act", tr2, psb[4][:, 0:512], ["ps4"], ["trs2%d" % half])
                        self.dma(dst[:, rows].rearrange("(h p) n -> p h n", p=128), tr2.rearrange("p (j n) -> p j n", j=4), ["trs2%d" % half], [])

                l2_a(0)
                yield
                for t in range(ntb):
                    if t + 1 < ntb:
                        l2_a(t + 1)
                    l2_b(t)
                    yield

        g0, g1, g2 = gen_a0(), gen_tm(), gen_fm()
        alive = {0: g0, 1: g1, 2: g2}
        plan = (0, 2, 2, 1, 2, 2, 1, 2, 2, 1, 2, 2)
        while alive:
            for k in plan:
                g = alive.get(k)
                if g is None:
                    continue
                try:
                    next(g)
                except StopIteration:
                    del alive[k]

    def gates_finish(self):
        ntl = self.NT // 128
        gx3 = self.gx.rearrange("p (t c) -> p t c", c=8)
        self.act(self.gx, self.gx, AF.Exp, ["gx"], ["gx"])
        self.act(self.gx, self.gx, AF.Ln, ["gx"], ["gx"], bias=1.0)
        self.tt("dve", gx3, gx3, self.nega.unsqueeze(1).to_broadcast([128, ntl, 8]), ALU.mult, ["gx", "const"], ["gx"])
        self.act(self.gb, self.gb, AF.Tanh, ["gb"], ["gb"], scale=0.5)
        self.ts("dve", self.gb, self.gb, 0.5, 0.5, ALU.mult, ALU.add, ["gb"], ["gb"])
        if self.debug:
            for t in range(ntl):
                self.dma(self.d_gate[t * 128:(t + 1) * 128, 0:8], self.gx[:, t * 8:(t + 1) * 8], ["gx"], [])
                self.dma(self.d_gate[t * 128:(t + 1) * 128, 8:16], self.gb[:, t * 8:(t + 1) * 8], ["gb"], [])

    def phaseB1(self):
        if self.debug == "nodn":
            dn_in = self.nc.dram_tensor("dn_in", [self.NT, 512], BF16, kind="ExternalInput").ap()
            for i in range(self.NT // 128):
                self.dma(self.d_mix[i * 128:(i + 1) * 128, 0:512], dn_in[i * 128:(i + 1) * 128, :], (), [])
            return
        A = self.A
        self.d_of = self.nc.dram_tensor("d_of", [self.NT, 512], F32, kind="Internal").ap()
        v3 = lambda a: a.rearrange("p (h d) -> p h d", h=4)
        identbc = self.identf.unsqueeze(1).to_broadcast([128, 4, 128])

        def mkbufs(sid):
            B = {}
            for nm in ("kn0", "kn1", "vt0", "vt1", "kT0", "kT1", "qT0", "qT1", "kb", "kbT", "kbg", "vb", "kdec", "P0", "P1", "Q0", "Q1",
                       "Rb", "qkT", "wT", "vnew", "Sb"):
                B[nm] = A.bf16(512)
            for nm in ("Rg", "E", "Es", "Xf", "Rf", "usb", "tmp", "ot0", "ot1", "Sf"):
                B[nm] = A.f32(512)
            B["gsm"] = A.f32(24)
            B["e20"] = A.f32(24)
            B["bg"] = A.f32(8)
            return B

        def dn_pass(sid, T0, S_, gt0, dr, B, bk):
            ps = [self.ps[b] for b in bk]
            psb = [self.psb[b] for b in bk]
            pt = ["ps%d" % b for b in bk]
            T = lambda nm: "%s_%d" % (nm, sid)
            nt = S_ // 128
            if dr == 0:
                CUM, ARGL, RMASK, MINCL, MSTR = self.UI, self.LS, self.UI, self.UI, self.US
                tiles = range(nt)
                chunks = (0, 64)
            else:
                CUM, ARGL, RMASK, MINCL, MSTR = self.LI, self.US, self.LI, self.LI, self.LS
                tiles = range(nt - 1, -1, -1)
                chunks = (64, 0)
            Sf, Sb = B["Sf"], B["Sb"]
            self.memset("pool", Sf, 0.0, [T("Sf")])
            self.memset("pool", Sb, 0.0, [T("Sb")])
            cnt = 0
            for i in tiles:
                b = cnt % 2
                cnt += 1
                gt = gt0 + i
                rows = slice(T0 + i * 128, T0 + (i + 1) * 128)
                kn_t, v_t, kT_t, qT_t = B["kn%d" % b], B["vt%d" % b], B["kT%d" % b], B["qT%d" % b]
                tkn, tv, tkT, tqT = T("kn%d" % b), T("vt%d" % b), T("kT%d" % b), T("qT%d" % b)
                self.dma(kn_t, self.d_kn[rows, :], (), [tkn], q="act")
                self.dma(v_t, self.d_v[rows, :], (), [tv], q="act")
                self.dma(v3(kT_t), self.d_kT[:, rows].rearrange("(h p) n -> p h n", p=128), (), [tkT], q="act")
                self.dma(v3(qT_t), self.d_qT[:, rows].rearrange("(h p) n -> p h n", p=128), (), [tqT], q="act")
                g4 = self.gx[:, gt * 8 + dr * 4:gt * 8 + dr * 4 + 4]
                be4 = self.gb[:, gt * 8 + dr * 4:gt * 8 + dr * 4 + 4]
                bch = lambda a, n=128: a.unsqueeze(2).to_broadcast([n, 4, 128])
                gsm, e20, bg = B["gsm"], B["e20"], B["bg"]
                kb, kbT, kbg, vb, kdec = B["kb"], B["kbT"], B["kbg"], B["vb"], B["kdec"]
                Rg, E, Es, Xf, Rf, Rb, qkT = B["Rg"], B["E"], B["Es"], B["Xf"], B["Rf"], B["Rb"], B["qkT"]
                usb, wT, vnew, tmp = B["usb"], B["wT"], B["vnew"], B["tmp"]
                self.mm(ps[3][:, 0:4], CUM, g4, True, True, ["const", "gx"], [pt[3]])
                self.mm(ps[3][:, 4:8], self.BD, g4, True, True, ["const", "gx"], [pt[3]])
                self.mm(ps[3][:, 8:12], self.onesA, g4, True, True, ["const", "gx"], [pt[3]])
                self.mm(ps[3][:, 12:16], self.onesB, g4, True, True, ["const", "gx"], [pt[3]])
                self.cp("dve", gsm[:, 0:16], ps[3][:, 0:16], [pt[3]], [T("gsm")])
                self.tt("dve", gsm[:, 16:20], gsm[:, 4:8], gsm[:, 0:4], ALU.subtract, [T("gsm")], [T("gsm")])
                self.act(e20[:, 0:20], gsm[:, 0:20], AF.Exp, [T("gsm")], [T("e20")])
                egc, etA, etB, erest = e20[:, 0:4], e20[:, 8:12], e20[:, 12:16], e20[:, 16:20]
                self.tt("dve", bg[:, 0:4], be4, egc, ALU.mult, ["gb", T("e20")], [T("bg")])
                yield
                self.tt("dve", v3(kb), v3(kn_t), bch(be4), ALU.mult, [tkn, "gb"], [T("kb")])
                for h in range(4):
                    self.tr(psb[3][:, 512 + h * 128:512 + (h + 1) * 128], kb[:, h * 128:(h + 1) * 128], self.ident, [T("kb"), "const"], [pt[3]])
                self.cp("act", kbT, psb[3][:, 512:1024], [pt[3]], [T("kbT")])
                self.tt("dve", v3(kbg), v3(kn_t), bch(bg[:, 0:4]), ALU.mult, [tkn, T("bg")], [T("kbg")])
                self.tt("pool", v3(vb), v3(v_t), bch(be4), ALU.mult, [tv, "gb"], [T("vb")])
                self.tt("pool", v3(kdec), v3(kn_t), bch(erest), ALU.mult, [tkn, T("e20")], [T("kdec")])
                self.tt("pool", v3(Rg), RMASK.unsqueeze(1).to_broadcast([128, 4, 128]), bch(g4), ALU.mult, ["const", "gx"], [T("Rg")])
                self.mm(ps[2][:, 0:512], ARGL, Rg, True, True, ["const", T("Rg")], [pt[2]])
                for h in range(4):
                    hs = slice(h * 128, (h + 1) * 128)
                    self.mm(ps[0][:, hs], kT_t[:, hs], kbT[:, hs], True, True, [tkT, T("kbT")], [pt[0]])
                    self.mm(ps[1][:, hs], kT_t[:, hs], qT_t[:, hs], True, True, [tkT, tqT], [pt[1]])
                self.act(E, ps[2][:, 0:512], AF.Exp, [pt[2]], [T("E")])
                self.tt("pool", v3(E), v3(E), MINCL.unsqueeze(1).to_broadcast([128, 4, 128]), ALU.mult, [T("E"), "const"], [T("E")])
                self.tt("pool", v3(Es), v3(E), MSTR.unsqueeze(1).to_broadcast([128, 4, 128]), ALU.mult, [T("E"), "const"], [T("Es")])
                self.tt("dve", Xf, ps[0][:, 0:512], Es, ALU.mult, [pt[0], T("Es")], [T("Xf")])
                P, Q = B["P0"], B["Q0"]
                self.cp("act", P, Xf, [T("Xf")], [T("P0")])
                self.tt("pool", v3(Rf), identbc, v3(Xf), ALU.subtract, ["const", T("Xf")], [T("Rf")])
                self.cp("act", Rb, Rf, [T("Rf")], [T("Rb")])
                self.tt("dve", qkT, ps[1][:, 0:512], E, ALU.mult, [pt[1], T("E")], [T("qkT")])
                for h in range(4):
                    self.tr(psb[3][:, 512 + h * 128:512 + (h + 1) * 128], P[:, h * 128:(h + 1) * 128], self.ident, [T("P0"), "const"], [pt[3]])
                self.cp("act", Q, psb[3][:, 512:1024], [pt[3]], [T("Q0")])
                yield
                pi = 0
                for k in range(1, 6):
                    tP, tQ = T("P%d" % pi), T("Q%d" % pi)
                    pn = 1 - pi
                    tPn, tQn = T("P%d" % pn), T("Q%d" % pn)
                    Pn, Qn = B["P%d" % pn], B["Q%d" % pn]
                    for h in range(4):
                        hs = slice(h * 128, (h + 1) * 128)
                        self.mm(ps[1][:, hs], P[:, hs], Q[:, hs], True, True, [tP, tQ], [pt[1]])
                    if k < 5:
                        for h in range(4):
                            hs = slice(h * 128, (h + 1) * 128)
                            self.mm(ps[0][:, hs], Q[:, hs], P[:, hs], True, True, [tP, tQ], [pt[0]])
                    self.cp("dve", Qn, ps[1][:, 0:512], [pt[1]], [tQn])
                    if k < 5:
                        self.cp("act", Pn, ps[0][:, 0:512], [pt[0]], [tPn])
                    for h in range(4):
                        hs = slice(h * 128, (h + 1) * 128)
                        self.mm(ps[2][:, hs], Qn[:, hs], Rb[:, hs], True, True, [tQn, T("Rb")], [pt[2]])
                    self.tt("dve", Rf, Rf, ps[2][:, 0:512], ALU.add, [T("Rf"), pt[2]], [T("Rf")])
                    self.cp("act", Rb, Rf, [T("Rf")], [T("Rb")])
                    P, Q, pi = Pn, Qn, pn
                    yield
                for h in range(4):
                    hs = slice(h * 128, (h + 1) * 128)
                    self.mm(ps[2][:, hs], Rb[:, hs], vb[:, hs], True, True, [T("Rb"), T("vb")], [pt[2]])
                    self.mm(ps[0][:, hs], kbg[:, hs], Rb[:, hs], True, True, [T("Rb"), T("kbg")], [pt[0]])
                self.cp("dve", usb, ps[2][:, 0:512], [pt[2]], [T("usb")])
                self.cp("act", wT, ps[0][:, 0:512], [pt[0]], [T("wT")])
                yield
                ob_ = B["ot%d" % b]
                tot = T("ot%d" % b)
                for r0 in chunks:
                    rs = slice(r0, r0 + 64)
                    etc = etA if r0 == 0 else etB
                    for h in range(4):
                        hs = slice(h * 128, (h + 1) * 128)
                        self.mm(ps[1][rs, hs], wT[:, h * 128 + r0:h * 128 + r0 + 64], Sb[:, hs], True, True, [T("wT"), T("Sb")], [pt[1]])
                    for h in range(4):
                        hs = slice(h * 128, (h + 1) * 128)
                        self.mm(ps[2][rs, hs], qT_t[:, h * 128 + r0:h * 128 + r0 + 64], Sb[:, hs], True, True, [tqT, T("Sb")], [pt[2]])
                    self.tt("dve", vnew[rs, :], usb[rs, :], ps[1][rs, 0:512], ALU.subtract, [T("usb"), pt[1]], [T("vnew")])
                    for h in range(4):
                        hs = slice(h * 128, (h + 1) * 128)
                        self.mm(ps[0][:, hs], kdec[rs, hs], vnew[rs, hs], True, True, [T("kdec"), T("vnew")], [pt[0]])
                    for h in range(4):
                        hs = slice(h * 128, (h + 1) * 128)
                        self.mm(ps[3][rs, hs], qkT[rs, h * 128 + r0:h * 128 + r0 + 64], vnew[rs, hs], True, True, [T("qkT"), T("vnew")], [pt[3]])
                    for h in range(4):
                        hs = slice(h * 128, (h + 1) * 128)
                        self.stt(Sf[:, hs], Sf[:, hs], etc[:, h:h + 1], ps[0][:, hs], ALU.mult, ALU.add, [T("Sf"), T("e20"), pt[0]], [T("Sf")])
                    self.cp("act", Sb, Sf, [T("Sf")], [T("Sb")])
                    self.tt("dve", v3(tmp)[rs], v3(ps[2][:, 0:512])[rs], bch(egc[rs, :], 64), ALU.mult, [pt[2], T("e20")], [T("tmpB")])
                    self.tt("dve", ob_[rs, :], tmp[rs, :], ps[3][rs, 0:512], ALU.add, [T("tmpB"), pt[3]], [tot])
                    yield
                dst = self.d_ob if dr == 1 else self.d_of
                self.dma(dst[rows, :], ob_, [tot], [("od%d" % dr, gt)])

        base = A.off
        BS = [mkbufs(k) for k in range(4)]
        offs = []
        T0 = 0
        gt0 = 0
        for S_ in self.seqs:
            offs.append((T0, S_, gt0))
            T0 += S_
            gt0 += S_ // 128
        groups = []
        i = 0
        while i < len(offs):
            if i + 1 < len(offs) and offs[i][1] == offs[i + 1][1]:
                groups.append([offs[i], offs[i + 1]])
                i += 2
            else:
                groups.append([offs[i]])
                i += 1
        for grp in groups:
            gens = []
            for k, (T0_, S_, gt0_) in enumerate(grp):
                gens.append(dn_pass(2 * k, T0_, S_, gt0_, 1, BS[2 * k], (0, 1, 2, 3)))
                gens.append(dn_pass(2 * k + 1, T0_, S_, gt0_, 0, BS[2 * k + 1], (4, 5, 6, 7)))
            for k, g in enumerate(gens):
                for _ in range((len(gens) - 1 - k) * (10 // len(gens))):
                    next(g)
            alive = list(gens)
            while alive:
                for g in list(alive):
                    try:
                        next(g)
                    except StopIteration:
                        alive.remove(g)
        self.S.barrier()
        A.off = base
        obt = [A.f32(512), A.f32(512)]
        oft = [A.f32(512), A.f32(512)]
        dzt = [A.bf16(512), A.bf16(512)]
        osum = [A.f32(512), A.f32(512)]
        sq = A.f32(512)
        th = [A.f32(512), A.f32(512)]
        mixo = [A.bf16(512), A.bf16(512)]
        smc = [A.f32(8), A.f32(8)]
        bch = lambda a, n=128: a.unsqueeze(2).to_broadcast([n, 4, 128])
        for gt in range(self.NT // 128):
            b = gt % 2
            rows = slice(gt * 128, (gt + 1) * 128)
            X = lambda nm: "%s%d" % (nm, b)
            self.dma(obt[b], self.d_ob[rows, :], [("od1", gt)], [X("obt")], q="act")
            self.dma(oft[b], self.d_of[rows, :], [("od0", gt)], [X("oft")], q="act")
            self.dma(dzt[b], self.d_dz[rows, :], (), [X("dzt")], q="act")
            self.tt("pool", osum[b], oft[b], obt[b], ALU.add, [X("oft"), X("obt")], [X("osum")])
            self.act(sq, osum[b], AF.Square, [X("osum")], ["sqB"])
            self.red(smc[b][:, 0:4], v3(sq), ["sqB"], [X("ssD")])
            self.rsqrt_small(smc[b][:, 4:8], smc[b][:, 0:4], 1.0 / 128, [X("ssD")], [X("rsD")])
            self.tt("dve", v3(osum[b]), v3(osum[b]), bch(smc[b][:, 4:8]), ALU.mult, [X("osum"), X("rsD")], [X("osum")])
            self.tt("pool", v3(osum[b]), v3(osum[b]), self.dng.unsqueeze(1).to_broadcast([128, 4, 128]), ALU.mult, [X("osum"), "const"], [X("osum")])
            self.act(th[b], dzt[b], AF.Tanh, [X("dzt")], [X("thB")], scale=0.5)
            self.stt(th[b], th[b], 1.0, dzt[b], ALU.add, ALU.mult, [X("thB"), X("dzt")], [X("thB")])
            self.stt(mixo[b], osum[b], 0.5, th[b], ALU.mult, ALU.mult, [X("osum"), X("thB")], [X("mixo")])
            self.dma(self.d_mix[rows, 0:512], mixo[b], [X("mixo")], [])

    def phaseB2(self):
        A, ps = self.A, self.ps
        Smax = max(self.seqs)
        ntm = Smax // 128
        kaT = [A.bf16(Smax) for _ in range(3)]
        va = [A.bf16(ntm * 130).rearrange("p (t d) -> p t d", d=130) for _ in range(3)]
        qa = [A.bf16(512), A.bf16(512), A.bf16(512)]
        PT = [A.bf16(1024).rearrange("p (c n) -> p c n", c=2) for _ in range(3)]
        o1 = A.f32(128)
        oo = A.f32(128)
        ob = [A.bf16(128), A.bf16(128)]
        sm = A.f32(16)
        sqo = A.f32(128)
        steps = []
        T0 = 0
        for S_ in self.seqs:
            nt = S_ // 128
            QB = min(512, S_)
            for h in range(4):
                for qb in range(S_ // QB):
                    for kt in range(nt):
                        steps.append((T0, S_, h, qb, kt))
            T0 += S_
        state = {"hkey": None, "hcount": -1, "qkey": None, "qcount": -1}

        def prep_qk(n):
            T0, S_, h, qb, kt = steps[n]
            nt = S_ // 128
            QB = min(512, S_)
            if state["hkey"] != (T0, h):
                state["hkey"] = (T0, h)
                state["hcount"] += 1
                hb = state["hcount"] % 3
                tk, tv = "kaT%d" % hb, "va%d" % hb
                self.dma(kaT[hb][:, 0:S_], self.d_kaT[h * 128:(h + 1) * 128, T0:T0 + S_], (), [tk])
                self.dma(va[hb][:, 0:nt, 0:128], self.d_va[T0:T0 + S_, h * 128:(h + 1) * 128].rearrange("(t p) d -> p t d", p=128), (), [tv])
                self.memset("pool", va[hb][:, 0:nt, 128:129], 1.0, [tv])
            if state["qkey"] != (T0, h, qb):
                state["qkey"] = (T0, h, qb)
                state["qcount"] += 1
                qq = state["qcount"] % 3
                self.dma(qa[qq][:, 0:QB], self.d_qaT[h * 128:(h + 1) * 128, T0 + qb * QB:T0 + (qb + 1) * QB], (), ["qa%d" % qq])
            hb = state["hcount"] % 3
            qq = state["qcount"] % 3
            a = 2 * (n % 2)
            for comp in range(2):
                r0 = comp * 64
                self.mm(ps[a + comp][:, 0:QB], kaT[hb][r0:r0 + 64, kt * 128:(kt + 1) * 128], qa[qq][r0:r0 + 64, 0:QB], True, True,
                        ["kaT%d" % hb, "qa%d" % qq], ["ps%d" % (a + comp)])
            return hb

        oc = {"n": 0, "q": 0}
        accS = [A.f32(3 * 512).rearrange("p (b n) -> p b n", b=3) for _ in range(2)]

        def do_exp(n):
            T0, S_, h, qb, kt = steps[n]
            QB = min(512, S_)
            a = 2 * (n % 2)
            pb = n % 3
            tp = "PT%d" % pb
            if QB == 512:
                self.act(PT[pb].rearrange("p c n -> p (c n)"), self.psall[:, a * 512:(a + 2) * 512], AF.Exp, ["ps%d" % a, "ps%d" % (a + 1)], [tp])
            else:
                for comp in range(2):
                    self.act(PT[pb][:, comp, 0:QB], ps[a + comp][:, 0:QB], AF.Exp, ["ps%d" % (a + comp)], [tp])

        def do_pv(n, hb):
            T0, S_, h, qb, kt = steps[n]
            nt = S_ // 128
            QB = min(512, S_)
            nqs = QB // 128
            pb = n % 3
            tp = "PT%d" % pb
            tv = "va%d" % hb
            place = {}
            idx = 0
            for comp in range(2):
                for qs in range(nqs):
                    place[(comp, qs)] = (4 + idx // 3, (idx % 3) * 129)
                    idx += 1
            if kt == 0:
                state["started"] = set()
            started = state["started"]
            for comp in range(2):
                for qs in range(nqs):
                    bk, col = place[(comp, qs)]
                    st = bk not in started
                    started.add(bk)
                    self.mm(ps[bk][:, col:col + 129], PT[pb][:, comp, qs * 128:(qs + 1) * 128], va[hb][:, kt, 0:129], st, kt == nt - 1,
                            [tp, tv], ["ps%d" % bk])
            if kt == nt - 1:
                qp = oc["q"] % 2
                oc["q"] += 1
                aS = accS[qp]
                ta = "accS%d" % qp
                nacc = 2 * nqs
                for bk in sorted(set(b for b, _ in place.values())):
                    ncols = 129 * min(3, nacc - 3 * (bk - 4))
                    self.cp("dve", aS[:, bk - 4, 0:ncols], ps[bk][:, 0:ncols], ["ps%d" % bk], [ta])
                for qs in range(nqs):
                    b0, c0 = place[(0, qs)]
                    b1, c1 = place[(1, qs)]
                    oq = oc["n"] % 2
                    oc["n"] += 1
                    a0_, a1_ = aS[:, b0 - 4, :], aS[:, b1 - 4, :]
                    self.S.add("dve", lambda e, a0_=a0_, c0=c0: e.reciprocal(out=sm[:, 0:1], in_=a0_[:, c0 + 128:c0 + 129]), [ta], ["smr0"])
                    self.S.add("dve", lambda e, a1_=a1_, c1=c1: e.reciprocal(out=sm[:, 1:2], in_=a1_[:, c1 + 128:c1 + 129]), [ta], ["smr1"])
                    self.tt("dve", sm[:, 2:3], sm[:, 1:2], self.lam[:, 3:4], ALU.mult, ["smr1", "const"], ["smr2"])
                    self.ts("dve", o1, a0_[:, c0:c0 + 128], sm[:, 0:1], None, ALU.mult, None, [ta, "smr0"], ["o1"])
                    self.stt(oo, a1_[:, c1:c1 + 128], sm[:, 2:3], o1, ALU.mult, ALU.add, [ta, "smr2", "o1"], ["oo"])
                    self.tt("pool", sqo, oo, oo, ALU.mult, ["oo"], ["sqo"])
                    self.red(sm[:, 4:5], sqo, ["sqo"], ["ssB"])
                    self.rsqrt_small(sm[:, 5:6], sm[:, 4:5], 1.0 / 128, ["ssB"], ["rsB"])
                    self.stt(ob[oq], oo, sm[:, 5:6], self.subg, ALU.mult, ALU.mult, ["oo", "rsB", "const"], ["ob%d" % oq])
                    r0 = T0 + qb * QB + qs * 128
                    self.dma(self.d_mix[r0:r0 + 128, 512 + h * 128:512 + (h + 1) * 128], ob[oq], ["ob%d" % oq], [])

        hbs = {}
        hbs[0] = prep_qk(0)
        for n in range(len(steps)):
            if n + 1 < len(steps):
                hbs[n + 1] = prep_qk(n + 1)
            if n >= 1:
                do_pv(n - 1, hbs[n - 1])
            do_exp(n)
        do_pv(len(steps) - 1, hbs[len(steps) - 1])

    def phaseC1(self):
        A, ps, psb = self.A, self.ps, self.psb
        wout = A.bf16(8 * D).rearrange("p (k n) -> p k n", k=8)
        stage = [A.f32(1024), A.f32(1024)]
        self.load_weight_bf(wout, self.w_out, 8, D, None, "wout", stage)
        ntl = self.NT // 128

        def c1_stream(sid, tiles, bk):
            ps_ = [ps[b] for b in bk]
            psb_ = [psb[b] for b in bk]
            pt = ["ps%d" % b for b in bk]
            N = lambda nm: "%s_s%d" % (nm, sid)
            mixb = [A.bf16(1024), A.bf16(1024)]
            mixT = A.bf16(1024).rearrange("p (k n) -> p k n", k=8)
            xin = [A.f32(1024), A.f32(1024)]
            x1 = [A.f32(1024), A.f32(1024)]
            hnb = [A.bf16(1024), A.bf16(1024)]
            hnT = [A.bf16(1024).rearrange("p (k n) -> p k n", k=8) for _ in range(2)]
            jk = A.bf16(1024)
            sm = [A.f32(8), A.f32(8)]

            def p1(j):
                i = tiles[j]
                b = j % 2
                rows = slice(i * 128, (i + 1) * 128)
                tm, tx, t1 = N("mixb%d" % b), N("xinC%d" % b), N("x1C%d" % b)
                self.dma(mixb[b], self.d_mix[rows, :], (), [tm], q="act")
                self.dma(xin[b], self.x[rows, :], (), [tx], q="act")
                for kc in range(8):
                    self.tr(psb_[0][:, kc * 128:(kc + 1) * 128], mixb[b][:, kc * 128:(kc + 1) * 128], self.ident, [tm, "const"], [pt[0]])
                self.cp("act", mixT, psb_[0][:, 0:1024].rearrange("p (k n) -> p k n", k=8), [pt[0]], [N("mixT")])
                for nb in range(2):
                    for kc in range(8):
                        self.mm(ps_[1 + nb][:, 0:512], mixT[:, kc, :], wout[:, kc, nb * 512:(nb + 1) * 512], kc == 0, kc == 7, [N("mixT"), "wout"], [pt[1 + nb]])
                    self.tt("dve", x1[b][:, nb * 512:(nb + 1) * 512], ps_[1 + nb][:, 0:512], xin[b][:, nb * 512:(nb + 1) * 512], ALU.add,
                            [pt[1 + nb], tx], [t1])
                self.dma(self.d_x1[rows, :], x1[b], [t1], [("x1d", i)])
                self.act(jk, x1[b], AF.Square, [t1], [N("jkC"), N("ssC%d" % b)], accum_out=sm[b][:, 0:1])
                self.rsqrt_small(sm[b][:, 1:2], sm[b][:, 0:1], 1.0 / D, [N("ssC%d" % b)], [N("rsC%d" % b)])
                self.ts("dve", hnb[b], x1[b], sm[b][:, 1:2], None, ALU.mult, None, [t1, N("rsC%d" % b)], [N("hnb%d" % b)])

            def p2(j):
                i = tiles[j]
                b = j % 2
                rows = slice(i * 128, (i + 1) * 128)
                for kc in range(8):
                    self.tr(psb_[3][:, kc * 128:(kc + 1) * 128], hnb[b][:, kc * 128:(kc + 1) * 128], self.ident, [N("hnb%d" % b), "const"], [pt[3]])
                self.cp("act", hnT[b], psb_[3][:, 0:1024].rearrange("p (k n) -> p k n", k=8), [pt[3]], [N("hnT%d" % b)])
                self.dma(self.d_hnT[:, rows].rearrange("(k p) n -> p k n", p=128), hnT[b], [N("hnT%d" % b)], [("hnTd", i)])

            p1(0)
            yield
            for j in range(len(tiles)):
                if j + 1 < len(tiles):
                    p1(j + 1)
                    yield
                p2(j)
                yield

        gens = [c1_stream(k, list(range(k, ntl, 4)), (0, 1, 2, 3) if k % 2 == 0 else (4, 5, 6, 7)) for k in range(4)]
        for k, g in enumerate(gens):
            for _ in range(3 - k):
                next(g)
        while gens:
            for g in list(gens):
                try:
                    next(g)
                except StopIteration:
                    gens.remove(g)

    def phaseC2(self):
        A, ps = self.A, self.ps
        wup = A.bf16(8 * 2 * DFF).rearrange("p (k n) -> p k n", k=8)
        wdn = A.bf16(22 * D).rearrange("p (k n) -> p k n", k=22)
        mark = A.off
        stage = [A.f32(1408), A.f32(1408)]
        self.load_weight_bf(wup, self.w_up, 8, 2 * DFF, self.g2, "wup", stage)
        self.load_weight_bf(wdn, self.w_down, 22, D, None, "wdn", stage)
        self.S.barrier()
        A.off = mark
        BLM = min(512, max(self.seqs))
        hb = A.bf16(8 * (BLM + 2)).rearrange("p (k n) -> p k n", k=8)
        actT = A.bf16(22 * BLM).rearrange("p (c n) -> p c n", c=22)
        gsb = A.f32(2 * (BLM + 2))
        acc = A.f32(BLM)
        sl = A.f32(BLM)
        x1 = A.f32(1024)
        T0 = 0
        for S_ in self.seqs:
            BL = min(512, S_)
            for blk in range(S_ // BL):
                t0 = T0 + blk * BL
                ntb = BL // 128
                first, last = blk == 0, blk == S_ // BL - 1
                lo = 1 if first else 0
                hi = BL + 1 if last else BL + 2
                c_lo, c_hi = t0 - 1 + lo, t0 - 1 + hi
                rd = [("hnTd", j) for j in range(c_lo // 128, (c_hi - 1) // 128 + 1)]
                self.dma(hb[:, :, lo:hi], self.d_hnT[:, c_lo:c_hi].rearrange("(k p) n -> p k n", p=128), rd, ["hb"])
                if first:
                    self.memset("pool", hb[:, :, 0:1], 0.0, ["hb"])
                if last:
                    self.memset("pool", hb[:, :, BL + 1:BL + 2], 0.0, ["hb"])
                for c in range(22):
                    bg = c % 2
                    bu = 2 + c % 2
                    for kc in range(8):
                        self.mm(ps[bg][:, 0:BL], wup[:, kc, c * 128:(c + 1) * 128], hb[:, kc, 1:BL + 1], kc == 0, kc == 7, ["hb", "wup"], ["ps%d" % bg])
                    for kc in range(8):
                        self.mm(ps[4][:, 0:2], wup[:, kc, c * 128:(c + 1) * 128], hb[:, kc, 0:BL + 2:BL + 1], kc == 0, kc == 7, ["hb", "wup"], ["ps4"])
                    for kc in range(8):
                        self.mm(ps[bu][:, 0:BL], wup[:, kc, DFF + c * 128:DFF + (c + 1) * 128], hb[:, kc, 1:BL + 1], kc == 0, kc == 7,
                                ["hb", "wup"], ["ps%d" % bu])
                    self.cp("act", gsb[:, 1:BL + 1], ps[bg][:, 0:BL], ["ps%d" % bg], ["gsb"])
                    self.cp("dve", gsb[:, 0:BL + 2:BL + 1], ps[4][:, 0:2], ["ps4"], ["gsb"])
                    self.ts("dve", acc[:, 0:BL], gsb[:, 0:BL], self.fw[:, c * 3:c * 3 + 1], self.fb[:, c:c + 1], ALU.mult, ALU.add, ["gsb", "const"], ["accC"])
                    self.stt(acc[:, 0:BL], gsb[:, 1:BL + 1], self.fw[:, c * 3 + 1:c * 3 + 2], acc[:, 0:BL], ALU.mult, ALU.add, ["gsb", "accC", "const"], ["accC"])
                    self.stt(acc[:, 0:BL], gsb[:, 2:BL + 2], self.fw[:, c * 3 + 2:c * 3 + 3], acc[:, 0:BL], ALU.mult, ALU.add, ["gsb", "accC", "const"], ["accC"])
                    self.act(sl[:, 0:BL], acc[:, 0:BL], AF.Silu, ["accC"], ["sl"])
                    self.tt("dve", actT[:, c, 0:BL], sl[:, 0:BL], ps[bu][:, 0:BL], ALU.mult, ["sl", "ps%d" % bu], ["actT"])
                for j in range(ntb):
                    ti = t0 // 128 + j
                    rows = slice(t0 + j * 128, t0 + (j + 1) * 128)
                    self.dma(x1, self.d_x1[rows, :], [("x1d", ti)], ["x1F"])
                    for nb in range(2):
                        bk = 5 + nb
                        for c in range(22):
                            self.mm(ps[bk][:, 0:512], actT[:, c, j * 128:(j + 1) * 128], wdn[:, c, nb * 512:(nb + 1) * 512], c == 0, c == 21,
                                    ["actT", "wdn"], ["ps%d" % bk])
                        self.tt("dve", x1[:, nb * 512:(nb + 1) * 512], ps[bk][:, 0:512], x1[:, nb * 512:(nb + 1) * 512], ALU.add, ["ps%d" % bk, "x1F"], ["x1F"])
                    self.dma(self.d_x1[rows, :], x1, ["x1F"], [("x1d", ti)])
            T0 += S_

    def phaseC3(self):
        A, ps, psb = self.A, self.ps, self.psb
        wg = A.bf16(8 * D).rearrange("p (k n) -> p k n", k=8)
        pp = A.bf16(2 * D).rearrange("p (k n) -> p k n", k=2)
        stage = [A.f32(1024), A.f32(1024)]
        self.load_weight_bf(wg, self.w_ple_gate, 8, D, self.g3, "wg", stage)
        self.load_weight_bf(pp, self.ple_proj, 2, D, None, "pp", stage)
        pleg = A.f32(1024)
        self.dma(pleg, self.ple_norm_g[0:1, :].partition_broadcast(128), (), ["pleg"])
        ntl = self.NT // 128

        def c3_stream(sid, tiles, bk):
            ps_ = [ps[b] for b in bk]
            psb_ = [psb[b] for b in bk]
            pt = ["ps%d" % b for b in bk]
            N = lambda nm: "%s_s%d" % (nm, sid)
            x2 = [A.f32(1024), A.f32(1024)]
            pin = [A.f32(256), A.f32(256)]
            pbf = A.bf16(256)
            peT = A.bf16(256).rearrange("p (k n) -> p k n", k=2)
            ee = [A.f32(1024), A.f32(1024)]
            xnb = [A.bf16(1024), A.bf16(1024)]
            xnT = A.bf16(1024).rearrange("p (k n) -> p k n", k=8)
            th = A.f32(1024)
            yb = [A.f32(1024), A.f32(1024)]
            jk = A.bf16(1024)
            sm = [A.f32(8), A.f32(8)]

            def q1(j):
                i = tiles[j]
                b = j % 2
                rows = slice(i * 128, (i + 1) * 128)
                X = lambda nm: N("%s%d" % (nm, b))
                tx, tp = X("x2_"), X("pin")
                s_ = sm[b]
                self.dma(x2[b], self.d_x1[rows, :], [("x1d", i)], [tx], q="act")
                self.dma(pin[b], self.p[rows, :], (), [tp], q="act")
                self.cp("dve", pbf, pin[b], [tp], [N("pbf")])
                for k in range(2):
                    self.tr(psb_[0][:, k * 128:(k + 1) * 128], pbf[:, k * 128:(k + 1) * 128], self.ident, [N("pbf"), "const"], [pt[0]])
                self.cp("act", peT, psb_[0][:, 0:256].rearrange("p (k n) -> p k n", k=2), [pt[0]], [N("peT")])
                for nb in range(2):
                    for k in range(2):
                        self.mm(ps_[1 + nb][:, 0:512], peT[:, k, :], pp[:, k, nb * 512:(nb + 1) * 512], k == 0, k == 1, [N("peT"), "pp"], [pt[1 + nb]])
                    self.act(jk[:, 0:512], ps_[1 + nb][:, 0:512], AF.Square, [pt[1 + nb]], [N("jk3"), X("sse%d_" % nb)], accum_out=s_[:, nb:nb + 1])
                self.tt("dve", s_[:, 2:3], s_[:, 0:1], s_[:, 1:2], ALU.add, [X("sse0_"), X("sse1_")], [X("sse")])
                self.rsqrt_small(s_[:, 3:4], s_[:, 2:3], 1.0 / D, [X("sse")], [X("rse")])
                for nb in range(2):
                    self.stt(ee[b][:, nb * 512:(nb + 1) * 512], ps_[1 + nb][:, 0:512], s_[:, 3:4], pleg[:, nb * 512:(nb + 1) * 512], ALU.mult, ALU.mult,
                             [pt[1 + nb], X("rse"), "pleg"], [X("ee")])
                self.act(jk, x2[b], AF.Square, [tx], [N("jk3"), X("ssx")], accum_out=s_[:, 4:5])
                self.rsqrt_small(s_[:, 5:6], s_[:, 4:5], 1.0 / D, [X("ssx")], [X("rsx")])
                self.ts("dve", xnb[b], x2[b], s_[:, 5:6], None, ALU.mult, None, [tx, X("rsx")], [X("xnb3")])

            def q2(j):
                i = tiles[j]
                b = j % 2
                rows = slice(i * 128, (i + 1) * 128)
                X = lambda nm: N("%s%d" % (nm, b))
                tx, ty = X("x2_"), X("yb")
                for kc in range(8):
                    self.tr(psb_[3][:, kc * 128:(kc + 1) * 128], xnb[b][:, kc * 128:(kc + 1) * 128], self.ident, [X("xnb3"), "const"], [pt[3]])
                self.cp("act", xnT, psb_[3][:, 0:1024].rearrange("p (k n) -> p k n", k=8), [pt[3]], [N("xnT3")])
                for nb in range(2):
                    for kc in range(8):
                        self.mm(ps_[1 + nb][:, 0:512], xnT[:, kc, :], wg[:, kc, nb * 512:(nb + 1) * 512], kc == 0, kc == 7, [N("xnT3"), "wg"], [pt[1 + nb]])
                    self.act(th[:, nb * 512:(nb + 1) * 512], ps_[1 + nb][:, 0:512], AF.Tanh, [pt[1 + nb]], [N("th")], scale=0.5)
                self.stt(th, th, 1.0, ee[b], ALU.add, ALU.mult, [N("th"), X("ee")], [N("th")])
                self.stt(yb[b], th, 0.5, x2[b], ALU.mult, ALU.add, [N("th"), tx], [ty])
                self.dma(self.y[rows, :], yb[b], [ty], [])

            q1(0)
            yield
            for j in range(len(tiles)):
                if j + 1 < len(tiles):
                    q1(j + 1)
                    yield
                q2(j)
                yield

        gens = [c3_stream(k, list(range(k, ntl, 4)), (0, 1, 2, 3) if k % 2 == 0 else (4, 5, 6, 7)) for k in range(4)]
        for k, g in enumerate(gens):
            for _ in range(3 - k):
                next(g)
        while gens:
            for g in list(gens):
                try:
                    next(g)
                except StopIteration:
                    gens.remove(g)


def rope_table():
    inv = ROPE_THETA ** (-np.arange(0, 16, 2, dtype=np.float32) / 16.0)
    pos = np.arange(4096, dtype=np.float32)
    ang = (pos[:, None] * inv[None, :].astype(np.float32)).astype(np.float32)
    cs = np.concatenate([np.cos(ang), np.sin(ang)], axis=1).astype(np.float32)
    return np.ascontiguousarray(cs.reshape(32, 128, 16).transpose(1, 0, 2).reshape(128, 512))


def host_layout(inp):
    f = lambda a: np.ascontiguousarray(np.asarray(a, dtype=np.float32))
    m = {}
    m["w_in"] = f(inp["w_in"][0]); m["w_out"] = f(inp["w_out"][0]); m["w_up"] = f(inp["w_up"][0])
    m["w_down"] = f(inp["w_down"][0]); m["ple_proj"] = f(inp["ple_proj"][0]); m["w_ple_gate"] = f(inp["w_ple_gate"][0])
    m["ln1_g"] = f(np.asarray(inp["ln1_g"][0]).reshape(8, 128).T)
    m["ln2_g"] = f(np.asarray(inp["ln2_g"][0]).reshape(8, 128).T)
    m["ple_gate_norm_g"] = f(np.asarray(inp["ple_gate_norm_g"][0]).reshape(8, 128).T)
    m["ple_norm_g"] = f(np.asarray(inp["ple_norm_g"][0]).reshape(1, D))
    m["cw_l"] = f(np.asarray(inp["dn_conv_w"][0]).reshape(5, 12, 128).transpose(2, 1, 0).reshape(128, 60))
    m["fw_l"] = f(np.asarray(inp["ffn_conv_w"][0]).reshape(3, 22, 128).transpose(2, 1, 0).reshape(128, 66))
    m["fb_l"] = f(np.asarray(inp["ffn_conv_b"][0]).reshape(22, 128).T)
    m["dn_a_log"] = f(np.asarray(inp["dn_a_log"][0]).reshape(1, 8))
    m["dn_dt_bias"] = f(np.asarray(inp["dn_dt_bias"][0]).reshape(1, 8))
    m["dn_norm_g"] = f(np.asarray(inp["dn_norm_g"][0]).reshape(1, 128))
    m["da_qk_norm_g"] = f(np.asarray(inp["da_qk_norm_g"][0]).reshape(1, 128))
    m["da_lambda"] = f(np.asarray(inp["da_lambda"][0]).reshape(1, 256))
    m["da_subln_g"] = f(np.asarray(inp["da_subln_g"][0]).reshape(1, 128))
    m["rope_cs"] = rope_table()
    return m


def kernel(**inp):
    xp = np.asarray(inp["x_prompt"], dtype=np.float32)
    xs = np.asarray(inp["x_sample"], dtype=np.float32)
    pp = np.asarray(inp["p_prompt"], dtype=np.float32)[0]
    psm = np.asarray(inp["p_sample"], dtype=np.float32)[0]
    nb, SP = xp.shape[0], xp.shape[1]
    SS = xs.shape[1]
    per = xs.shape[0] // nb
    seqs = [SP] + [SS] * per
    common = host_layout(inp)
    in_maps = []
    for c in range(nb):
        m = dict(common)
        m["x"] = np.ascontiguousarray(np.concatenate([xp[c]] + [xs[c * per + j] for j in range(per)], axis=0))
        m["p"] = np.ascontiguousarray(np.concatenate([pp[c]] + [psm[c * per + j] for j in range(per)], axis=0))
        in_maps.append(m)
    nc = Builder(seqs).build()
    res = run_bass_kernel_spmd(nc, in_maps, core_ids=list(range(nb)))
    yp = np.stack([res.results[c]["y"][0:SP] for c in range(nb)], axis=0)
    ys = np.stack([res.results[c]["y"][SP + j * SS:SP + (j + 1) * SS] for c in range(nb) for j in range(per)], axis=0)
    return (yp.astype(np.float32), ys.astype(np.float32))
```
